# Optimizing a Trainium2 kernel written in Bass

```python
import math
import jax
import jax.numpy as jnp
from jax import lax
import numpy as np

D_MODEL = 1024
BATCH = 8
SEQ = 2048
DEPTH = 2
DEC_BATCH = 128
DEC_SEQ = 1
PAST_LEN = 16384
PAGE_SIZE = 128

N_EVEN = (DEPTH + 1) // 2
N_ODD = DEPTH // 2
HALF = D_MODEL // 2

GLA_H = 4
GLA_DV = HALF // GLA_H
GLA_DK = GLA_DV // 2
GLA_KW = GLA_H * GLA_DK
GLA_VW = GLA_H * GLA_DV
GLA_LR = 16
GLA_TAU = 16.0
CHUNK = 64

LRU_W = HALF
LRU_H = 8
LRU_BW = LRU_W // LRU_H
CONV_W = 4
RG_C = 8.0

S5_W = HALF
S5_GH = 16
S5_G = S5_W // S5_GH
S5_P = 64

HG_H = 4
HG_D = HALF // HG_H
HG_W = HG_H * HG_D

D_FF = ((8 * D_MODEL // 3 + 255) // 256) * 256

EVEN_SPLITS = (GLA_KW, GLA_KW, GLA_VW, GLA_VW, GLA_LR, LRU_W, LRU_W)
ODD_SPLITS = (S5_W, HG_W, HG_W, HG_W, HG_W)
EVEN_IN = sum(EVEN_SPLITS)
ODD_IN = sum(ODD_SPLITS)
EVEN_OUT = GLA_VW + LRU_W
ODD_OUT = S5_W + HG_W

ALPHA = (2.0 * DEPTH) ** 0.25
BETA = (8.0 * DEPTH) ** -0.25
LN_EPS = 1e-5
RMS_EPS = 1e-6

kernel_name = 'hybrid_gla_rglru_s5_hgrn2_step'


def split_cols(z, sizes):
    idx = [int(s) for s in np.cumsum(sizes)[:-1]]
    return jnp.split(z, idx, axis=-1)


def layer_norm(x, g, b):
    xf = x.astype(jnp.float32)
    mu = jnp.mean(xf, axis=-1, keepdims=True)
    var = jnp.mean(jnp.square(xf - mu), axis=-1, keepdims=True)
    return (xf - mu) * lax.rsqrt(var + LN_EPS) * g + b


def rms_norm(x, g):
    xf = x.astype(jnp.float32)
    return xf * lax.rsqrt(jnp.mean(xf * xf, axis=-1, keepdims=True) + RMS_EPS) * g


def swiglu(h, w_in, w_out):
    gate, up = jnp.split(jnp.einsum('bld,df->blf', h, w_in), 2, axis=-1)
    return jnp.einsum('blf,fd->bld', jax.nn.silu(gate) * up, w_out)


def gated_linear_attention(q, k, v, log_a, s0):
    bsz, L = q.shape[0], q.shape[1]
    c = min(CHUNK, L)
    n = -(-L // c)
    pad = n * c - L

    def blocks(t):
        t = jnp.pad(t.astype(jnp.float32), [(0, 0), (0, pad)] + [(0, 0)] * (t.ndim - 2))
        return jnp.moveaxis(t.reshape((bsz, n, c) + t.shape[2:]), 1, 0)

    qb, kb, vb, gb = blocks(q), blocks(k), blocks(v), blocks(log_a)
    cum = jnp.cumsum(gb, axis=2)
    causal = jnp.tril(jnp.ones((c, c), dtype=bool))[:, :, None, None]

    def step(s, blk):
        qc, kc, vc, bc = blk
        rel = jnp.where(causal, bc[:, :, None] - bc[:, None, :], -jnp.inf)
        scores = jnp.einsum('btshk,bthk,bshk->bhts', jnp.exp(rel), qc, kc)
        last = bc[:, -1]
        o = (jnp.einsum('bhts,bshv->bthv', scores, vc)
             + jnp.einsum('bthk,bhkv->bthv', qc * jnp.exp(bc), s))
        s_new = (jnp.exp(last)[..., None] * s
                 + jnp.einsum('bshk,bshv->bhkv', kc * jnp.exp(last[:, None] - bc), vc))
        return s_new, o

    s_fin, ob = lax.scan(step, s0.astype(jnp.float32), (qb, kb, vb, cum))
    o = jnp.moveaxis(ob, 0, 1).reshape((bsz, n * c) + ob.shape[3:])[:, :L]
    return o, s_fin


def causal_depthwise_conv(buf, x, w, b):
    L = x.shape[1]
    xp = jnp.concatenate([buf, x], axis=1)
    y = b + xp[:, 0:L] * w[0]
    for j in range(1, CONV_W):
        y = y + xp[:, j:j + L] * w[j]
    return y, xp[:, xp.shape[1] - (CONV_W - 1):]


def s5_ssm(u, s_re, s_im, lam_re, lam_im, log_dt, b_re, b_im, c_re, c_im, d_skip):
    f32 = jnp.float32
    bsz, L = u.shape[0], u.shape[1]
    uf = u.astype(f32).reshape(bsz, L, S5_G, S5_GH)
    dt = jnp.exp(log_dt.astype(f32))[:, None]
    lr_, li_ = lam_re.astype(f32), lam_im.astype(f32)
    mag = jnp.exp(lr_ * dt)
    ab_re, ab_im = mag * jnp.cos(li_ * dt), mag * jnp.sin(li_ * dt)
    den = lr_ * lr_ + li_ * li_
    nr, ni = ab_re - 1.0, ab_im
    f_re = (nr * lr_ + ni * li_) / den
    f_im = (ni * lr_ - nr * li_) / den
    bb_re = f_re[..., None] * b_re - f_im[..., None] * b_im
    bb_im = f_re[..., None] * b_im + f_im[..., None] * b_re
    bu_re = jnp.einsum('blgh,gph->blgp', uf, bb_re)
    bu_im = jnp.einsum('blgh,gph->blgp', uf, bb_im)
    a_re = jnp.broadcast_to(ab_re, bu_re.shape)
    a_im = jnp.broadcast_to(ab_im, bu_re.shape)

    def combine(e1, e2):
        a1r, a1i, b1r, b1i = e1
        a2r, a2i, b2r, b2i = e2
        return (a1r * a2r - a1i * a2i, a1r * a2i + a1i * a2r,
                a2r * b1r - a2i * b1i + b2r, a2r * b1i + a2i * b1r + b2i)

    _, _, hr, hi = lax.associative_scan(combine, (a_re, a_im, bu_re, bu_im), axis=1)
    steps = jnp.arange(1, L + 1, dtype=f32)[:, None, None]
    pm = jnp.exp(lr_ * dt * steps)
    ang = li_ * dt * steps
    p_re, p_im = pm * jnp.cos(ang), pm * jnp.sin(ang)
    s0r, s0i = s_re.astype(f32)[:, None], s_im.astype(f32)[:, None]
    hr = hr + p_re * s0r - p_im * s0i
    hi = hi + p_re * s0i + p_im * s0r
    y = (jnp.einsum('ghp,blgp->blgh', c_re, hr) - jnp.einsum('ghp,blgp->blgh', c_im, hi)
         + d_skip.reshape(S5_G, S5_GH) * uf)
    return y.reshape(bsz, L, S5_W), hr[:, -1], hi[:, -1]


def even_mixer(h, s_gla, s_conv, s_lru, w_in, gla_w_lr, gla_b_lr, gla_norm_g,
               conv_w, conv_b, lru_w_r, lru_b_r, lru_w_i, lru_b_i, lru_lam, w_out):
    bsz, L = h.shape[0], h.shape[1]
    z = jnp.einsum('bld,de->ble', h, w_in).astype(jnp.float32)
    q, k, v, g, lr, xr, gr = split_cols(z, EVEN_SPLITS)
    log_alpha = jax.nn.log_sigmoid(jnp.einsum('blr,rk->blk', lr, gla_w_lr) + gla_b_lr) / GLA_TAU
    o, s_gla_new = gated_linear_attention(
        q.reshape(bsz, L, GLA_H, GLA_DK) * (GLA_DK ** -0.5),
        k.reshape(bsz, L, GLA_H, GLA_DK),
        v.reshape(bsz, L, GLA_H, GLA_DV),
        log_alpha.reshape(bsz, L, GLA_H, GLA_DK), s_gla)
    y_gla = rms_norm(o, gla_norm_g).reshape(bsz, L, GLA_VW) * jax.nn.silu(g)
    xc, s_conv_new = causal_depthwise_conv(s_conv.astype(jnp.float32), xr, conv_w, conv_b)
    xb = xc.reshape(bsz, L, LRU_H, LRU_BW)
    r = jax.nn.sigmoid(jnp.einsum('blhi,hij->blhj', xb, lru_w_r).reshape(bsz, L, LRU_W) + lru_b_r)
    i = jax.nn.sigmoid(jnp.einsum('blhi,hij->blhj', xb, lru_w_i).reshape(bsz, L, LRU_W) + lru_b_i)
    log_a = -RG_C * r * jax.nn.softplus(-lru_lam.astype(jnp.float32))
    u = jnp.sqrt(-jnp.expm1(2.0 * log_a)) * (i * xc)

    def lru_step(h_prev, inp):
        a_t, u_t = inp
        h_t = a_t * h_prev + u_t
        return h_t, h_t

    s_lru_new, hs = lax.scan(lru_step, s_lru.astype(jnp.float32),
                             (jnp.swapaxes(jnp.exp(log_a), 0, 1), jnp.swapaxes(u, 0, 1)))
    y_lru = jnp.swapaxes(hs, 0, 1) * jax.nn.gelu(gr)
    out = jnp.einsum('ble,ed->bld', jnp.concatenate([y_gla, y_lru], axis=-1), w_out)
    return out, s_gla_new, s_conv_new, s_lru_new


def odd_mixer(h, s_re, s_im, s_hg, lower_bound, w_in, lam_re, lam_im, log_dt, b_re, b_im,
              c_re, c_im, d_skip, w_glu, b_glu, hg_norm_g, w_out):
    bsz, L = h.shape[0], h.shape[1]
    z = jnp.einsum('bld,de->ble', h, w_in).astype(jnp.float32)
    u, q, f, i, g = split_cols(z, ODD_SPLITS)
    y, s_re_new, s_im_new = s5_ssm(u, s_re, s_im, lam_re, lam_im, log_dt, b_re, b_im,
                                   c_re, c_im, d_skip)
    y = jax.nn.gelu(y)
    y_s5 = y * jax.nn.sigmoid(jnp.einsum('ble,ef->blf', y, w_glu) + b_glu)
    forget = lower_bound + (1.0 - lower_bound) * jax.nn.sigmoid(f)
    hd = lambda t: t.reshape(bsz, L, HG_H, HG_D)
    o, s_hg_new = gated_linear_attention(hd(jax.nn.silu(q)), hd(1.0 - forget), hd(i),
                                         hd(jnp.log(forget)), s_hg)
    y_hg = rms_norm(o, hg_norm_g).reshape(bsz, L, HG_W) * jax.nn.silu(g)
    out = jnp.einsum('ble,ed->bld', jnp.concatenate([y_s5, y_hg], axis=-1), w_out)
    return out, s_re_new, s_im_new, s_hg_new


def setup_inputs(seed: int = 0) -> dict:
    key = jax.random.key(seed)
    ks = iter(jax.random.split(key, 64))
    f32 = jnp.float32

    def nrm(shape, scale):
        return jax.random.normal(next(ks), shape, f32) * scale

    x_prompt = nrm((BATCH, SEQ, D_MODEL), 1.0)
    x_sample = nrm((DEC_BATCH, DEC_SEQ, D_MODEL), 1.0)
    c_prompt = nrm((BATCH, D_MODEL), 1.0)
    c_sample = nrm((DEC_BATCH, D_MODEL), 1.0)
    state_gla = nrm((N_EVEN, DEC_BATCH, GLA_H, GLA_DK, GLA_DV), 1.0)
    state_rglru_conv = nrm((N_EVEN, DEC_BATCH, CONV_W - 1, LRU_W), 1.0)
    state_rglru_h = nrm((N_EVEN, DEC_BATCH, LRU_W), 0.5)
    state_s5_re = nrm((N_ODD, DEC_BATCH, S5_G, S5_P), 0.1)
    state_s5_im = nrm((N_ODD, DEC_BATCH, S5_G, S5_P), 0.1)
    state_hgrn = nrm((N_ODD, DEC_BATCH, HG_H, HG_D, HG_D), 0.4)

    ev_w_in = nrm((N_EVEN, D_MODEL, EVEN_IN), D_MODEL ** -0.5)
    ev_gla_w_lr = nrm((N_EVEN, GLA_LR, GLA_KW), GLA_LR ** -0.5)
    ev_gla_b_lr = nrm((N_EVEN, GLA_KW), 0.1)
    ev_gla_norm_g = 1.0 + nrm((N_EVEN, GLA_DV), 0.1)
    ev_conv_w = nrm((N_EVEN, CONV_W, LRU_W), CONV_W ** -0.5)
    ev_conv_b = nrm((N_EVEN, LRU_W), 0.01)
    ev_lru_w_r = nrm((N_EVEN, LRU_H, LRU_BW, LRU_BW), LRU_BW ** -0.5)
    ev_lru_b_r = nrm((N_EVEN, LRU_W), 0.01)
    ev_lru_w_i = nrm((N_EVEN, LRU_H, LRU_BW, LRU_BW), LRU_BW ** -0.5)
    ev_lru_b_i = nrm((N_EVEN, LRU_W), 0.01)
    a_c = jax.random.uniform(next(ks), (N_EVEN, LRU_W), f32, minval=0.9, maxval=0.999)
    sig = a_c ** (1.0 / RG_C)
    ev_lru_lam = jnp.log(sig) - jnp.log1p(-sig)
    ev_w_out = nrm((N_EVEN, EVEN_OUT, D_MODEL), BETA * EVEN_OUT ** -0.5)

    od_w_in = nrm((N_ODD, D_MODEL, ODD_IN), D_MODEL ** -0.5)
    od_s5_lam_re = -0.5 + nrm((N_ODD, S5_G, S5_P), 0.01)
    od_s5_lam_im = jnp.pi * jnp.arange(S5_P, dtype=f32) + nrm((N_ODD, S5_G, S5_P), 0.01)
    od_s5_log_dt = jax.random.uniform(next(ks), (N_ODD, S5_G), f32,
                                      minval=math.log(0.001), maxval=math.log(0.1))
    od_s5_b_re = nrm((N_ODD, S5_G, S5_P, S5_GH), (2.0 * S5_GH) ** -0.5)
    od_s5_b_im = nrm((N_ODD, S5_G, S5_P, S5_GH), (2.0 * S5_GH) ** -0.5)
    od_s5_c_re = nrm((N_ODD, S5_G, S5_GH, S5_P), S5_P ** -0.5)
    od_s5_c_im = nrm((N_ODD, S5_G, S5_GH, S5_P), S5_P ** -0.5)
    od_s5_d = nrm((N_ODD, S5_W), 0.5)
    od_s5_w_glu = nrm((N_ODD, S5_W, S5_W), S5_W ** -0.5)
    od_s5_b_glu = nrm((N_ODD, S5_W), 0.01)
    hg_lb_logits = nrm((DEPTH, HG_W), 0.5)
    od_hg_norm_g = 1.0 + nrm((N_ODD, HG_D), 0.1)
    od_w_out = nrm((N_ODD, ODD_OUT, D_MODEL), BETA * ODD_OUT ** -0.5)

    w_ada = nrm((DEPTH, D_MODEL, 6 * D_MODEL), 0.5 * D_MODEL ** -0.5)
    b_ada = nrm((DEPTH, 6 * D_MODEL), 0.01)
    ln_g = 1.0 + nrm((DEPTH, 2, D_MODEL), 0.1)
    ln_b = nrm((DEPTH, 2, D_MODEL), 0.01)
    ffn_w_in = nrm((DEPTH, D_MODEL, 2 * D_FF), D_MODEL ** -0.5)
    ffn_w_out = nrm((DEPTH, D_FF, D_MODEL), BETA * D_FF ** -0.5)

    return {
        'x_prompt': x_prompt, 'x_sample': x_sample, 'c_prompt': c_prompt, 'c_sample': c_sample,
        'state_gla': state_gla, 'state_rglru_conv': state_rglru_conv,
        'state_rglru_h': state_rglru_h, 'state_s5_re': state_s5_re,
        'state_s5_im': state_s5_im, 'state_hgrn': state_hgrn,
        'ev_w_in': ev_w_in, 'ev_gla_w_lr': ev_gla_w_lr, 'ev_gla_b_lr': ev_gla_b_lr,
        'ev_gla_norm_g': ev_gla_norm_g, 'ev_conv_w': ev_conv_w, 'ev_conv_b': ev_conv_b,
        'ev_lru_w_r': ev_lru_w_r, 'ev_lru_b_r': ev_lru_b_r, 'ev_lru_w_i': ev_lru_w_i,
        'ev_lru_b_i': ev_lru_b_i, 'ev_lru_lam': ev_lru_lam, 'ev_w_out': ev_w_out,
        'od_w_in': od_w_in, 'od_s5_lam_re': od_s5_lam_re, 'od_s5_lam_im': od_s5_lam_im,
        'od_s5_log_dt': od_s5_log_dt, 'od_s5_b_re': od_s5_b_re, 'od_s5_b_im': od_s5_b_im,
        'od_s5_c_re': od_s5_c_re, 'od_s5_c_im': od_s5_c_im, 'od_s5_d': od_s5_d,
        'od_s5_w_glu': od_s5_w_glu, 'od_s5_b_glu': od_s5_b_glu,
        'hg_lb_logits': hg_lb_logits, 'od_hg_norm_g': od_hg_norm_g, 'od_w_out': od_w_out,
        'w_ada': w_ada, 'b_ada': b_ada, 'ln_g': ln_g, 'ln_b': ln_b,
        'ffn_w_in': ffn_w_in, 'ffn_w_out': ffn_w_out,
    }


def reference(x_prompt, x_sample, c_prompt, c_sample,
              state_gla, state_rglru_conv, state_rglru_h, state_s5_re, state_s5_im, state_hgrn,
              ev_w_in, ev_gla_w_lr, ev_gla_b_lr, ev_gla_norm_g, ev_conv_w, ev_conv_b,
              ev_lru_w_r, ev_lru_b_r, ev_lru_w_i, ev_lru_b_i, ev_lru_lam, ev_w_out,
              od_w_in, od_s5_lam_re, od_s5_lam_im, od_s5_log_dt, od_s5_b_re, od_s5_b_im,
              od_s5_c_re, od_s5_c_im, od_s5_d, od_s5_w_glu, od_s5_b_glu,
              hg_lb_logits, od_hg_norm_g, od_w_out,
              w_ada, b_ada, ln_g, ln_b, ffn_w_in, ffn_w_out):
    f32 = jnp.float32
    sm = jax.nn.softmax(hg_lb_logits.astype(f32), axis=0)
    lower_bounds = jnp.cumsum(sm, axis=0) - sm[0]

    def run(x, c, s_gla, s_conv, s_lru, s_re, s_im, s_hg):
        x = x.astype(f32)
        cond = jax.nn.silu(c.astype(f32))
        n_gla, n_conv, n_lru, n_re, n_im, n_hg = [], [], [], [], [], []
        for l in range(DEPTH):
            mod = jnp.einsum('bd,de->be', cond, w_ada[l]) + b_ada[l]
            sh_m, sc_m, gt_m, sh_f, sc_f, gt_f = [m[:, None, :] for m in jnp.split(mod, 6, axis=-1)]
            h = x * (1.0 + sc_m) + sh_m
            if l % 2 == 0:
                e = l // 2
                out, t1, t2, t3 = even_mixer(
                    h, s_gla[e], s_conv[e], s_lru[e], ev_w_in[e], ev_gla_w_lr[e], ev_gla_b_lr[e],
                    ev_gla_norm_g[e], ev_conv_w[e], ev_conv_b[e], ev_lru_w_r[e], ev_lru_b_r[e],
                    ev_lru_w_i[e], ev_lru_b_i[e], ev_lru_lam[e], ev_w_out[e])
                n_gla.append(t1)
                n_conv.append(t2)
                n_lru.append(t3)
            else:
                o = l // 2
                out, t1, t2, t3 = odd_mixer(
                    h, s_re[o], s_im[o], s_hg[o], lower_bounds[l], od_w_in[o], od_s5_lam_re[o],
                    od_s5_lam_im[o], od_s5_log_dt[o], od_s5_b_re[o], od_s5_b_im[o], od_s5_c_re[o],
                    od_s5_c_im[o], od_s5_d[o], od_s5_w_glu[o], od_s5_b_glu[o], od_hg_norm_g[o],
                    od_w_out[o])
                n_re.append(t1)
                n_im.append(t2)
                n_hg.append(t3)
            x = layer_norm(ALPHA * x + (1.0 + gt_m) * out, ln_g[l, 0], ln_b[l, 0])
            h = x * (1.0 + sc_f) + sh_f
            x = layer_norm(ALPHA * x + (1.0 + gt_f) * swiglu(h, ffn_w_in[l], ffn_w_out[l]),
                           ln_g[l, 1], ln_b[l, 1])
        return (x, jnp.stack(n_gla), jnp.stack(n_conv), jnp.stack(n_lru),
                jnp.stack(n_re), jnp.stack(n_im), jnp.stack(n_hg))

    bp = x_prompt.shape[0]
    zeros = lambda shape: jnp.zeros(shape, f32)
    y_p, gla_p, conv_p, lru_p, re_p, im_p, hg_p = run(
        x_prompt, c_prompt,
        zeros((N_EVEN, bp, GLA_H, GLA_DK, GLA_DV)), zeros((N_EVEN, bp, CONV_W - 1, LRU_W)),
        zeros((N_EVEN, bp, LRU_W)), zeros((N_ODD, bp, S5_G, S5_P)),
        zeros((N_ODD, bp, S5_G, S5_P)), zeros((N_ODD, bp, HG_H, HG_D, HG_D)))
    y_s, gla_s, conv_s, lru_s, re_s, im_s, hg_s = run(
        x_sample, c_sample, state_gla, state_rglru_conv, state_rglru_h,
        state_s5_re, state_s5_im, state_hgrn)
    return (y_p.astype(x_prompt.dtype), y_s.astype(x_sample.dtype), gla_p, gla_s, conv_p, conv_s,
            lru_p, lru_s, re_p, re_s, im_p, im_s, hg_p, hg_s)
```

```python
import numpy as np
import concourse.bass as bass
import concourse.mybir as mybir
from concourse.bass_utils import run_bass_kernel_spmd
from contextlib import ExitStack

F32 = mybir.dt.float32
BF16 = mybir.dt.bfloat16
ALU = mybir.AluOpType
AF = mybir.ActivationFunctionType
AX = mybir.AxisListType

NCORES = 8
D = 1024
L = 2048
NT = 16
NS = 16
TW = L + NS
DFF = 2816
ALPHA = 4.0 ** 0.25
LN_EPS = 1e-5
RMS_EPS = 1e-6
GELU_C = 1.5957691216057308


class StopBuild(Exception):
    pass


import os as _os
KSTOP = float(_os.environ.get("KSTOP", "99"))


_DEAD = [False]


def stage(n):
    if n > KSTOP:
        _DEAD[0] = True


class Dep:
    __slots__ = ("w", "re", "rd")

    def __init__(self):
        self.w = None
        self.re = {}
        self.rd = []


class DSem:
    def __init__(self, sem, i):
        self.sem = sem
        self.cnt = 0
        self.id = i


class Prog:
    def __init__(self, nc, es):
        self.nc = nc
        self.es = es
        self.engs = {"pe": nc.tensor, "act": nc.scalar, "dve": nc.vector, "pool": nc.gpsimd, "sp": nc.sync}
        self.sem = {k: es.enter_context(nc.semaphore("sem_" + k)) for k in self.engs}
        self.cnt = {k: 0 for k in self.engs}
        self.known = {k: {} for k in self.engs}
        self.dsems = []

    def dsem(self):
        s = DSem(self.es.enter_context(self.nc.semaphore("dsem%d" % len(self.dsems))), len(self.dsems))
        self.dsems.append(s)
        return s

    def _wait(self, e, tok):
        if tok[0] == "e":
            _, src, val = tok
            if self.known[e].get(src, 0) >= val:
                return
            self.known[e][src] = val
            self.engs[e].wait_ge(self.sem[src], val)
        else:
            ds = tok[1]
            val = ds.cnt
            key = ("d", ds.id)
            if self.known[e].get(key, 0) >= val:
                return
            self.known[e][key] = val
            self.engs[e].wait_ge(ds.sem, val)

    def _deps(self, e, reads, writes, dma_group=False):
        for d in reads:
            for t in self._wl(d):
                self._wait(e, t)
        for d in writes:
            if dma_group and d.w is not None and all(t[0] == "d" for t in self._wl(d)) and not d.re and not d.rd:
                continue
            for t in self._wl(d):
                self._wait(e, t)
            for src, val in d.re.items():
                self._wait(e, ("e", src, val))
            for t in d.rd:
                self._wait(e, t)

    @staticmethod
    def _wl(d):
        if d.w is None:
            return []
        return d.w if isinstance(d.w, list) else [d.w]

    def _done(self, tok, reads, writes, dma_group=False):
        for d in reads:
            if tok[0] == "e":
                d.re[tok[1]] = tok[2]
            else:
                d.rd = [t for t in d.rd if t[1] is not tok[1]] + [tok]
        for d in writes:
            if dma_group and d.w is not None and all(t[0] == "d" for t in self._wl(d)) and not d.re and not d.rd:
                d.w = [t for t in self._wl(d) if t[1] is not tok[1]] + [tok]
                continue
            d.w = tok
            d.re = {}
            d.rd = []

    def op(self, e, fn, reads=(), writes=()):
        if _DEAD[0]:
            return
        self._deps(e, reads, writes)
        ins = fn(self.engs[e])
        self.cnt[e] += 1
        ins.then_inc(self.sem[e], 1)
        self._done(("e", e, self.cnt[e]), reads, writes)

    def dma(self, q, out, in_, ds, reads=(), writes=()):
        if _DEAD[0]:
            return
        if ds is None or ds == "in" or ds == "out":
            if not hasattr(self, "pool_sems"):
                self.pool_sems = [self.dsem() for _ in range(24)]
                self.pool_i = 0
            ds = self.pool_sems[self.pool_i % len(self.pool_sems)]
            self.pool_i += 1
            if ds.cnt > 0:
                self._wait(q, ("d", ds))
        self._deps(q, reads, writes, dma_group=True)
        ins = self.engs[q].dma_start(out=out, in_=in_)
        ds.cnt += 16
        ins.then_inc(ds.sem, 16)
        self._done(("d", ds), reads, writes, dma_group=True)

    def barrier(self):
        for e in self.engs:
            for src in self.engs:
                if src != e and self.cnt[src] > 0:
                    self._wait(e, ("e", src, self.cnt[src]))
            for ds in self.dsems:
                if ds.cnt > 0:
                    self._wait(e, ("d", ds))


def build_program():
    nc = bass.Bass("TRN2", target_bir_lowering=False)

    def din(name, shape):
        return nc.dram_tensor(name, list(shape), F32, kind="ExternalInput").ap()

    def dout(name, shape):
        return nc.dram_tensor(name, list(shape), F32, kind="ExternalOutput").ap()

    xp = din("xp", [L, D])
    xsT_d = din("xsT", [128, 8, NS])
    cT_d = din("cT", [128, 8, 17])
    w_ada = din("w_ada", [2, D, 6 * D])
    b_adaT_d = din("b_adaT", [128, 2, 48])
    ln_gb_d = din("ln_gb", [1, 8 * D])
    ln_gT_d = din("ln_gT", [128, 4, 8])
    ln_bT_d = din("ln_bT", [128, 4, 8])
    ev_w_in = din("ev_w_in", [D, 2576])
    ev_w_out = din("ev_w_out", [D, D])
    od_w_in = din("od_w_in", [D, 2560])
    od_w_out = din("od_w_out", [D, D])
    ffn_w_in = din("ffn_w_in", [2, D, 2 * DFF])
    ffn_w_out = din("ffn_w_out", [2, DFF, D])
    w_glu = din("w_glu", [512, 512])
    gla_w_lr_d = din("gla_w_lr", [16, 256])
    gla_b_lrT_d = din("gla_b_lrT", [128, 2])
    gla_ng_d = din("gla_ng", [1, 512])
    hg_ng_d = din("hg_ng", [1, 512])
    conv_wT_d = din("conv_wT", [128, 4, 4])
    lru_vecT_d = din("lru_vecT", [128, 4, 4])
    lru_wbd_d = din("lru_wbd", [128, 8, 128])
    s5_vecT_d = din("s5_vecT", [128, 3, 16])
    s5_bbd_d = din("s5_bbd", [128, 32, 128])
    s5_cbd_d = din("s5_cbd", [128, 32, 128])
    s5_dT_d = din("s5_dT", [128, 2, 4])
    hg_lbT_d = din("hg_lbT", [128, 2, 4])
    st_gla_d = din("st_gla", [128, NS, 2, 128])
    st_conv_d = din("st_conv", [128, 4, 3, NS])
    st_lru_d = din("st_lru", [128, 4, NS])
    st_s5_d = din("st_s5", [128, 2, 16, NS])
    st_hg_d = din("st_hg", [128, NS, 4, 128])

    y_p = dout("y_p", [L, D])
    y_sT = dout("y_sT", [128, 8, NS])
    o_gla_p = dout("o_gla_p", [128, 2, 128])
    o_gla_s = dout("o_gla_s", [128, NS, 2, 128])
    o_conv_p = dout("o_conv_p", [128, 4, 3])
    o_conv_s = dout("o_conv_s", [128, 4, 3, NS])
    o_lru_p = dout("o_lru_p", [128, 4])
    o_lru_s = dout("o_lru_s", [128, 4, NS])
    o_s5_p = dout("o_s5_p", [128, 2, 16])
    o_s5_s = dout("o_s5_s", [128, 2, 16, NS])
    o_hg_p = dout("o_hg_p", [128, 4, 128])
    o_hg_s = dout("o_hg_s", [128, NS, 4, 128])

    es = ExitStack()
    with es:
        P = Prog(nc, es)

        _nm = [0]

        def sb(name, shape, dt=F32, stack=es):
            _nm[0] += 1
            return stack.enter_context(nc.sbuf_tensor("t%d_%s" % (_nm[0], name), list(shape), dt))

        X = sb("X", [128, NT, D])
        XsT = sb("XsT", [128, 8, NS])
        hT = sb("hT", [128, 8, TW], BF16)
        modT = sb("modT", [128, 2, 48, 17])
        ident = sb("ident", [128, 128])
        identb = sb("identb", [128, 128], BF16)
        onesf = sb("onesf", [128, 128])
        cmask = sb("cmask", [128, 4, 64], BF16)
        pp2 = [es.enter_context(nc.psum_tensor("pp%d" % i, [128, 1024], F32)) for i in range(2)]
        ps = [None] * 4 + [es.enter_context(nc.psum_tensor("ps%d" % i, [128, 512], F32)) for i in range(4, 8)]
        dpp = [Dep(), Dep()]

        dX = [Dep() for _ in range(NT)]
        dXs = Dep()
        dhT = Dep()
        dhTs = Dep()
        dyT = [Dep() for _ in range(8)]
        dyTs = Dep()
        dmod = Dep()
        dconst = Dep()
        dbc = [Dep() for _ in range(3)]
        dps = [Dep() for _ in range(8)]
        s_out = "out"
        s_in = "in"

        def c_ident(e):
            e.memset(ident[:], 0.0)
            e.memset(onesf[:], 1.0)
            return e.memset(cmask[:], 1.0)
        P.op("pool", c_ident, writes=[dconst])

        def c_sel(e):
            e.affine_select(out=ident[:], in_=onesf[:], pattern=[[-1, 128]], compare_op=ALU.is_equal,
                            fill=0.0, base=0, channel_multiplier=1)
            e.affine_select(out=cmask[0:64], in_=cmask[0:64], pattern=[[0, 4], [1, 64]], compare_op=ALU.is_ge,
                            fill=0.0, base=0, channel_multiplier=-1)
            return e.affine_select(out=cmask[64:128], in_=cmask[64:128], pattern=[[0, 4], [1, 64]], compare_op=ALU.is_ge,
                                   fill=0.0, base=0, channel_multiplier=-1)
        P.op("pool", c_sel, reads=[dconst], writes=[dconst])
        P.op("dve", lambda e: e.tensor_copy(out=identb[:], in_=ident[:]), reads=[dconst], writes=[dconst])

        xpv = xp.rearrange("(n p) d -> p n d", p=128)
        for tt in range(NT):
            P.dma("sp", X[:, tt, :], xpv[:, tt, :], s_in, writes=[dX[tt]])
        P.dma("sp", XsT[:], xsT_d[:, :, :], s_in, writes=[dXs])

        def setup_mod(stack):
            cT = sb("cT", [128, 8, 17], stack=stack)
            condT = sb("condT", [128, 8, 17], BF16, stack=stack)
            b_adaT = sb("b_adaT", [128, 2, 48], stack=stack)
            wab = [sb("wab%d" % i, [128, 8, 512], BF16, stack=stack) for i in range(2)]
            mrow = [sb("mrow%d" % i, [17, 512], stack=stack) for i in range(2)]
            dwab = [Dep(), Dep()]
            swab = [P.dsem(), P.dsem()]
            dc = Dep()
            dmrow = [Dep(), Dep()]
            P.dma("sp", cT[:], cT_d[:, :, :], s_in, writes=[dc])
            P.dma("sp", b_adaT[:], b_adaT_d[:, :, :], s_in, writes=[dc])
            P.op("act", lambda e: e.activation(out=condT[:], in_=cT[:], func=AF.Silu), reads=[dc], writes=[dc])
            blocks = [(l, blk) for l in range(2) for blk in range(12)]
            nb_ = len(blocks)
            stt = {"dma": 0, "A": 0, "B": 0, "C": 0, "D": 0}

            def dma(i):
                l, blk = blocks[i]
                wv = w_ada[l].rearrange("(c p) f -> p c f", p=128)
                sl_ = i % 2
                for c in range(8):
                    P.dma("pool", wab[sl_][:, c, :], wv[:, c, blk * 512:(blk + 1) * 512], swab[sl_], writes=[dwab[sl_]])

            def stageA(i):
                sl_ = i % 2

                def mmf(e):
                    ins = None
                    for c in range(8):
                        ins = e.matmul(ps[5][0:17, 0:512], lhsT=condT[:, c, :], rhs=wab[sl_][:, c, :], start=(c == 0), stop=(c == 7))
                    return ins
                P.op("pe", mmf, reads=[dwab[sl_], dc], writes=[dps[5]])

            def stageB(i):
                k = i % 2
                P.op("act", lambda e: e.activation(out=mrow[k][:], in_=ps[5][0:17, 0:512], func=AF.Copy), reads=[dps[5]], writes=[dmrow[k]])

            def stageC(i):
                k = i % 2
                pb = 6 + k

                def trf(e):
                    ins = None
                    for q in range(4):
                        ins = e.transpose(out=ps[pb][:, q * 17:(q + 1) * 17], in_=mrow[k][0:17, q * 128:(q + 1) * 128], identity=ident[0:17, 0:17])
                    return ins
                P.op("pe", trf, reads=[dmrow[k], dconst], writes=[dps[pb]])

            def stageD(i):
                l, blk = blocks[i]
                pb = 6 + i % 2
                P.op("dve", lambda e: e.tensor_tensor(
                    out=modT[:, l, blk * 4:(blk + 1) * 4, :],
                    in0=ps[pb][:, 0:68].rearrange("p (j n) -> p j n", n=17),
                    in1=b_adaT[:, l, blk * 4:(blk + 1) * 4].unsqueeze(2).broadcast_to([128, 4, 17]), op=ALU.add),
                    reads=[dps[pb], dc], writes=[dmod])

            def slot():
                if stt["D"] < stt["C"]:
                    stageD(stt["D"])
                    stt["D"] += 1
                if stt["C"] < stt["B"]:
                    stageC(stt["C"])
                    stt["C"] += 1
                if stt["B"] < stt["A"]:
                    stageB(stt["B"])
                    stt["B"] += 1
                if stt["A"] < nb_:
                    i = stt["A"]
                    while stt["dma"] <= min(i + 1, nb_ - 1):
                        dma(stt["dma"])
                        stt["dma"] += 1
                    stageA(i)
                    stt["A"] += 1

            def issue(n):
                for _ in range(n):
                    if stt["D"] >= nb_:
                        return
                    slot()

            def drain():
                while stt["D"] < stt["A"]:
                    if stt["D"] < stt["C"]:
                        stageD(stt["D"])
                        stt["D"] += 1
                    if stt["C"] < stt["B"]:
                        stageC(stt["C"])
                        stt["C"] += 1
                    if stt["B"] < stt["A"]:
                        stageB(stt["B"])
                        stt["B"] += 1
            issue.drain = drain
            return issue

        def modp(l, g):
            return modT[:, l, g * 8:(g + 1) * 8, 0]

        def mods(l, g):
            return modT[:, l, g * 8:(g + 1) * 8, 1:17]

        small = sb("small", [128, 64])
        dsmall = Dep()
        hs_tmp = sb("hs_tmp", [128, 8, NS])
        dhs_tmp = Dep()
        wfm = [sb("wfm%d" % i, [128, 8, 128], BF16) for i in range(3)]
        dwfm = [Dep() for _ in range(3)]
        swfm = [P.dsem() for _ in range(3)]
        wcnt = [0]
        pcnt = [0]
        k67 = [0]

        def nxt67():
            k67[0] += 1
            return 6 + (k67[0] % 2)

        def load_w(buf, dep, ds, wsrc, c0, n, nk=8):
            wv = wsrc.rearrange("(c p) f -> p c f", p=128)
            P.dma("pool", buf[:, 0:nk, 0:n], wv[:, :, c0:c0 + n], ds, writes=[dep])

        def make_hT(l, g_sc, g_sh):
            P.op("dve", lambda e: e.tensor_scalar(out=small[:, 0:8], in0=modp(l, g_sc), scalar1=1.0, scalar2=None, op0=ALU.add),
                 reads=[dmod], writes=[dsmall])
            for tb in range(4):
                for dc_ in range(8):
                    pb = nxt67()

                    def tr(e, tb=tb, dc_=dc_, pb=pb):
                        ins = None
                        for q in range(4):
                            ins = e.transpose(out=ps[pb][:, q * 128:(q + 1) * 128], in_=X[:, tb * 4 + q, dc_ * 128:(dc_ + 1) * 128],
                                              identity=ident[:])
                        return ins
                    P.op("pe", tr, reads=[dX[tb * 4 + q] for q in range(4)] + [dconst], writes=[dps[pb]])
                    P.op("act", lambda e, tb=tb, dc_=dc_, pb=pb: e.activation(
                        out=hT[:, dc_, tb * 512:(tb + 1) * 512], in_=ps[pb][:], func=AF.Identity,
                        scale=small[:, dc_:dc_ + 1], bias=modp(l, g_sh)[:, dc_:dc_ + 1]),
                        reads=[dps[pb], dsmall, dmod], writes=[dhT])
            P.op("dve", lambda e: e.scalar_tensor_tensor(out=hs_tmp[:], in0=mods(l, g_sc), scalar=1.0, in1=XsT[:],
                                                         op0=ALU.add, op1=ALU.mult),
                 reads=[dmod, dXs], writes=[dhs_tmp])
            P.op("dve", lambda e: e.tensor_tensor(out=hT[:, :, L:TW], in0=hs_tmp[:], in1=mods(l, g_sh), op=ALU.add),
                 reads=[dhs_tmp, dmod], writes=[dhTs])

        def proj_fm(wsrc, c0, n, src, dsrc_p, dsrc_s, cons_p, cons_s, nk=8, wbuf=None):
            if wbuf is None:
                i = wcnt[0] % 3
                wcnt[0] += 1
                load_w(wfm[i], dwfm[i], swfm[i], wsrc, c0, n, nk)
                wb, dwb = wfm[i], dwfm[i]
                lw = lambda c: wb[:, c, 0:n]
            else:
                lw, dwb = wbuf
            if cons_p is not None:
                for th in range(2):
                    j = pcnt[0] % 2
                    pcnt[0] += 1

                    def mm(e, th=th, j=j):
                        ins = None
                        for tb in range(2):
                            for c in range(nk):
                                ins = e.matmul(pp2[j][0:n, tb * 512:(tb + 1) * 512], lhsT=lw(c),
                                               rhs=src[:, c, th * 1024 + tb * 512: th * 1024 + (tb + 1) * 512],
                                               start=(c == 0), stop=(c == nk - 1))
                        return ins
                    P.op("pe", mm, reads=[dwb, dsrc_p], writes=[dpp[j]])
                    cons_p(th, pp2[j][0:n, :], dpp[j])
            if cons_s is not None:
                def mms(e):
                    ins = None
                    for c in range(nk):
                        ins = e.matmul(ps[4][0:n, 0:NS], lhsT=lw(c), rhs=src[:, c, L:TW], start=(c == 0), stop=(c == nk - 1))
                    return ins
                P.op("pe", mms, reads=[dwb, dsrc_s], writes=[dps[4]])
                cons_s(ps[4][0:n, 0:NS], dps[4])

        def gelu_from(src_ap, dsrc, out_ap, dout, tmp, dtmp, n):
            P.op("act", lambda e: e.activation(out=tmp, in_=src_ap, func=AF.Square), reads=[dsrc], writes=[dtmp])
            P.op("dve", lambda e: e.tensor_scalar(out=tmp, in0=tmp, scalar1=0.044715, scalar2=1.0, op0=ALU.mult, op1=ALU.add),
                 reads=[dtmp], writes=[dtmp])
            P.op("dve", lambda e: e.tensor_tensor(out=tmp, in0=src_ap, in1=tmp, op=ALU.mult), reads=[dsrc, dtmp], writes=[dtmp])
            P.op("act", lambda e: e.activation(out=tmp, in_=tmp, func=AF.Sigmoid, scale=GELU_C), reads=[dtmp], writes=[dtmp])
            P.op("dve", lambda e: e.tensor_tensor(out=out_ap, in0=src_ap, in1=tmp, op=ALU.mult), reads=[dsrc, dtmp], writes=[dout])

        I16 = sb("I16", [128, NS, NS], BF16)
        P.op("pool", lambda e: e.memset(I16[:], 1.0), writes=[dconst], reads=[dconst])
        P.op("pool", lambda e: e.affine_select(out=I16[:], in_=I16[:], pattern=[[1, NS], [-1, NS]], compare_op=ALU.is_equal,
                                                fill=0.0, base=0, channel_multiplier=0), reads=[dconst], writes=[dconst])
        stg = sb("stg", [128, 64])
        dstg = Dep()

        def gla_chunks(dk, nh, qsel, kt, kd, vtok, gg, el, S, Sb_unused, ydst, dyT, deps_in, S_out_dram):
            hp = 128 // dk
            nfq = nh // hp
            NV = nh * 128
            NC = 2 * NT
            dS_ = Dep()
            with ExitStack() as st:
                scT = [sb("scT%d" % i, [128, nh, 64], BF16, stack=st) for i in range(2)]
                dscT = [Dep(), Dep()]
                Sb2 = [sb("Sb2_%d" % i, [128, nfq, 128], BF16, stack=st) for i in range(2)]
                dSb2 = [Dep(), Dep()]
                osq = sb("osq", [128, NV], stack=st)
                rs = sb("rs", [128, 2 * nh], stack=st)
                ytok = [sb("ytok%d" % i, [128, NV], BF16, stack=st) for i in range(2)]
                dytok = [Dep(), Dep()]
                dloc = Dep()
                dppB = [Dep(), Dep()]
                P.op("dve", lambda e: e.memset(S[:], 0.0), writes=[dS_])
                for i in range(2):
                    P.op("dve", lambda e, i=i: e.memset(Sb2[i][:], 0.0), writes=[dSb2[i]])
                    P.op("dve", lambda e, i=i: e.memset(scT[i][:], 0.0), writes=[dscT[i]])

                def issue_scores(c):
                    tt, cc = divmod(c, 2)
                    po, t0, j = cc * 64, c * 64, cc

                    def sc_mm(e):
                        ins = None
                        for hl in range(nh):
                            fq = hl // hp
                            ins = e.matmul(pp2[j][po:po + 64, hl * 64:(hl + 1) * 64], lhsT=kt[:, fq, t0:t0 + 64],
                                           rhs=qsel(hl)[:, t0:t0 + 64], start=True, stop=True)
                        return ins
                    P.op("pe", sc_mm, reads=deps_in, writes=[dpp[j]])
                    P.op("dve", lambda e: e.tensor_tensor(
                        out=scT[j][po:po + 64], in0=pp2[j][po:po + 64, 0:nh * 64].rearrange("p (h t) -> p h t", t=64),
                        in1=cmask[po:po + 64, 0:nh, :], op=ALU.mult), reads=[dpp[j], dconst], writes=[dscT[j]])

                def issue_state(c):
                    tt, cc = divmod(c, 2)
                    po = cc * 64
                    db = 7 if cc == 0 else 4

                    def ds_mm(e):
                        ins = None
                        for hl in range(nh):
                            pr = (hl % hp) * dk
                            fq = hl // hp
                            ins = e.matmul(ps[db][pr:pr + dk, fq * 128:(fq + 1) * 128], lhsT=kd[po:po + 64, tt, hl * dk:(hl + 1) * dk],
                                           rhs=vtok[po:po + 64, tt, hl * 128:(hl + 1) * 128], start=True, stop=True)
                        return ins
                    P.op("pe", ds_mm, reads=deps_in, writes=[dps[db]])
                    for fq in range(nfq):
                        P.op("dve", lambda e, fq=fq: e.scalar_tensor_tensor(
                            out=S[:, fq, :], in0=S[:, fq, :], scalar=el[:, fq, c:c + 1], in1=ps[db][:, fq * 128:(fq + 1) * 128],
                            op0=ALU.mult, op1=ALU.add), reads=[dps[db]] + deps_in, writes=[dS_])
                    P.op("act", lambda e: e.activation(out=Sb2[c % 2][:], in_=S[:], func=AF.Copy), reads=[dS_], writes=[dSb2[c % 2]])

                def issue_out(c):
                    tt, cc = divmod(c, 2)
                    po, t0, j = cc * 64, c * 64, cc
                    ob = pp2[tt % 2]
                    sprev = Sb2[(c - 1) % 2]

                    def o_mm(e):
                        ins = None
                        for hl in range(nh):
                            fq = hl // hp
                            e.matmul(ob[po:po + 64, 512 + hl * 128:512 + (hl + 1) * 128], lhsT=scT[j][:, hl, :],
                                     rhs=vtok[:, tt, hl * 128:(hl + 1) * 128], start=True, stop=False)
                            ins = e.matmul(ob[po:po + 64, 512 + hl * 128:512 + (hl + 1) * 128], lhsT=qsel(hl)[:, t0:t0 + 64],
                                           rhs=sprev[:, fq, 0:128], start=False, stop=True)
                        return ins
                    P.op("pe", o_mm, reads=deps_in + [dscT[j], dSb2[(c - 1) % 2]], writes=[dppB[tt % 2]])

                def issue_epilogue(tt):
                    ob = pp2[tt % 2][:, 512:512 + NV]
                    dob = dppB[tt % 2]
                    yk = ytok[tt % 2]
                    P.op("act", lambda e: e.activation(out=osq[:], in_=ob, func=AF.Square), reads=[dob], writes=[dloc])
                    P.op("dve", lambda e: e.tensor_reduce(out=rs[:, 0:nh], in_=osq[:].rearrange("p (h v) -> p h v", v=128),
                                                          axis=AX.X, op=ALU.add), reads=[dloc], writes=[dloc])
                    P.op("act", lambda e: e.activation(out=rs[:, 0:nh], in_=rs[:, 0:nh], func=AF.Sqrt, scale=1.0 / 128, bias=RMS_EPS),
                         reads=[dloc], writes=[dloc])
                    P.op("dve", lambda e: e.reciprocal(out=rs[:, nh:2 * nh], in_=rs[:, 0:nh]), reads=[dloc], writes=[dloc])
                    P.op("dve", lambda e: e.tensor_tensor(out=osq[:].rearrange("p (h v) -> p h v", v=128),
                                                          in0=ob.rearrange("p (h v) -> p h v", v=128),
                                                          in1=rs[:, nh:2 * nh].unsqueeze(2).broadcast_to([128, nh, 128]), op=ALU.mult),
                         reads=[dob, dloc], writes=[dloc])
                    P.op("dve", lambda e: e.tensor_tensor(out=yk[:], in0=osq[:], in1=gg[:, tt, :], op=ALU.mult),
                         reads=[dloc] + deps_in, writes=[dytok[tt % 2]])

                def issue_ytr(tt):
                    yk = ytok[tt % 2]
                    pb = 6 if tt % 2 == 0 else 5

                    def ytr(e):
                        ins = None
                        for q in range(nh):
                            ins = e.transpose(out=ps[pb][:].bitcast(BF16)[:, q * 128:(q + 1) * 128], in_=yk[:, q * 128:(q + 1) * 128],
                                              identity=identb[:])
                        return ins
                    P.op("pe", ytr, reads=[dytok[tt % 2], dconst], writes=[dps[pb]])
                    P.op("act", lambda e: e.activation(
                        out=ydst(tt), in_=ps[pb][:].bitcast(BF16)[:, 0:NV].rearrange("p (q t) -> p q t", t=128), func=AF.Copy),
                        reads=[dps[pb]], writes=[dyT])

                issue_scores(0)
                for c in range(NC):
                    tt, cc = divmod(c, 2)
                    issue_state(c)
                    if c + 1 < NC:
                        issue_scores(c + 1)
                    issue_out(c)
                    if cc == 1:
                        issue_epilogue(tt)
                        if tt >= 1:
                            issue_ytr(tt - 1)
                issue_ytr(NT - 1)
                P.dma("sp", S_out_dram, S[:], s_out, reads=[dS_])
                P.barrier()

        def transpose_kd(kdT, dkdT, kd, dkd, fq, nf):
            for tb in range(4):
                pb = nxt67()

                def tr(e, tb=tb, pb=pb):
                    ins = None
                    for q in range(4):
                        ins = e.transpose(out=ps[pb][:].bitcast(BF16)[:, q * 128:(q + 1) * 128],
                                          in_=kdT[:, (tb * 4 + q) * 128:(tb * 4 + q + 1) * 128], identity=identb[:])
                    return ins
                P.op("pe", tr, reads=[dkdT, dconst], writes=[dps[pb]])
                P.op("act", lambda e, tb=tb, pb=pb: e.activation(
                    out=kd[:, tb * 4:(tb + 1) * 4, fq * 128:(fq + 1) * 128],
                    in_=ps[pb][:].bitcast(BF16)[:, 0:512].rearrange("p (q t) -> p q t", t=128), func=AF.Copy),
                    reads=[dps[pb]], writes=[dkd])

        def proj_tm(wsrc, c0, wt, dwt, swt, cons_p, cons_s, n=512):
            load_w(wt, dwt, swt, wsrc, c0, n)
            for tt in range(NT):
                pb = 5 + (tt % 3)

                def mm(e, tt=tt, pb=pb):
                    ins = None
                    for c in range(8):
                        ins = e.matmul(ps[pb][:, 0:n], lhsT=hT[:, c, tt * 128:(tt + 1) * 128], rhs=wt[:, c, 0:n], start=(c == 0), stop=(c == 7))
                    return ins
                P.op("pe", mm, reads=[dwt, dhT], writes=[dps[pb]])
                cons_p(tt, ps[pb][:, 0:n], dps[pb])

            def mms(e):
                ins = None
                for c in range(8):
                    ins = e.matmul(ps[5][0:NS, 0:n], lhsT=hT[:, c, L:TW], rhs=wt[:, c, 0:n], start=(c == 0), stop=(c == 7))
                return ins
            P.op("pe", mms, reads=[dwt, dhTs], writes=[dps[5]])
            cons_s(ps[5][0:NS, 0:n], dps[5])

        def sample_state_update(dk, q_s, k_tok, v_tok, a_s, Sst, dSst, dq, st, tag):
            hp = 128 // dk
            nfq = 4 // hp
            KM = sb("KM" + tag, [NS, NS, 4 * dk], BF16, stack=st)
            QM = sb("QM" + tag, [128, 4, NS, NS], stack=st)
            qz = sb("qz" + tag, [128, 4, NS], stack=st)
            dkm = Dep()
            P.op("dve", lambda e: e.tensor_tensor(out=KM[:], in0=k_tok.unsqueeze(1).broadcast_to([NS, NS, 4 * dk]),
                                                  in1=ident[0:NS, 0:NS].unsqueeze(2).broadcast_to([NS, NS, 4 * dk]), op=ALU.mult),
                 reads=dq + [dconst], writes=[dkm])
            P.op("dve", lambda e: e.memset(qz[:], 0.0), writes=[dkm])
            for h in range(4):
                pr = (h % hp) * dk
                P.op("dve", lambda e, h=h, pr=pr: e.tensor_copy(out=qz[pr:pr + dk, h, :], in_=q_s[pr:pr + dk, h // hp, :]), reads=dq + [dkm], writes=[dkm])
            P.op("dve", lambda e: e.tensor_tensor(out=QM[:], in0=qz[:].unsqueeze(2).broadcast_to([128, 4, NS, NS]),
                                                  in1=I16[:].unsqueeze(1).broadcast_to([128, 4, NS, NS]), op=ALU.mult),
                 reads=dq + [dconst, dkm], writes=[dkm])
            grp = 0
            for h in range(4):
                pr = (h % hp) * dk
                fq = h // hp
                for b0 in range(0, NS, 4):
                    db = 7 if grp % 2 == 0 else 4
                    grp += 1

                    def dsm(e, h=h, b0=b0, pr=pr, db=db):
                        ins = None
                        for q in range(4):
                            ins = e.matmul(ps[db][pr:pr + dk, q * 128:(q + 1) * 128], lhsT=KM[0:NS, b0 + q, h * dk:(h + 1) * dk],
                                           rhs=v_tok[0:NS, h * 128:(h + 1) * 128], start=True, stop=True)
                        return ins
                    P.op("pe", dsm, reads=[dkm] + dq, writes=[dps[db]])
                    for q in range(4):
                        b = b0 + q
                        P.op("dve", lambda e, b=b, q=q, pr=pr, fq=fq, db=db: e.scalar_tensor_tensor(
                            out=Sst[pr:pr + dk, b, fq, :], in0=Sst[pr:pr + dk, b, fq, :], scalar=a_s[pr:pr + dk, fq, b:b + 1],
                            in1=ps[db][pr:pr + dk, q * 128:(q + 1) * 128], op0=ALU.mult, op1=ALU.add), reads=[dps[db]] + dq, writes=[dSst])

                    def omm(e, h=h, b0=b0, fq=fq):
                        ins = None
                        for q in range(4):
                            b = b0 + q
                            ins = e.matmul(ps[5][0:NS, h * 128:(h + 1) * 128], lhsT=QM[:, h, b, :], rhs=Sst[:, b, fq, :],
                                           start=(b == 0), stop=(b == NS - 1))
                        return ins
                    P.op("pe", omm, reads=[dkm, dSst], writes=[dps[5]])

        def rms_gate_sample(gg_s, dgg, ydst_s, dyTs, st, tag):
            osq = sb("osq_s" + tag, [NS, 512], stack=st)
            rs = sb("rs_s" + tag, [NS, 8], stack=st)
            d = Dep()
            P.op("act", lambda e: e.activation(out=osq[:], in_=ps[5][0:NS, :], func=AF.Square), reads=[dps[5]], writes=[d])
            P.op("dve", lambda e: e.tensor_reduce(out=rs[:, 0:4], in_=osq[:].rearrange("p (h v) -> p h v", v=128), axis=AX.X, op=ALU.add),
                 reads=[d], writes=[d])
            P.op("act", lambda e: e.activation(out=rs[:, 0:4], in_=rs[:, 0:4], func=AF.Sqrt, scale=1.0 / 128, bias=RMS_EPS), reads=[d], writes=[d])
            P.op("dve", lambda e: e.reciprocal(out=rs[:, 4:8], in_=rs[:, 0:4]), reads=[d], writes=[d])
            P.op("dve", lambda e: e.tensor_tensor(out=osq[:].rearrange("p (h v) -> p h v", v=128),
                                                  in0=ps[5][0:NS, :].rearrange("p (h v) -> p h v", v=128),
                                                  in1=rs[:, 4:8].unsqueeze(2).broadcast_to([NS, 4, 128]), op=ALU.mult),
                 reads=[dps[5], d], writes=[d])
            P.op("dve", lambda e: e.tensor_tensor(out=osq[:], in0=osq[:], in1=gg_s, op=ALU.mult), reads=[d, dgg], writes=[d])

            def tr(e):
                ins = None
                for q in range(4):
                    ins = e.transpose(out=ps[6][:, q * NS:(q + 1) * NS], in_=osq[0:NS, q * 128:(q + 1) * 128], identity=ident[0:NS, 0:NS])
                return ins
            P.op("pe", tr, reads=[d, dconst], writes=[dps[6]])
            P.op("act", lambda e: e.activation(out=ydst_s, in_=ps[6][:, 0:4 * NS].rearrange("p (q t) -> p q t", t=NS), func=AF.Copy),
                 reads=[dps[6]], writes=[dyTs])
        def make_rmask(rmask, drm):
            P.op("pool", lambda e: e.memset(rmask[:], 1.0), writes=[drm])
            P.op("pool", lambda e: e.affine_select(out=rmask[:].rearrange("p (c j) -> p c j", j=64),
                                                    in_=rmask[:].rearrange("p (c j) -> p c j", j=64),
                                                    pattern=[[0, 32], [1, 64]], compare_op=ALU.is_gt, fill=0.0, base=0,
                                                    channel_multiplier=0), reads=[drm], writes=[drm])

        def acopy(out, in_, reads, writes, eng="act"):
            if eng == "act":
                P.op("act", lambda e: e.activation(out=out, in_=in_, func=AF.Copy), reads=reads, writes=writes)
            else:
                P.op(eng, lambda e: e.tensor_copy(out=out, in_=in_), reads=reads, writes=writes)

        def even_mixer(l, yoth, dyoth, dyoths, bg=None, bg_stack=None):
            W = ev_w_in
            stage(3)
            with ExitStack() as st:
                lruv = sb("lruv", [128, 4, 4], stack=st)
                convw = sb("convw", [128, 4, 4], stack=st)
                wbd = sb("wbd", [128, 8, 128], BF16, stack=st)
                stconv = sb("stconv", [128, 4, 3, NS], stack=st)
                stlru = sb("stlru", [128, 4, NS], stack=st)
                oconv = sb("oconv", [128, 4, 3, NS], stack=st)
                olru = sb("olru", [128, 4, NS], stack=st)
                xrp = sb("xrp", [128, L + 3], stack=st)
                xc = sb("xc", [128, L], stack=st)
                xcb = sb("xcb", [128, L], BF16, stack=st)
                rr = sb("rr", [128, L], stack=st)
                ig = sb("ig", [128, L], stack=st)
                aa = sb("aa", [128, L], stack=st)
                gg = sb("gg", [128, L], stack=st)
                sm = sb("lru_sm", [128, 16, NS], stack=st)
                smb = sb("lru_smb", [128, NS], BF16, stack=st)
                dpar, dxr, dxc, drr, dig, daa, dgg, dsm, dost = Dep(), Dep(), Dep(), Dep(), Dep(), Dep(), Dep(), Dep(), Dep()
                swbd = P.dsem()
                P.dma("sp", lruv[:], lru_vecT_d[:, :, :], s_in, writes=[dpar])
                P.dma("sp", convw[:], conv_wT_d[:, :, :], s_in, writes=[dpar])
                P.dma("sp", stconv[:], st_conv_d[:, :, :, :], s_in, writes=[dpar])
                P.dma("sp", stlru[:], st_lru_d[:, :, :], s_in, writes=[dpar])
                P.dma("pool", wbd[:], lru_wbd_d[:, :, :], swbd, writes=[dpar])
                c8 = small[:, 16:20]
                c16 = small[:, 20:24]
                P.op("act", lambda e: e.activation(out=small[:, 24:28], in_=lruv[:, :, 3], func=AF.Exp, scale=-1.0), reads=[dpar], writes=[dsmall])
                P.op("act", lambda e: e.activation(out=small[:, 24:28], in_=small[:, 24:28], func=AF.Ln, bias=1.0), reads=[dsmall], writes=[dsmall])
                P.op("dve", lambda e: e.tensor_scalar(out=c8, in0=small[:, 24:28], scalar1=-8.0, scalar2=None, op0=ALU.mult), reads=[dsmall], writes=[dsmall])
                P.op("dve", lambda e: e.tensor_scalar(out=c16, in0=small[:, 24:28], scalar1=-16.0, scalar2=None, op0=ALU.mult), reads=[dsmall], writes=[dsmall])
                P.op("dve", lambda e: e.memset(xrp[:, 0:3], 0.0), writes=[dxr])
                for g in range(4):
                    def cons_xr(th, pap, dp):
                        acopy(xrp[:, 3 + th * 1024: 3 + (th + 1) * 1024], pap, [dp], [dxr])

                    def cons_xr_s(pap, dp):
                        acopy(sm[:, 0, :], pap, [dp], [dsm])
                    proj_fm(W, 1552 + g * 128, 128, hT, dhT, dhTs, cons_xr, cons_xr_s)
                    if bg is not None:
                        bg(1)
                    def cons_gr(th, pap, dp):
                        sl = slice(th * 1024, (th + 1) * 1024)
                        gelu_from(pap, dp, gg[:, sl], dgg, aa[:, sl], daa, 1024)

                    def cons_gr_s(pap, dp):
                        gelu_from(pap, dp, sm[:, 1, :], dsm, sm[:, 2, :], dsm, NS)
                    proj_fm(W, 2064 + g * 128, 128, hT, dhT, dhTs, cons_gr, cons_gr_s)
                    if bg is not None:
                        bg(1)
                    P.op("dve", lambda e, g=g: e.tensor_scalar(out=xc[:], in0=xrp[:, 3:3 + L], scalar1=convw[:, g, 3:4], scalar2=lruv[:, g, 0:1],
                                                              op0=ALU.mult, op1=ALU.add), reads=[dxr, dpar], writes=[dxc])
                    for j in range(3):
                        P.op("dve", lambda e, g=g, j=j: e.scalar_tensor_tensor(out=xc[:], in0=xrp[:, j:j + L], scalar=convw[:, g, j:j + 1], in1=xc[:],
                                                                           op0=ALU.mult, op1=ALU.add), reads=[dxr, dpar, dxc], writes=[dxc])
                    acopy(xcb[:], xc[:], [dxc], [dxc])
                    acopy(stg[:, g * 3:(g + 1) * 3], xrp[:, L:L + 3], [dxr], [dstg], eng="dve")
                    if bg is not None:
                        bg(1)
                    P.op("dve", lambda e, g=g: e.tensor_scalar(out=sm[:, 3, :], in0=sm[:, 0, :], scalar1=convw[:, g, 3:4], scalar2=lruv[:, g, 0:1],
                                                              op0=ALU.mult, op1=ALU.add), reads=[dsm, dpar], writes=[dsm])
                    for j in range(3):
                        P.op("dve", lambda e, g=g, j=j: e.scalar_tensor_tensor(out=sm[:, 3, :], in0=stconv[:, g, j, :], scalar=convw[:, g, j:j + 1],
                                                                           in1=sm[:, 3, :], op0=ALU.mult, op1=ALU.add), reads=[dsm, dpar], writes=[dsm])
                    acopy(smb[:], sm[:, 3, :], [dsm], [dsm])
                    acopy(oconv[:, g, 0:2, :], stconv[:, g, 1:3, :], [dpar], [dost], eng="dve")
                    acopy(oconv[:, g, 2, :], sm[:, 0, :], [dsm], [dost], eng="dve")
                    for gi, (dst, ddst) in enumerate(((rr, drr), (ig, dig))):
                        for th in range(2):
                            j = pcnt[0] % 2
                            pcnt[0] += 1

                            def gmm(e, th=th, j=j, gi=gi, g=g):
                                ins = None
                                for tb in range(2):
                                    ins = e.matmul(pp2[j][:, tb * 512:(tb + 1) * 512], lhsT=wbd[:, gi * 4 + g, :],
                                                   rhs=xcb[:, th * 1024 + tb * 512: th * 1024 + (tb + 1) * 512], start=True, stop=True)
                                return ins
                            P.op("pe", gmm, reads=[dpar, dxc], writes=[dpp[j]])
                            P.op("act", lambda e, th=th, j=j, gi=gi, g=g, dst=dst: e.activation(
                                out=dst[:, th * 1024:(th + 1) * 1024], in_=pp2[j][:, :], func=AF.Sigmoid, bias=lruv[:, g, 1 + gi:2 + gi]),
                                reads=[dpp[j], dpar], writes=[ddst])
                        P.op("pe", lambda e, gi=gi, g=g: e.matmul(ps[4][:, 0:NS], lhsT=wbd[:, gi * 4 + g, :], rhs=smb[:], start=True, stop=True),
                             reads=[dpar, dsm], writes=[dps[4]])
                        P.op("act", lambda e, gi=gi, g=g: e.activation(out=sm[:, 4 + gi, :], in_=ps[4][:, 0:NS], func=AF.Sigmoid,
                                                                      bias=lruv[:, g, 1 + gi:2 + gi]), reads=[dps[4], dpar], writes=[dsm])
                    if bg is not None:
                        bg(1)
                    P.op("act", lambda e, g=g: e.activation(out=aa[:], in_=rr[:], func=AF.Exp, scale=c8[:, g:g + 1]), reads=[drr, dsmall, dgg], writes=[daa])
                    P.op("act", lambda e, g=g: e.activation(out=rr[:], in_=rr[:], func=AF.Exp, scale=c16[:, g:g + 1]), reads=[dsmall], writes=[drr])
                    P.op("act", lambda e: e.activation(out=rr[:], in_=rr[:], func=AF.Sqrt, scale=-1.0, bias=1.0), reads=[], writes=[drr])
                    P.op("dve", lambda e: e.tensor_tensor(out=ig[:], in0=ig[:], in1=xc[:], op=ALU.mult), reads=[dxc], writes=[dig])
                    P.op("dve", lambda e: e.tensor_tensor(out=ig[:], in0=ig[:], in1=rr[:], op=ALU.mult), reads=[drr], writes=[dig])
                    P.op("dve", lambda e: e.tensor_tensor_scan(out=xc[:], data0=aa[:], data1=ig[:], initial=0.0, op0=ALU.mult, op1=ALU.add),
                         reads=[daa, dig], writes=[dxc])
                    P.op("dve", lambda e, g=g: e.tensor_tensor(out=yoth[:, g, 0:L], in0=xc[:], in1=gg[:], op=ALU.mult), reads=[dxc, dgg], writes=[dyoth])
                    acopy(stg[:, 12 + g:13 + g], xc[:, L - 1:L], [dxc], [dstg], eng="dve")
                    if bg is not None:
                        bg(1)
                    P.op("act", lambda e, g=g: e.activation(out=sm[:, 6, :], in_=sm[:, 4, :], func=AF.Exp, scale=c8[:, g:g + 1]), reads=[dsm, dsmall], writes=[dsm])
                    P.op("act", lambda e, g=g: e.activation(out=sm[:, 7, :], in_=sm[:, 4, :], func=AF.Exp, scale=c16[:, g:g + 1]), reads=[dsm, dsmall], writes=[dsm])
                    P.op("act", lambda e: e.activation(out=sm[:, 7, :], in_=sm[:, 7, :], func=AF.Sqrt, scale=-1.0, bias=1.0), reads=[dsm], writes=[dsm])
                    P.op("dve", lambda e: e.tensor_tensor(out=sm[:, 5, :], in0=sm[:, 5, :], in1=sm[:, 3, :], op=ALU.mult), reads=[dsm], writes=[dsm])
                    P.op("dve", lambda e: e.tensor_tensor(out=sm[:, 5, :], in0=sm[:, 5, :], in1=sm[:, 7, :], op=ALU.mult), reads=[dsm], writes=[dsm])
                    P.op("dve", lambda e, g=g: e.tensor_tensor(out=sm[:, 6, :], in0=sm[:, 6, :], in1=stlru[:, g, :], op=ALU.mult), reads=[dsm, dpar], writes=[dsm])
                    P.op("dve", lambda e, g=g: e.tensor_tensor(out=olru[:, g, :], in0=sm[:, 6, :], in1=sm[:, 5, :], op=ALU.add), reads=[dsm], writes=[dost])
                    P.op("dve", lambda e, g=g: e.tensor_tensor(out=yoth[:, g, L:TW], in0=olru[:, g, :], in1=sm[:, 1, :], op=ALU.mult), reads=[dsm, dost], writes=[dyoths])
                P.dma("sp", o_conv_s[:, :, :, :], oconv[:], s_out, reads=[dost])
                P.dma("sp", o_lru_s[:, :, :], olru[:], s_out, reads=[dost])
                P.dma("sp", o_conv_p.rearrange("p a b -> p (a b)"), stg[:, 0:12], s_out, reads=[dstg])
                P.dma("sp", o_lru_p[:, :], stg[:, 12:16], s_out, reads=[dstg])
                P.barrier()
            if bg is not None:
                bg(24)
                P.barrier()
                bg_stack.close()
            stage(4)
            with ExitStack() as st:
                qt = sb("qt", [128, 2, 2, L], BF16, stack=st)
                kt = sb("kt", [128, 2, L], BF16, stack=st)
                kd = sb("kd", [128, NT, 256], BF16, stack=st)
                elT = sb("elT", [128, 2, 32], stack=st)
                S = sb("S", [128, 2, 128], stack=st)
                Sb = sb("Sb", [128, 2, 128], BF16, stack=st)
                q_s = sb("q_s", [128, 2, NS], stack=st)
                a_s = sb("a_s", [128, 2, NS], stack=st)
                k_tok = sb("k_tok", [NS, 256], BF16, stack=st)
                v_tok = sb("v_tok", [NS, 512], BF16, stack=st)
                gg_s = sb("gg_s", [NS, 512], stack=st)
                gnB = sb("gnB", [128, 512], stack=st)
                dprep, dSst, dgs = Dep(), Dep(), Dep()
                P.op("pool", lambda e: e.memset(qt[:], 0.0), writes=[dprep])
                P.dma("sp", gnB[:], gla_ng_d.partition_broadcast(128), s_in, writes=[dgs])
                with ExitStack() as st2:
                    rmask = sb("rmask", [128, L], stack=st2)
                    csp = sb("csp", [128, L], stack=st2)
                    ee = sb("ee", [128, L], stack=st2)
                    kdT = sb("kdT", [128, L], BF16, stack=st2)
                    lrT = sb("lrT", [16, 1, TW], BF16, stack=st2)
                    wlr = sb("wlr", [16, 256], BF16, stack=st2)
                    blr = sb("blr", [128, 2], stack=st2)
                    sps = sb("sps", [128, NS], stack=st2)
                    drm, dcs, dee, ddd, dkdT, dlr, dw = Dep(), Dep(), Dep(), Dep(), Dep(), Dep(), Dep()
                    swl = P.dsem()
                    make_rmask(rmask, drm)
                    P.dma("pool", wlr[:], gla_w_lr_d[:, :], swl, writes=[dw])
                    P.dma("sp", blr[:], gla_b_lrT_d[:, :], s_in, writes=[dw])
                    P.op("dve", lambda e: e.tensor_scalar(out=blr[:], in0=blr[:], scalar1=-1.0, scalar2=None, op0=ALU.mult), reads=[dw], writes=[dw])
                    proj_fm(W, 1536, 16, hT, dhT, dhTs,
                            lambda th, pap, dp: acopy(lrT[:, 0, th * 1024:(th + 1) * 1024], pap, [dp], [dlr]),
                            lambda pap, dp: acopy(lrT[:, 0, L:TW], pap, [dp], [dlr]))
                    for fq in range(2):
                        def cons_g(th, pap, dp):
                            sl = slice(th * 1024, (th + 1) * 1024)
                            P.op("act", lambda e: e.activation(out=ee[:, sl], in_=pap, func=AF.Exp, scale=-1.0, bias=blr[:, fq:fq + 1]), reads=[dp, dw], writes=[dee])
                            P.op("act", lambda e: e.activation(out=ee[:, sl], in_=ee[:, sl], func=AF.Ln, bias=1.0), reads=[dee], writes=[dee])

                        def cons_g_s(pap, dp):
                            P.op("act", lambda e: e.activation(out=sps[:], in_=pap, func=AF.Exp, scale=-1.0, bias=blr[:, fq:fq + 1]), reads=[dp, dw], writes=[dee])
                            P.op("act", lambda e: e.activation(out=sps[:], in_=sps[:], func=AF.Ln, bias=1.0), reads=[dee], writes=[dee])
                            P.op("act", lambda e: e.activation(out=a_s[:, fq, :], in_=sps[:], func=AF.Exp, scale=-1.0 / 16), reads=[dee], writes=[dprep])
                        proj_fm(None, 0, 128, lrT, dlr, dlr, cons_g, cons_g_s, nk=1,
                                wbuf=(lambda c, fq=fq: wlr[0:16, fq * 128:(fq + 1) * 128], dw))
                        P.op("dve", lambda e: e.tensor_tensor_scan(out=csp[:], data0=rmask[:], data1=ee[:], initial=0.0, op0=ALU.mult, op1=ALU.add),
                             reads=[drm, dee, dkdT], writes=[dcs])
                        P.op("act", lambda e, fq=fq: e.activation(out=elT[:, fq, :], in_=csp[:].rearrange("p (c j) -> p c j", j=64)[:, :, 63],
                                                                   func=AF.Exp, scale=-1.0 / 16), reads=[dcs], writes=[dprep])
                        P.op("act", lambda e: e.activation(out=ee[:], in_=csp[:], func=AF.Exp, scale=-1.0 / 16), reads=[dcs], writes=[dee])

                        def cons_q(th, pap, dp):
                            sl = slice(th * 1024, (th + 1) * 1024)
                            for par in range(2):
                                pr_ = slice(par * 64, (par + 1) * 64)
                                P.op("dve", lambda e, par=par, pr_=pr_: e.scalar_tensor_tensor(
                                    out=qt[pr_, par, fq, sl], in0=pap[pr_], scalar=0.125, in1=ee[pr_, sl], op0=ALU.mult, op1=ALU.mult),
                                    reads=[dp, dee, dprep], writes=[dprep])

                        def cons_q_s(pap, dp):
                            P.op("dve", lambda e: e.tensor_scalar(out=q_s[:, fq, :], in0=pap, scalar1=0.125, scalar2=None, op0=ALU.mult), reads=[dp], writes=[dprep])
                        proj_fm(W, fq * 128, 128, hT, dhT, dhTs, cons_q, cons_q_s)
                        P.op("act", lambda e: e.activation(out=ee[:], in_=csp[:], func=AF.Exp, scale=1.0 / 16), reads=[dcs, dprep], writes=[dee])

                        def cons_k(th, pap, dp):
                            sl = slice(th * 1024, (th + 1) * 1024)
                            P.op("dve", lambda e: e.tensor_tensor(out=kt[:, fq, sl], in0=pap, in1=ee[:, sl], op=ALU.mult), reads=[dp, dee], writes=[dprep])
                            P.op("dve", lambda e: e.tensor_tensor(out=ee[:, sl].rearrange("p (c j) -> p c j", j=64), in0=ee[:, sl].rearrange("p (c j) -> p c j", j=64),
                                                                  in1=elT[:, fq, th * 16:(th + 1) * 16].unsqueeze(2).broadcast_to([128, 16, 64]), op=ALU.mult),
                                 reads=[dprep], writes=[dee])
                            P.op("dve", lambda e: e.tensor_tensor(out=kdT[:, sl], in0=pap, in1=ee[:, sl], op=ALU.mult), reads=[dp, dee], writes=[dkdT])
                        i = wcnt[0] % 3
                        proj_fm(W, 256 + fq * 128, 128, hT, dhT, dhTs, cons_k, None)
                        def mks(e, i=i):
                            ins = None
                            for c in range(8):
                                ins = e.matmul(ps[5][0:NS, 0:128], lhsT=hT[:, c, L:TW], rhs=wfm[i][:, c, 0:128], start=(c == 0), stop=(c == 7))
                            return ins
                        P.op("pe", mks, reads=[dwfm[i], dhTs], writes=[dps[5]])
                        acopy(k_tok[:, fq * 128:(fq + 1) * 128], ps[5][0:NS, 0:128], [dps[5]], [dprep])
                        transpose_kd(kdT, dkdT, kd, dprep, fq, 2)
                    P.barrier()
                stage(4.1)
                with ExitStack() as st3:
                    vtok = sb("vtok", [128, NT, 512], BF16, stack=st3)
                    ggt = sb("ggt", [128, NT, 512], BF16, stack=st3)
                    dv, dg, dgt = Dep(), Dep(), Dep()
                    with ExitStack() as st4:
                        wv = sb("wv", [128, 8, 256], BF16, stack=st4)
                        gtmp = sb("gtmp", [128, 256], stack=st4)
                        dwv = Dep()
                        sv_ = P.dsem()
                        for hv in range(2):
                            cs = slice(hv * 256, (hv + 1) * 256)
                            proj_tm(W, 512 + hv * 256, wv, dwv, sv_,
                                    lambda tt, pap, dp, cs=cs: acopy(vtok[:, tt, cs], pap, [dp], [dv]),
                                    lambda pap, dp, cs=cs: acopy(v_tok[:, cs], pap, [dp], [dprep]), n=256)

                            def cons_gt(tt, pap, dp, cs=cs):
                                P.op("act", lambda e: e.activation(out=gtmp[:], in_=pap, func=AF.Silu), reads=[dp], writes=[dgt])
                                P.op("dve", lambda e: e.tensor_tensor(out=ggt[:, tt, cs], in0=gtmp[:], in1=gnB[:, cs], op=ALU.mult), reads=[dgt, dgs], writes=[dg])

                            def cons_gt_s(pap, dp, cs=cs):
                                P.op("act", lambda e: e.activation(out=gtmp[0:NS, :], in_=pap, func=AF.Silu), reads=[dp], writes=[dgt])
                                P.op("dve", lambda e: e.tensor_tensor(out=gg_s[:, cs], in0=gtmp[0:NS, :], in1=gnB[0:NS, cs], op=ALU.mult), reads=[dgt, dgs], writes=[dgs])
                            proj_tm(W, 1024 + hv * 256, wv, dwv, sv_, cons_gt, cons_gt_s, n=256)
                        P.barrier()
                    stage(4.2)
                    gla_chunks(64, 4, lambda h: qt[:, h % 2, h // 2, :], kt, kd, vtok, ggt, elT, S, Sb,
                               lambda tt: hT[:, 0:4, tt * 128:(tt + 1) * 128], dhT, [dprep, dv, dg], o_gla_p[:, :, :])
                    stage(4.3)
                with ExitStack() as st5:
                    Sst = sb("Sst", [128, NS, 2, 128], stack=st5)
                    P.dma("sp", Sst[:], st_gla_d[:, :, :, :], s_in, writes=[dSst])
                    sample_state_update(64, q_s[:], k_tok[:], v_tok[:], a_s, Sst, dSst, [dprep], st5, "e")
                    rms_gate_sample(gg_s[:], dgs, hT[:, 0:4, L:TW], dhTs, st5, "e")
                    P.dma("sp", o_gla_s[:, :, :, :], Sst[:], s_out, reads=[dSst])
                    P.barrier()

        def build_bcast(dst, ddst, col_ap, add_one):
            for half in range(2):
                pb = nxt67()
                for q in range(4):
                    dc_ = half * 4 + q
                    P.op("dve", lambda e, dc_=dc_: e.tensor_scalar(out=hs_tmp[:].rearrange("p a b -> p (a b)")[:, 0:128], in0=ident[:],
                                                                    scalar1=col_ap[:, dc_:dc_ + 1], scalar2=None, op0=ALU.mult),
                         reads=[dmod, dconst, dsmall], writes=[dhs_tmp])
                    P.op("pe", lambda e, q=q, pb=pb: e.matmul(ps[pb][:, q * 128:(q + 1) * 128], lhsT=onesf[:],
                                                             rhs=hs_tmp[:].rearrange("p a b -> p (a b)")[:, 0:128], start=True, stop=True),
                         reads=[dhs_tmp, dconst], writes=[dps[pb]])
                if add_one:
                    P.op("act", lambda e, half=half, pb=pb: e.activation(out=dst[:, half * 512:(half + 1) * 512], in_=ps[pb][:], func=AF.Identity, bias=1.0),
                         reads=[dps[pb]], writes=[ddst])
                else:
                    acopy(dst[:, half * 512:(half + 1) * 512], ps[pb][:], [dps[pb]], [ddst])

        def ln_bufs(st, tag):
            stats = [sb("ln_stats%s%d" % (tag, i), [128, 2, 6], stack=st) for i in range(2)]
            mv = [sb("ln_mv%s%d" % (tag, i), [128, 8], stack=st) for i in range(2)]
            tmpn = [sb("ln_tmpn%s%d" % (tag, i), [128, D], stack=st) for i in range(2)]
            return stats, mv, tmpn, [Dep(), Dep()]

        def ln_s1(tt, lb):
            stats, mv, tmpn, dls = lb
            k = tt % 2
            for hh in range(2):
                P.op("dve", lambda e, hh=hh: e.bn_stats(out=stats[k][:, hh, :], in_=X[:, tt, hh * 512:(hh + 1) * 512]), reads=[dX[tt]], writes=[dls[k]])
            P.op("dve", lambda e: e.bn_aggr(out=mv[k][:, 0:2], in_=stats[k][:].rearrange("p a b -> p (a b)")), reads=[dls[k]], writes=[dls[k]])
            P.op("act", lambda e: e.activation(out=mv[k][:, 2:3], in_=mv[k][:, 1:2], func=AF.Sqrt, bias=LN_EPS), reads=[dls[k]], writes=[dls[k]])

        def ln_s2(tt, lb):
            stats, mv, tmpn, dls = lb
            k = tt % 2
            P.op("dve", lambda e: e.reciprocal(out=mv[k][:, 3:4], in_=mv[k][:, 2:3]), reads=[dls[k]], writes=[dls[k]])
            P.op("dve", lambda e: e.scalar_tensor_tensor(out=mv[k][:, 4:5], in0=mv[k][:, 0:1], scalar=-1.0, in1=mv[k][:, 3:4], op0=ALU.mult, op1=ALU.mult),
                 reads=[dls[k]], writes=[dls[k]])
            P.op("act", lambda e: e.activation(out=tmpn[k][:], in_=X[:, tt, :], func=AF.Identity, scale=mv[k][:, 3:4], bias=mv[k][:, 4:5]),
                 reads=[dls[k], dX[tt]], writes=[dls[k]])

        def ln_s3(tt, lb, bcs, dbcs):
            stats, mv, tmpn, dls = lb
            k = tt % 2
            P.op("dve", lambda e: e.tensor_tensor(out=tmpn[k][:], in0=tmpn[k][:], in1=bcs[1][:], op=ALU.mult), reads=[dls[k], dbcs[1]], writes=[dls[k]])
            P.op("dve", lambda e: e.tensor_tensor(out=X[:, tt, :], in0=tmpn[k][:], in1=bcs[2][:], op=ALU.add), reads=[dls[k], dbcs[2]], writes=[dX[tt]])

        def ln_sample(l, lni, g_gt, outT_ps_view, dpsv, st):
            v = sb("lns_v%d%d" % (l, lni), [128, 8, 2, NS], stack=st)
            mom = sb("lns_m%d%d" % (l, lni), [128, 4, NS], stack=st)
            lgb = sb("lns_g%d%d" % (l, lni), [128, 2, 8], stack=st)
            d = Dep()
            P.dma("sp", lgb[:, 0, :], ln_gT_d[:, l * 2 + lni, :], s_in, writes=[d])
            P.dma("sp", lgb[:, 1, :], ln_bT_d[:, l * 2 + lni, :], s_in, writes=[d])
            P.op("dve", lambda e: e.scalar_tensor_tensor(out=v[:, :, 0, :], in0=mods(l, g_gt), scalar=1.0, in1=outT_ps_view, op0=ALU.add, op1=ALU.mult),
                 reads=[dmod, dpsv], writes=[d])
            P.op("dve", lambda e: e.scalar_tensor_tensor(out=v[:, :, 0, :], in0=XsT[:], scalar=ALPHA, in1=v[:, :, 0, :], op0=ALU.mult, op1=ALU.add),
                 reads=[dXs, d], writes=[d])
            P.op("dve", lambda e: e.tensor_tensor(out=v[:, :, 1, :], in0=v[:, :, 0, :], in1=v[:, :, 0, :], op=ALU.mult), reads=[d], writes=[d])

            def mm(e):
                ins = None
                for c in range(8):
                    ins = e.matmul(ps[6][:, 0:2 * NS], lhsT=onesf[:], rhs=v[:, c, :, :].rearrange("p a b -> p (a b)"), start=(c == 0), stop=(c == 7))
                return ins
            P.op("pe", mm, reads=[d, dconst], writes=[dps[6]])
            P.op("dve", lambda e: e.tensor_scalar(out=mom[:, 0:2, :], in0=ps[6][:, 0:2 * NS].rearrange("p (a b) -> p a b", b=NS), scalar1=1.0 / D, scalar2=None,
                                                  op0=ALU.mult), reads=[dps[6]], writes=[d])
            P.op("dve", lambda e: e.tensor_tensor(out=mom[:, 2, :], in0=mom[:, 0, :], in1=mom[:, 0, :], op=ALU.mult), reads=[d], writes=[d])
            P.op("dve", lambda e: e.tensor_tensor(out=mom[:, 1, :], in0=mom[:, 1, :], in1=mom[:, 2, :], op=ALU.subtract), reads=[d], writes=[d])
            P.op("act", lambda e: e.activation(out=mom[:, 1, :], in_=mom[:, 1, :], func=AF.Sqrt, bias=LN_EPS), reads=[d], writes=[d])
            P.op("dve", lambda e: e.reciprocal(out=mom[:, 3, :], in_=mom[:, 1, :]), reads=[d], writes=[d])
            P.op("dve", lambda e: e.tensor_tensor(out=v[:, :, 0, :], in0=v[:, :, 0, :], in1=mom[:, 0, :].unsqueeze(1).broadcast_to([128, 8, NS]), op=ALU.subtract),
                 reads=[d], writes=[d])
            P.op("dve", lambda e: e.tensor_tensor(out=v[:, :, 0, :], in0=v[:, :, 0, :], in1=mom[:, 3, :].unsqueeze(1).broadcast_to([128, 8, NS]), op=ALU.mult),
                 reads=[d], writes=[d])
            P.op("dve", lambda e: e.tensor_tensor(out=v[:, :, 0, :], in0=v[:, :, 0, :], in1=lgb[:, 0, :].unsqueeze(2).broadcast_to([128, 8, NS]), op=ALU.mult),
                 reads=[d], writes=[d])
            P.op("dve", lambda e: e.tensor_tensor(out=XsT[:], in0=v[:, :, 0, :], in1=lgb[:, 1, :].unsqueeze(2).broadcast_to([128, 8, NS]), op=ALU.add),
                 reads=[d], writes=[dXs])

        def load_bcs(l, lni, g_gt, st):
            bcs = [sb("bc%d_%d%d" % (i, l, lni), [128, D], stack=st) for i in range(3)]
            dbcs = [Dep(), Dep(), Dep()]
            build_bcast(bcs[0], dbcs[0], modp(l, g_gt), True)
            P.dma("sp", bcs[1][:], ln_gb_d[:, (l * 2 + lni) * D:(l * 2 + lni + 1) * D].partition_broadcast(128), s_in, writes=[dbcs[1]])
            P.dma("sp", bcs[2][:], ln_gb_d[:, (4 + l * 2 + lni) * D:(4 + l * 2 + lni + 1) * D].partition_broadcast(128), s_in, writes=[dbcs[2]])
            return bcs, dbcs

        def out_proj_ln(l, wsrc, ysrc):
            with ExitStack() as st:
                wo = sb("wo", [128, 8, D], BF16, stack=st)
                lb = ln_bufs(st, "o%d" % l)
                dwo = Dep()
                swo = P.dsem()
                for c in range(8):
                    P.dma("pool", wo[:, c, :], wsrc[c * 128:(c + 1) * 128, :], swo, writes=[dwo])
                bcs, dbcs = load_bcs(l, 0, 2, st)
                for tt in range(NT):
                    if tt >= 1:
                        ln_s2(tt - 1, lb)
                    j = pcnt[0] % 2
                    pcnt[0] += 1

                    def mm(e, tt=tt, j=j):
                        ins = None
                        for dh in range(2):
                            for fc in range(8):
                                ins = e.matmul(pp2[j][:, dh * 512:(dh + 1) * 512], lhsT=ysrc(fc)[0](tt),
                                               rhs=wo[:, fc, dh * 512:(dh + 1) * 512], start=(fc == 0), stop=(fc == 7))
                        return ins
                    P.op("pe", mm, reads=[dwo] + [ysrc(fc)[1] for fc in range(8)], writes=[dpp[j]])
                    P.op("dve", lambda e, j=j: e.tensor_tensor(out=pp2[j][:, :], in0=pp2[j][:, :], in1=bcs[0][:], op=ALU.mult), reads=[dbcs[0]], writes=[dpp[j]])
                    P.op("dve", lambda e, tt=tt, j=j: e.scalar_tensor_tensor(out=X[:, tt, :], in0=X[:, tt, :], scalar=ALPHA, in1=pp2[j][:, :], op0=ALU.mult, op1=ALU.add),
                         reads=[dpp[j]], writes=[dX[tt]])
                    ln_s1(tt, lb)
                    if tt >= 1:
                        ln_s3(tt - 1, lb, bcs, dbcs)
                ln_s2(NT - 1, lb)
                ln_s3(NT - 1, lb, bcs, dbcs)
                def mms(e):
                    ins = None
                    for dc_ in range(8):
                        for fc in range(8):
                            ins = e.matmul(ps[4][:, dc_ * NS:(dc_ + 1) * NS], lhsT=wo[:, fc, dc_ * 128:(dc_ + 1) * 128], rhs=ysrc(fc)[2],
                                           start=(fc == 0), stop=(fc == 7))
                    return ins
                P.op("pe", mms, reads=[dwo] + [ysrc(fc)[3] for fc in range(8)], writes=[dps[4]])
                ln_sample(l, 0, 2, ps[4][:, 0:8 * NS].rearrange("p (c n) -> p c n", n=NS), dps[4], st)
                P.barrier()

        def ffn(l):
            make_hT(l, 4, 3)
            Wi = ffn_w_in[l]
            Wo = ffn_w_out[l]
            with ExitStack() as st:
                aT = sb("aT", [128, 22, 1024 + NS], BF16, stack=st)
                woh = sb("woh", [128, 22, 512], BF16, stack=st)
                sg = sb("ffn_sg", [128, 1024], stack=st)
                lb = ln_bufs(st, "f%d" % l)
                if len(wfm) < 4:
                    wfm.append(sb("wfm3_%d" % l, [128, 8, 128], BF16, stack=st))
                    dwfm.append(Dep())
                    swfm.append(P.dsem())
                fcnt = [0]
                daT, dwoh, dsg, daTs = Dep(), Dep(), Dep(), Dep()
                swoh = P.dsem()
                bcs, dbcs = load_bcs(l, 1, 5, st)
                for tblk in range(2):
                    for jf in range(22):
                        if jf == 2:
                            for jw in range(22):
                                P.dma("pool", woh[:, jw, :], Wo[jw * 128:(jw + 1) * 128, 0:512], swoh, writes=[dwoh])
                        ig_ = fcnt[0] % 4
                        fcnt[0] += 1
                        load_w(wfm[ig_], dwfm[ig_], swfm[ig_], Wi, jf * 128, 128)
                        iu_ = fcnt[0] % 4
                        fcnt[0] += 1
                        load_w(wfm[iu_], dwfm[iu_], swfm[iu_], Wi, DFF + jf * 128, 128)
                        for (wi_, j) in ((ig_, 0), (iu_, 1)):
                            def mm(e, wi_=wi_, j=j):
                                ins = None
                                for tb in range(2):
                                    for c in range(8):
                                        ins = e.matmul(pp2[j][:, tb * 512:(tb + 1) * 512], lhsT=wfm[wi_][:, c, :],
                                                       rhs=hT[:, c, tblk * 1024 + tb * 512: tblk * 1024 + (tb + 1) * 512], start=(c == 0), stop=(c == 7))
                                return ins
                            P.op("pe", mm, reads=[dwfm[wi_], dhT], writes=[dpp[j]])
                        P.op("act", lambda e: e.activation(out=sg[:], in_=pp2[0][:, :], func=AF.Silu), reads=[dpp[0]], writes=[dsg])
                        P.op("dve", lambda e, jf=jf: e.tensor_tensor(out=aT[:, jf, 0:1024], in0=pp2[1][:, :], in1=sg[:], op=ALU.mult), reads=[dpp[1], dsg], writes=[daT])
                        if tblk == 0:
                            for (wi_, col) in ((ig_, 0), (iu_, NS)):
                                def mms(e, wi_=wi_, col=col):
                                    ins = None
                                    for c in range(8):
                                        ins = e.matmul(ps[4][:, col:col + NS], lhsT=wfm[wi_][:, c, :], rhs=hT[:, c, L:TW], start=(c == 0), stop=(c == 7))
                                    return ins
                                P.op("pe", mms, reads=[dwfm[wi_], dhTs], writes=[dps[4]])
                            P.op("act", lambda e: e.activation(out=sg[:, 0:NS], in_=ps[4][:, 0:NS], func=AF.Silu), reads=[dps[4], dsg], writes=[dsg])
                            P.op("dve", lambda e, jf=jf: e.tensor_tensor(out=aT[:, jf, 1024:1024 + NS], in0=ps[4][:, NS:2 * NS], in1=sg[:, 0:NS], op=ALU.mult),
                                 reads=[dps[4], dsg], writes=[daTs])
                    for dh in range(2):
                        if dh == 1:
                            for jf in range(22):
                                P.dma("pool", woh[:, jf, :], Wo[jf * 128:(jf + 1) * 128, 512:1024], swoh, writes=[dwoh])
                        for t8 in range(8):
                            tt = tblk * 8 + t8
                            if dh == 1 and t8 >= 1:
                                ln_s2(tt - 1, lb)
                            pb = nxt67()

                            def mm2(e, t8=t8, pb=pb):
                                ins = None
                                for jf in range(22):
                                    ins = e.matmul(ps[pb][:, :], lhsT=aT[:, jf, t8 * 128:(t8 + 1) * 128], rhs=woh[:, jf, :], start=(jf == 0), stop=(jf == 21))
                                return ins
                            P.op("pe", mm2, reads=[daT, dwoh], writes=[dps[pb]])
                            P.op("dve", lambda e, pb=pb, dh=dh: e.tensor_tensor(out=ps[pb][:, :], in0=ps[pb][:, :], in1=bcs[0][:, dh * 512:(dh + 1) * 512], op=ALU.mult),
                                 reads=[dbcs[0]], writes=[dps[pb]])
                            P.op("dve", lambda e, tt=tt, dh=dh, pb=pb: e.scalar_tensor_tensor(out=X[:, tt, dh * 512:(dh + 1) * 512], in0=X[:, tt, dh * 512:(dh + 1) * 512],
                                                                                    scalar=ALPHA, in1=ps[pb][:, :], op0=ALU.mult, op1=ALU.add),
                                 reads=[dps[pb]], writes=[dX[tt]])
                            if dh == 1:
                                ln_s1(tt, lb)
                                if t8 >= 1:
                                    ln_s3(tt - 1, lb, bcs, dbcs)
                        if dh == 1:
                            ln_s2(tblk * 8 + 7, lb)
                            ln_s3(tblk * 8 + 7, lb, bcs, dbcs)
                        if tblk == 0:
                            def mms2(e, dh=dh):
                                ins = None
                                for q in range(4):
                                    for jf in range(22):
                                        ins = e.matmul(ps[5][:, (dh * 4 + q) * NS:(dh * 4 + q + 1) * NS], lhsT=woh[:, jf, q * 128:(q + 1) * 128],
                                                       rhs=aT[:, jf, 1024:1024 + NS], start=(jf == 0), stop=(jf == 21))
                                return ins
                            P.op("pe", mms2, reads=[daTs, dwoh], writes=[dps[5]])
                    if tblk == 0:
                        ln_sample(l, 1, 5, ps[5][:, 0:8 * NS].rearrange("p (c n) -> p c n", n=NS), dps[5], st)
                P.barrier()
                wfm.pop()
                dwfm.pop()
                swfm.pop()

        def s5_mixer(l, yoth, dyoth, dyoths):
            W = od_w_in
            PI = float(np.pi)
            with ExitStack() as st:
                sv = sb("s5v", [128, 3, 16], stack=st)
                pr = sb("s5pr", [128, 18, 16], stack=st)
                ce = sb("s5ce", [128, 11, 16], stack=st)
                dTt = sb("s5dT", [128, 2, 4], stack=st)
                Bb = sb("s5Bb", [128, 32, 128], BF16, stack=st)
                Cp = sb("s5Cp", [128, 32, 128], BF16, stack=st)
                s0 = sb("s5s0", [128, 2, 16, NS], stack=st)
                hl = sb("s5hl", [128, 2, 16], stack=st)
                ygb = sb("s5ygb", [128, 4, TW], BF16, stack=st)
                uT = yoth
                sms = sb("s5sm", [128, 1, NS], stack=st)
                ysa = sb("s5ysa", [128, 4, NS], stack=st)
                dpar, dB, dC, dCs, ds0, dos, dhl, dyg, dygs, du, dus = (Dep() for _ in range(11))
                dtab, dxr, dxi, dh, dt1_, dt2_, dsm, dysa = (Dep() for _ in range(8))
                dos_all = [Dep(), Dep()]
                dCst = [Dep(), Dep()]
                sB = P.dsem()
                P.dma("sp", sv[:], s5_vecT_d[:, :, :], s_in, writes=[dpar])
                P.dma("sp", dTt[:], s5_dT_d[:, :, :], s_in, writes=[dpar])
                P.dma("sp", s0[:], st_s5_d[:, :, :, :], s_in, writes=[ds0])
                for hf in range(2):
                    P.dma("pool", Bb[:, hf * 16:(hf + 1) * 16, :], s5_bbd_d[:, hf * 16:(hf + 1) * 16, :], sB, writes=[dB])
                LR, LI, DT, RHO, TH, C0, S0_, ABR, ABI, FR, FI, FIR, FII, T0, T1_, T2_ = (pr[:, i, :] for i in range(16))

                def dv(fn, reads=(dpar,), writes=(dpar,)):
                    P.op("dve", fn, reads=list(reads), writes=list(writes))

                def ac(fn):
                    P.op("act", fn, reads=[dpar], writes=[dpar])
                ac(lambda e: e.activation(out=DT, in_=sv[:, 2, :], func=AF.Exp))
                dv(lambda e: e.tensor_copy(out=LR, in_=sv[:, 0, :]))
                dv(lambda e: e.tensor_copy(out=LI, in_=sv[:, 1, :]))
                dv(lambda e: e.tensor_tensor(out=T0, in0=LR, in1=DT, op=ALU.mult))
                ac(lambda e: e.activation(out=RHO, in_=T0, func=AF.Exp))
                dv(lambda e: e.tensor_tensor(out=TH, in0=LI, in1=DT, op=ALU.mult))
                TWO_PI = 2 * PI

                def wrap_small(ap, tmp):
                    dv(lambda e: e.tensor_scalar(out=tmp, in0=ap, scalar1=1.0, scalar2=-1.0, op0=ALU.is_ge, op1=ALU.mult))
                    dv(lambda e: e.tensor_tensor(out=ap, in0=ap, in1=tmp, op=ALU.add))
                dv(lambda e: e.tensor_scalar(out=T0, in0=TH, scalar1=1.0 / TWO_PI, scalar2=None, op0=ALU.mult))
                for k in range(8):
                    wrap_small(T0, T1_)
                dv(lambda e: e.tensor_scalar(out=T1_, in0=T0, scalar1=0.0, scalar2=1.0, op0=ALU.is_lt, op1=ALU.mult))
                dv(lambda e: e.tensor_tensor(out=T0, in0=T0, in1=T1_, op=ALU.add))
                dv(lambda e: e.tensor_copy(out=ce[:, 0, :], in_=T0))
                for k in range(1, 11):
                    dv(lambda e, k=k: e.tensor_scalar(out=ce[:, k, :], in0=ce[:, k - 1, :], scalar1=2.0, scalar2=None, op0=ALU.mult))
                    wrap_small(ce[:, k, :], T1_)
                ac(lambda e: e.activation(out=S0_, in_=ce[:, 0, :], func=AF.Sin, scale=TWO_PI, bias=-PI))
                ac(lambda e: e.activation(out=T2_, in_=ce[:, 0, :], func=AF.Abs, scale=TWO_PI, bias=-PI))
                ac(lambda e: e.activation(out=C0, in_=T2_, func=AF.Sin, scale=-1.0, bias=PI / 2))
                dv(lambda e: e.scalar_tensor_tensor(out=ABR, in0=RHO, scalar=-1.0, in1=C0, op0=ALU.mult, op1=ALU.mult))
                dv(lambda e: e.scalar_tensor_tensor(out=ABI, in0=RHO, scalar=-1.0, in1=S0_, op0=ALU.mult, op1=ALU.mult))
                dv(lambda e: e.tensor_scalar(out=T0, in0=ABR, scalar1=-1.0, scalar2=None, op0=ALU.add))
                dv(lambda e: e.tensor_tensor(out=T1_, in0=LR, in1=LR, op=ALU.mult))
                dv(lambda e: e.tensor_tensor(out=T2_, in0=LI, in1=LI, op=ALU.mult))
                dv(lambda e: e.tensor_tensor(out=T1_, in0=T1_, in1=T2_, op=ALU.add))
                dv(lambda e: e.reciprocal(out=T1_, in_=T1_))
                dv(lambda e: e.tensor_tensor(out=FR, in0=T0, in1=LR, op=ALU.mult))
                dv(lambda e: e.tensor_tensor(out=T2_, in0=ABI, in1=LI, op=ALU.mult))
                dv(lambda e: e.tensor_tensor(out=FR, in0=FR, in1=T2_, op=ALU.add))
                dv(lambda e: e.tensor_tensor(out=FR, in0=FR, in1=T1_, op=ALU.mult))
                dv(lambda e: e.tensor_tensor(out=FI, in0=ABI, in1=LR, op=ALU.mult))
                dv(lambda e: e.tensor_tensor(out=T2_, in0=T0, in1=LI, op=ALU.mult))
                dv(lambda e: e.tensor_tensor(out=FI, in0=FI, in1=T2_, op=ALU.subtract))
                dv(lambda e: e.tensor_tensor(out=FI, in0=FI, in1=T1_, op=ALU.mult))
                dv(lambda e: e.tensor_tensor(out=T0, in0=FR, in1=FR, op=ALU.mult))
                dv(lambda e: e.tensor_tensor(out=T2_, in0=FI, in1=FI, op=ALU.mult))
                dv(lambda e: e.tensor_tensor(out=T0, in0=T0, in1=T2_, op=ALU.add))
                dv(lambda e: e.reciprocal(out=T0, in_=T0))
                dv(lambda e: e.tensor_tensor(out=FIR, in0=FR, in1=T0, op=ALU.mult))
                dv(lambda e: e.scalar_tensor_tensor(out=FII, in0=FI, scalar=-1.0, in1=T0, op0=ALU.mult, op1=ALU.mult))
                stc = ExitStack()
                Cst = [sb("s5Cst%d" % i, [128, 2, 128], stack=stc) for i in range(2)]
                ctmp = sb("s5ctmp", [128, 2, 128], stack=stc)
                for cg in range(16):
                    b_ = cg % 2
                    P.dma("sp", Cst[b_][:, 0, :], s5_cbd_d[:, cg, :], s_in, writes=[dCst[b_]])
                    P.dma("sp", Cst[b_][:, 1, :], s5_cbd_d[:, 16 + cg, :], s_in, writes=[dCst[b_]])
                    fr, fi = pr[:, 9, cg:cg + 1], pr[:, 10, cg:cg + 1]
                    P.op("dve", lambda e, b_=b_, fi=fi: e.tensor_scalar(out=ctmp[:, 0, :], in0=Cst[b_][:, 1, :], scalar1=fi, scalar2=None, op0=ALU.mult),
                         reads=[dCst[b_], dpar], writes=[dCs])
                    P.op("dve", lambda e, b_=b_, fr=fr, cg=cg: e.scalar_tensor_tensor(out=Cp[:, cg, :], in0=Cst[b_][:, 0, :], scalar=fr, in1=ctmp[:, 0, :],
                                                                                 op0=ALU.mult, op1=ALU.subtract), reads=[dCst[b_], dpar, dCs], writes=[dC])
                    P.op("dve", lambda e, b_=b_, fr=fr: e.tensor_scalar(out=ctmp[:, 1, :], in0=Cst[b_][:, 1, :], scalar1=fr, scalar2=-1.0, op0=ALU.mult, op1=ALU.mult),
                         reads=[dCst[b_], dpar], writes=[dCs])
                    P.op("dve", lambda e, b_=b_, fi=fi, cg=cg: e.scalar_tensor_tensor(out=ctmp[:, 0, :], in0=Cst[b_][:, 0, :], scalar=fi, in1=ctmp[:, 1, :],
                                                                                 op0=ALU.mult, op1=ALU.subtract), reads=[dCst[b_], dpar, dCs], writes=[dCs])
                    P.op("dve", lambda e, cg=cg: e.tensor_scalar(out=Cp[:, 16 + cg, :], in0=ctmp[:, 0, :], scalar1=-1.0, scalar2=None, op0=ALU.mult),
                         reads=[dCs], writes=[dC])
                P.barrier()
                stc.close()
                ctab = sb("s5c", [128, L], stack=st)
                stab = sb("s5s", [128, L], stack=st)
                xr = sb("s5xr", [128, L], stack=st)
                xi = sb("s5xi", [128, L], stack=st)
                hre = sb("s5hre", [128, L], BF16, stack=st)
                him = sb("s5him", [128, L], BF16, stack=st)
                t1 = xr[:, 0:512]
                t2 = xr[:, 512:1024]
                lo_t = sb("s5lo", [128, 4, 64], stack=st)
                hi_t = sb("s5hi", [128, 4, 32], stack=st)
                lh_tmp = sb("s5lht", [128, 4, 32], stack=st)
                dlohi = Dep()

                def build_lohi(cg0):
                    def lv(fn):
                        P.op("dve", fn, reads=[dlohi, dpar], writes=[dlohi])
                    for tab, k0, nlev in ((lo_t, 0, 6), (hi_t, 6, 5)):
                        lv(lambda e, tab=tab: e.memset(tab[:, :, 0:1], 0.0))
                        for k in range(nlev):
                            n = 1 << k
                            inc = ce[:, k0 + k, cg0:cg0 + 4].unsqueeze(2).broadcast_to([128, 4, n])
                            lv(lambda e, tab=tab, n=n, inc=inc: e.tensor_tensor(out=tab[:, :, n:2 * n], in0=tab[:, :, 0:n], in1=inc, op=ALU.add))
                            lv(lambda e, tab=tab, n=n: e.tensor_scalar(out=lh_tmp[:, :, 0:n], in0=tab[:, :, n:2 * n], scalar1=1.0, scalar2=-1.0,
                                                                       op0=ALU.is_ge, op1=ALU.mult))
                            lv(lambda e, tab=tab, n=n: e.tensor_tensor(out=tab[:, :, n:2 * n], in0=tab[:, :, n:2 * n], in1=lh_tmp[:, :, 0:n], op=ALU.add))
                dt1 = dxr
                dt2 = dxr
                for fo in range(4):
                    proj_fm(W, fo * 128, 128, hT, dhT, dhTs,
                            lambda th, pap, dp, fo=fo: acopy(uT[:, fo, th * 1024:(th + 1) * 1024], pap, [dp], [du]),
                            lambda pap, dp, fo=fo: acopy(uT[:, fo, L:TW], pap, [dp], [dus]))
                def rs_mm(e):
                    ins = None
                    for ri in range(2):
                        for cg_ in range(16):
                            ins = e.matmul(pp2[ri][:, cg_ * NS:(cg_ + 1) * NS], lhsT=Bb[:, ri * 16 + cg_, :], rhs=uT[:, cg_ // 4, L:TW], start=True, stop=True)
                    return ins
                P.op("pe", rs_mm, reads=[dB, dus], writes=[dpp[0], dpp[1]])
                RR = pp2[0][:, 0:16 * NS].rearrange("p (c n) -> p c n", n=NS)
                RI = pp2[1][:, 0:16 * NS].rearrange("p (c n) -> p c n", n=NS)
                tq = lambda i: xr[:, i * 256:(i + 1) * 256].rearrange("p (c n) -> p c n", n=NS)
                OS = xi[:, 0:512].rearrange("p (r c n) -> p r c n", r=2, n=NS)
                SMB = hre[:, 0:512].rearrange("p (r c n) -> p r c n", r=2, n=NS)
                bcp = lambda i: pr[:, i, :].unsqueeze(2).broadcast_to([128, 16, NS])

                def tt_(out, in0, in1, op, reads, writes):
                    P.op("dve", lambda e: e.tensor_tensor(out=out, in0=in0, in1=in1, op=op), reads=reads, writes=writes)
                rd = [dxr, dpar, ds0]
                tt_(tq(0), RI, bcp(10), ALU.mult, rd + [dpp[1]], [dxr])
                tt_(tq(1), RR, bcp(9), ALU.mult, rd + [dpp[0]], [dxr])
                tt_(tq(1), tq(1), tq(0), ALU.subtract, rd, [dxr])
                tt_(tq(0), RR, bcp(10), ALU.mult, rd + [dpp[0]], [dxr])
                tt_(tq(2), RI, bcp(9), ALU.mult, rd + [dpp[1]], [dxr])
                tt_(tq(2), tq(2), tq(0), ALU.add, rd, [dxr])
                tt_(tq(0), s0[:, 0, :, :], bcp(7), ALU.mult, rd, [dxr])
                tt_(tq(1), tq(1), tq(0), ALU.add, rd, [dxr])
                tt_(tq(0), s0[:, 1, :, :], bcp(8), ALU.mult, rd, [dxr])
                tt_(OS[:, 0], tq(1), tq(0), ALU.subtract, rd + [dxi], [dxi])
                tt_(tq(0), s0[:, 1, :, :], bcp(7), ALU.mult, rd, [dxr])
                tt_(tq(2), tq(2), tq(0), ALU.add, rd, [dxr])
                tt_(tq(0), s0[:, 0, :, :], bcp(8), ALU.mult, rd, [dxr])
                tt_(OS[:, 1], tq(2), tq(0), ALU.add, rd + [dxi], [dxi])
                P.dma("sp", o_s5_s[:, :, :, :], OS, s_out, reads=[dxi])
                tt_(tq(0), OS[:, 1], bcp(12), ALU.mult, rd + [dxi], [dxr])
                tt_(tq(1), OS[:, 0], bcp(11), ALU.mult, rd + [dxi], [dxr])
                tt_(SMB[:, 0], tq(1), tq(0), ALU.subtract, rd + [dh], [dh])
                tt_(tq(0), OS[:, 0], bcp(12), ALU.mult, rd + [dxi], [dxr])
                tt_(tq(1), OS[:, 1], bcp(11), ALU.mult, rd + [dxi], [dxr])
                tt_(SMB[:, 1], tq(1), tq(0), ALU.add, rd + [dh], [dh])

                def ys_mm(e):
                    ins = None
                    for fo_ in range(4):
                        out = pp2[0][:, 512 + fo_ * NS:512 + (fo_ + 1) * NS]
                        for k_ in range(4):
                            cg_ = fo_ * 4 + k_
                            e.matmul(out, lhsT=Cp[:, cg_, :], rhs=SMB[:, 0, cg_, :], start=(k_ == 0), stop=False)
                            ins = e.matmul(out, lhsT=Cp[:, 16 + cg_, :], rhs=SMB[:, 1, cg_, :], start=False, stop=(k_ == 3))
                    return ins
                P.op("pe", ys_mm, reads=[dC, dh], writes=[dpp[0]])
                P.op("dve", lambda e: e.tensor_copy(out=ysa[:], in_=pp2[0][:, 512:512 + 4 * NS].rearrange("p (f n) -> p f n", n=NS)), reads=[dpp[0]], writes=[dysa])
                for cg in range(16):
                    fo = cg // 4
                    first, last = (cg % 4 == 0), (cg % 4 == 3)
                    sc = lambda i: pr[:, i, cg:cg + 1]
                    if cg % 4 == 0:
                        build_lohi(cg)
                    cl = cg % 4
                    xr3 = xr[:].rearrange("p (a b) -> p a b", b=64)
                    P.op("dve", lambda e: e.tensor_tensor(out=xr3, in0=lo_t[:, cl, :].unsqueeze(1).broadcast_to([128, 32, 64]),
                                                          in1=hi_t[:, cl, :].unsqueeze(2).broadcast_to([128, 32, 64]), op=ALU.add),
                         reads=[dxr, dh, dlohi], writes=[dxr])
                    P.op("dve", lambda e: e.scalar_tensor_tensor(out=xr[:], in0=xr[:], scalar=1.0, in1=xr[:], op0=ALU.is_ge, op1=ALU.subtract),
                         reads=[dxr], writes=[dxr])
                    P.op("act", lambda e: e.activation(out=stab[:], in_=xr[:], func=AF.Sin, scale=-TWO_PI, bias=-PI), reads=[dxr, dh], writes=[dtab])
                    P.op("act", lambda e: e.activation(out=xi[:], in_=xr[:], func=AF.Abs, scale=-TWO_PI, bias=-PI), reads=[dxr, dxi, dh], writes=[dxi])
                    P.op("act", lambda e: e.activation(out=ctab[:], in_=xi[:], func=AF.Sin, scale=-1.0, bias=PI / 2), reads=[dxi], writes=[dtab])
                    for th in range(2):
                        sl = slice(th * 1024, (th + 1) * 1024)
                        for ri in range(2):
                            def rmm(e, ri=ri, th=th):
                                ins = None
                                for tb in range(2):
                                    ins = e.matmul(pp2[ri][:, tb * 512:(tb + 1) * 512], lhsT=Bb[:, ri * 16 + cg, :],
                                                   rhs=uT[:, fo, th * 1024 + tb * 512: th * 1024 + (tb + 1) * 512], start=True, stop=True)
                                return ins
                            P.op("pe", rmm, reads=[dB, du], writes=[dpp[ri]])
                        P0, P1 = pp2[0][:, :], pp2[1][:, :]
                        P.op("dve", lambda e, sl=sl: e.tensor_tensor(out=xr[:, sl], in0=P0, in1=ctab[:, sl], op=ALU.mult), reads=[dpp[0], dtab, dxr], writes=[dxr])
                        P.op("dve", lambda e, sl=sl: e.tensor_tensor(out=xi[:, sl], in0=P1, in1=ctab[:, sl], op=ALU.mult), reads=[dpp[1], dtab, dh], writes=[dxi])
                        P.op("dve", lambda e, sl=sl: e.tensor_tensor(out=P0, in0=P0, in1=stab[:, sl], op=ALU.mult), reads=[dtab, dxr], writes=[dpp[0]])
                        P.op("dve", lambda e, sl=sl: e.tensor_tensor(out=P1, in0=P1, in1=stab[:, sl], op=ALU.mult), reads=[dtab, dxi], writes=[dpp[1]])
                        P.op("dve", lambda e, sl=sl: e.tensor_tensor(out=xr[:, sl], in0=P1, in1=xr[:, sl], op=ALU.add), reads=[dpp[1]], writes=[dxr])
                        P.op("dve", lambda e, sl=sl: e.tensor_tensor(out=xi[:, sl], in0=xi[:, sl], in1=P0, op=ALU.subtract), reads=[dpp[0]], writes=[dxi])
                    rho_b = pr[:, 3, cg:cg + 1].broadcast_to([128, L])
                    P.op("dve", lambda e: e.tensor_tensor_scan(out=xr[:], data0=rho_b, data1=xr[:], initial=0.0, op0=ALU.mult, op1=ALU.add), reads=[dpar], writes=[dxr])
                    P.op("dve", lambda e: e.tensor_tensor_scan(out=xi[:], data0=rho_b, data1=xi[:], initial=0.0, op0=ALU.mult, op1=ALU.add), reads=[dpar], writes=[dxi])
                    for th in range(2):
                        sl = slice(th * 1024, (th + 1) * 1024)
                        P0, P1 = pp2[0][:, :], pp2[1][:, :]
                        P.op("dve", lambda e, sl=sl: e.tensor_tensor(out=P0, in0=xr[:, sl], in1=ctab[:, sl], op=ALU.mult), reads=[dxr, dtab], writes=[dpp[0]])
                        P.op("dve", lambda e, sl=sl: e.tensor_tensor(out=P1, in0=xi[:, sl], in1=ctab[:, sl], op=ALU.mult), reads=[dxi, dtab], writes=[dpp[1]])
                        P.op("dve", lambda e, sl=sl: e.tensor_tensor(out=xr[:, sl], in0=xr[:, sl], in1=stab[:, sl], op=ALU.mult), reads=[dtab, dpp[0]], writes=[dxr])
                        P.op("dve", lambda e, sl=sl: e.tensor_tensor(out=xi[:, sl], in0=xi[:, sl], in1=stab[:, sl], op=ALU.mult), reads=[dtab, dpp[1]], writes=[dxi])
                        if th == 1:
                            P.op("dve", lambda e: e.tensor_tensor(out=hl[:, 0, cg:cg + 1], in0=pp2[0][:, 1023:1024], in1=xi[:, L - 1:L], op=ALU.subtract),
                                 reads=[dpp[0], dxi], writes=[dhl])
                            P.op("dve", lambda e: e.tensor_tensor(out=hl[:, 1, cg:cg + 1], in0=pp2[1][:, 1023:1024], in1=xr[:, L - 1:L], op=ALU.add),
                                 reads=[dpp[1], dxr], writes=[dhl])
                        P.op("dve", lambda e, sl=sl: e.tensor_tensor(out=hre[:, sl], in0=P0, in1=xi[:, sl], op=ALU.subtract), reads=[dpp[0], dxi], writes=[dh])
                        P.op("dve", lambda e, sl=sl: e.tensor_tensor(out=him[:, sl], in0=P1, in1=xr[:, sl], op=ALU.add), reads=[dpp[1], dxr], writes=[dh])
                    for tb in range(4):
                        def ymm(e, tb=tb):
                            e.matmul(ps[4 + tb][:, :], lhsT=Cp[:, cg, :], rhs=hre[:, tb * 512:(tb + 1) * 512], start=first, stop=False)
                            return e.matmul(ps[4 + tb][:, :], lhsT=Cp[:, 16 + cg, :], rhs=him[:, tb * 512:(tb + 1) * 512], start=False, stop=last)
                        P.op("pe", ymm, reads=[dC, dh], writes=[dps[4 + tb]])
                    if last:
                        for tb in range(4):
                            sl = slice(tb * 512, (tb + 1) * 512)
                            P.op("dve", lambda e, tb=tb, sl=sl: e.scalar_tensor_tensor(out=t1, in0=uT[:, fo, sl], scalar=dTt[:, 0, fo:fo + 1], in1=ps[4 + tb][:, :],
                                                                                  op0=ALU.mult, op1=ALU.add), reads=[du, dpar, dps[4 + tb], dh], writes=[dt1])
                            gelu_from(t1, dt1, ygb[:, fo, sl], dyg, t2, dt2, 512)
                        P.op("dve", lambda e: e.scalar_tensor_tensor(out=t1[:, 0:NS], in0=uT[:, fo, L:TW], scalar=dTt[:, 0, fo:fo + 1], in1=ysa[:, fo, :],
                                                                     op0=ALU.mult, op1=ALU.add), reads=[dus, dpar, dysa, dt1], writes=[dt1])
                        gelu_from(t1[:, 0:NS], dt1, ygb[:, fo, L:TW], dygs, t2[:, 0:NS], dt2, NS)
                P.op("dve", lambda e: e.tensor_tensor(out=pr[:, 16, :], in0=hl[:, 1, :], in1=FI, op=ALU.mult), reads=[dhl, dpar], writes=[dpar])
                P.op("dve", lambda e: e.tensor_tensor(out=pr[:, 17, :], in0=hl[:, 0, :], in1=FR, op=ALU.mult), reads=[dhl, dpar], writes=[dpar])
                P.op("dve", lambda e: e.tensor_tensor(out=stg[:, 16:32], in0=pr[:, 17, :], in1=pr[:, 16, :], op=ALU.subtract), reads=[dpar], writes=[dstg])
                P.op("dve", lambda e: e.tensor_tensor(out=pr[:, 16, :], in0=hl[:, 0, :], in1=FI, op=ALU.mult), reads=[dhl, dpar], writes=[dpar])
                P.op("dve", lambda e: e.tensor_tensor(out=pr[:, 17, :], in0=hl[:, 1, :], in1=FR, op=ALU.mult), reads=[dhl, dpar], writes=[dpar])
                P.op("dve", lambda e: e.tensor_tensor(out=stg[:, 32:48], in0=pr[:, 17, :], in1=pr[:, 16, :], op=ALU.add), reads=[dpar], writes=[dstg])
                P.dma("sp", o_s5_p.rearrange("p a b -> p (a b)"), stg[:, 16:48], s_out, reads=[dstg])
                P.barrier()
                for fo2 in range(4):
                    def cons_g(th, pap, dp, fo2=fo2):
                        sl = slice(th * 1024, (th + 1) * 1024)
                        P.op("act", lambda e: e.activation(out=xr[:, sl], in_=pap, func=AF.Sigmoid, bias=dTt[:, 1, fo2:fo2 + 1]), reads=[dp, dpar], writes=[dxr])
                        P.op("dve", lambda e: e.tensor_tensor(out=yoth[:, fo2, sl], in0=xr[:, sl], in1=ygb[:, fo2, sl], op=ALU.mult), reads=[dxr, dyg], writes=[dyoth])

                    def cons_g_s(pap, dp, fo2=fo2):
                        P.op("act", lambda e: e.activation(out=sms[:, 0, :], in_=pap, func=AF.Sigmoid, bias=dTt[:, 1, fo2:fo2 + 1]), reads=[dp, dpar], writes=[dsm])
                        P.op("dve", lambda e: e.tensor_tensor(out=yoth[:, fo2, L:TW], in0=sms[:, 0, :], in1=ygb[:, fo2, L:TW], op=ALU.mult), reads=[dsm, dygs], writes=[dyoths])
                    proj_fm(w_glu, fo2 * 128, 128, ygb, dyg, dygs, cons_g, cons_g_s, nk=4)
                P.barrier()

        def hgrn_mixer(l, yh01, dyh01, yhs, dyhs):
            W = od_w_in
            with ExitStack() as st:
                lbt = sb("hg_lbt", [128, 2, 4], stack=st)
                lbv = sb("hg_lbv", [128, 2, 4], stack=st)
                q_s = sb("hq_s", [128, 4, NS], stack=st)
                a_s = sb("ha_s", [128, 4, NS], stack=st)
                kks = sb("hkks", [128, 4, NS], stack=st)
                k_tok = sb("hk_tok", [NS, 512], BF16, stack=st)
                v_tok = sb("hv_tok", [NS, 512], BF16, stack=st)
                gg_s = sb("hgg_s", [NS, 512], stack=st)
                gnB = sb("hgnB", [128, 512], stack=st)
                dpar, dsm, dgs, dSst = Dep(), Dep(), Dep(), Dep()
                P.dma("sp", lbt[:], hg_lbT_d[:, :, :], s_in, writes=[dpar])
                P.dma("sp", gnB[:], hg_ng_d.partition_broadcast(128), s_in, writes=[dgs])
                P.op("dve", lambda e: e.tensor_tensor(out=lbv[:, 0, :], in0=lbt[:, 1, :], in1=lbt[:, 0, :], op=ALU.subtract), reads=[dpar], writes=[dpar])
                P.op("act", lambda e: e.activation(out=lbv[:, 0, :], in_=lbv[:, 0, :], func=AF.Sigmoid), reads=[dpar], writes=[dpar])
                P.op("dve", lambda e: e.tensor_scalar(out=lbv[:, 1, :], in0=lbv[:, 0, :], scalar1=-1.0, scalar2=1.0, op0=ALU.mult, op1=ALU.add), reads=[dpar], writes=[dpar])
                for pss in range(2):
                    with ExitStack() as sp_:
                        qt = sb("hqt", [128, 2, L], BF16, stack=sp_)
                        kt = sb("hkt", [128, 2, L], BF16, stack=sp_)
                        kd = sb("hkd", [128, NT, 256], BF16, stack=sp_)
                        elT = sb("helT", [128, 2, 32], stack=sp_)
                        S = sb("hS", [128, 2, 128], stack=sp_)
                        Sb = sb("hSb", [128, 2, 128], BF16, stack=sp_)
                        dprep = Dep()
                        with ExitStack() as st2:
                            A = sb("hA", [128, L], stack=st2)
                            C = sb("hC", [128, L], stack=st2)
                            kdT = sb("hkdT", [128, 1024], BF16, stack=st2)
                            rmask = sb("hrmask", [128, 1024], stack=st2)
                            dA, dC, dkdT, drm = Dep(), Dep(), Dep(), Dep()
                            P.op("pool", lambda e: e.memset(rmask[:], 1.0), writes=[drm])
                            P.op("pool", lambda e: e.affine_select(out=rmask[:].rearrange("p (c j) -> p c j", j=64), in_=rmask[:].rearrange("p (c j) -> p c j", j=64),
                                                                    pattern=[[0, 16], [1, 64]], compare_op=ALU.is_gt, fill=0.0, base=0, channel_multiplier=0),
                                 reads=[drm], writes=[drm])
                            for hl in range(2):
                                h = pss * 2 + hl
                                lb_, oml_ = lbv[:, 0, h:h + 1], lbv[:, 1, h:h + 1]

                                def cons_f(th, pap, dp):
                                    sl = slice(th * 1024, (th + 1) * 1024)
                                    P.op("act", lambda e: e.activation(out=A[:, sl], in_=pap, func=AF.Sigmoid), reads=[dp, dprep, dkdT], writes=[dA])
                                    P.op("dve", lambda e: e.tensor_scalar(out=A[:, sl], in0=A[:, sl], scalar1=oml_, scalar2=lb_, op0=ALU.mult, op1=ALU.add),
                                         reads=[dpar], writes=[dA])

                                def cons_f_s(pap, dp):
                                    P.op("act", lambda e: e.activation(out=a_s[:, h, :], in_=pap, func=AF.Sigmoid), reads=[dp], writes=[dsm])
                                    P.op("dve", lambda e: e.tensor_scalar(out=a_s[:, h, :], in0=a_s[:, h, :], scalar1=oml_, scalar2=lb_, op0=ALU.mult, op1=ALU.add),
                                         reads=[dpar, dsm], writes=[dsm])
                                    P.op("dve", lambda e: e.tensor_scalar(out=kks[:, h, :], in0=a_s[:, h, :], scalar1=-1.0, scalar2=1.0, op0=ALU.mult, op1=ALU.add),
                                         reads=[dsm], writes=[dsm])
                                proj_fm(W, 1024 + h * 128, 128, hT, dhT, dhTs, cons_f, cons_f_s)
                                P.op("act", lambda e: e.activation(out=C[:], in_=A[:], func=AF.Ln), reads=[dA, dprep, dkdT], writes=[dC])
                                P.op("dve", lambda e: e.tensor_scalar(out=A[:], in0=A[:], scalar1=-1.0, scalar2=1.0, op0=ALU.mult, op1=ALU.add), reads=[dC], writes=[dA])
                                for th in range(2):
                                    sl = slice(th * 1024, (th + 1) * 1024)
                                    P.op("dve", lambda e, sl=sl: e.tensor_tensor_scan(out=C[:, sl], data0=rmask[:], data1=C[:, sl], initial=0.0, op0=ALU.mult, op1=ALU.add),
                                         reads=[drm], writes=[dC])
                                P.op("act", lambda e, hl=hl: e.activation(out=elT[:, hl, :], in_=C[:].rearrange("p (c j) -> p c j", j=64)[:, :, 63], func=AF.Exp),
                                     reads=[dC], writes=[dprep])
                                P.op("act", lambda e: e.activation(out=C[:], in_=C[:], func=AF.Exp), reads=[dprep], writes=[dC])

                                def cons_q(th, pap, dp, hl=hl):
                                    sl = slice(th * 1024, (th + 1) * 1024)
                                    P.op("act", lambda e: e.activation(out=pap, in_=pap, func=AF.Silu), reads=[], writes=[dp])
                                    P.op("dve", lambda e: e.tensor_tensor(out=qt[:, hl, sl], in0=pap, in1=C[:, sl], op=ALU.mult), reads=[dp, dC], writes=[dprep])

                                def cons_q_s(pap, dp):
                                    P.op("act", lambda e: e.activation(out=q_s[:, h, :], in_=pap, func=AF.Silu), reads=[dp], writes=[dsm])
                                proj_fm(W, 512 + h * 128, 128, hT, dhT, dhTs, cons_q, cons_q_s)
                                P.op("dve", lambda e: e.reciprocal(out=C[:], in_=C[:]), reads=[dprep], writes=[dC])
                                P.op("dve", lambda e, hl=hl: e.tensor_tensor(out=kt[:, hl, :], in0=A[:], in1=C[:], op=ALU.mult), reads=[dA, dC], writes=[dprep])
                                for th in range(2):
                                    sl = slice(th * 1024, (th + 1) * 1024)
                                    P.op("dve", lambda e, sl=sl, th=th, hl=hl: e.tensor_tensor(
                                        out=C[:, sl].rearrange("p (c j) -> p c j", j=64), in0=C[:, sl].rearrange("p (c j) -> p c j", j=64),
                                        in1=elT[:, hl, th * 16:(th + 1) * 16].unsqueeze(2).broadcast_to([128, 16, 64]), op=ALU.mult), reads=[dprep], writes=[dC])
                                    P.op("dve", lambda e, sl=sl: e.tensor_tensor(out=kdT[:], in0=A[:, sl], in1=C[:, sl], op=ALU.mult), reads=[dA, dC, dprep], writes=[dkdT])
                                    for tb in range(2):
                                        pb = nxt67()

                                        def tr(e, tb=tb, pb=pb):
                                            ins = None
                                            for q in range(4):
                                                ins = e.transpose(out=ps[pb][:].bitcast(BF16)[:, q * 128:(q + 1) * 128],
                                                                  in_=kdT[:, (tb * 4 + q) * 128:(tb * 4 + q + 1) * 128], identity=identb[:])
                                            return ins
                                        P.op("pe", tr, reads=[dkdT, dconst], writes=[dps[pb]])
                                        t_base = th * 8 + tb * 4
                                        P.op("act", lambda e, pb=pb, t_base=t_base, hl=hl: e.activation(
                                            out=kd[:, t_base:t_base + 4, hl * 128:(hl + 1) * 128],
                                            in_=ps[pb][:].bitcast(BF16)[:, 0:512].rearrange("p (q t) -> p q t", t=128), func=AF.Copy),
                                            reads=[dps[pb]], writes=[dprep])
                            P.barrier()
                        with ExitStack() as st3:
                            vtok = sb("hvtok", [128, NT, 256], BF16, stack=st3)
                            ggt = sb("hggt", [128, NT, 256], BF16, stack=st3)
                            dv_, dg_, dgt = Dep(), Dep(), Dep()
                            cs = slice(pss * 256, (pss + 1) * 256)
                            with ExitStack() as st4:
                                wv = sb("hwv", [128, 8, 256], BF16, stack=st4)
                                gtmp = sb("hgtmp", [128, 256], stack=st4)
                                dwv = Dep()
                                sv_ = P.dsem()
                                proj_tm(W, 1536 + pss * 256, wv, dwv, sv_,
                                        lambda tt, pap, dp: acopy(vtok[:, tt, :], pap, [dp], [dv_]),
                                        lambda pap, dp: acopy(v_tok[:, cs], pap, [dp], [dsm]), n=256)

                                def cons_gt(tt, pap, dp):
                                    P.op("act", lambda e: e.activation(out=gtmp[:], in_=pap, func=AF.Silu), reads=[dp], writes=[dgt])
                                    P.op("dve", lambda e: e.tensor_tensor(out=ggt[:, tt, :], in0=gtmp[:], in1=gnB[:, cs], op=ALU.mult), reads=[dgt, dgs], writes=[dg_])

                                def cons_gt_s(pap, dp):
                                    P.op("act", lambda e: e.activation(out=gtmp[0:NS, :], in_=pap, func=AF.Silu), reads=[dp], writes=[dgt])
                                    P.op("dve", lambda e: e.tensor_tensor(out=gg_s[:, cs], in0=gtmp[0:NS, :], in1=gnB[0:NS, cs], op=ALU.mult), reads=[dgt, dgs], writes=[dgs])
                                proj_tm(W, 2048 + pss * 256, wv, dwv, sv_, cons_gt, cons_gt_s, n=256)
                                P.barrier()
                            if pss == 0:
                                ydst = lambda tt: yh01[:, 0:2, tt * 128:(tt + 1) * 128]
                                dy = dyh01
                            else:
                                ydst = lambda tt: hT[:, 6:8, tt * 128:(tt + 1) * 128]
                                dy = dhT
                            gla_chunks(128, 2, lambda hl: qt[:, hl, :], kt, kd, vtok, ggt, elT, S, Sb, ydst, dy, [dprep, dv_, dg_],
                                       o_hg_p[:, pss * 2:(pss + 1) * 2, :])
                def ktr(e):
                    ins = None
                    for h in range(4):
                        ins = e.transpose(out=ps[6][0:NS, h * 128:(h + 1) * 128], in_=kks[:, h, :], identity=ident[:])
                    return ins
                P.op("pe", ktr, reads=[dsm, dconst], writes=[dps[6]])
                acopy(k_tok[:], ps[6][0:NS, :], [dps[6]], [dsm])
                with ExitStack() as st5:
                    Sst = sb("hSst", [128, NS, 4, 128], stack=st5)
                    P.dma("sp", Sst[:], st_hg_d[:, :, :, :], s_in, writes=[dSst])
                    sample_state_update(128, q_s[:], k_tok[:], v_tok[:], a_s, Sst, dSst, [dsm], st5, "o")
                    rms_gate_sample(gg_s[:], dgs, yhs[:], dyhs, st5, "o")
                    P.dma("sp", o_hg_s[:, :, :, :], Sst[:], s_out, reads=[dSst])
                    P.barrier()


        _DEAD[0] = False
        if True:
            stage(2)
            with ExitStack() as stl:
                yoth = sb("yoth", [128, 4, TW], BF16, stack=stl)
                dyoth, dyoths = Dep(), Dep()
                pa = ExitStack()
                mod_issue = setup_mod(pa)
                mod_issue(4)
                mod_issue.drain()
                make_hT(0, 1, 0)
                even_mixer(0, yoth, dyoth, dyoths, bg=mod_issue, bg_stack=pa)

                def ysrc0(fc):
                    if fc < 4:
                        return (lambda tt, fc=fc: hT[:, fc, tt * 128:(tt + 1) * 128], dhT, hT[:, fc, L:TW], dhTs)
                    return (lambda tt, fc=fc: yoth[:, fc - 4, tt * 128:(tt + 1) * 128], dyoth, yoth[:, fc - 4, L:TW], dyoths)
                stage(5)
                out_proj_ln(0, ev_w_out, ysrc0)
            stage(6)
            ffn(0)
            stage(7)
            make_hT(1, 1, 0)
            with ExitStack() as stl:
                yoth1 = sb("yoth1", [128, 4, TW], BF16, stack=stl)
                dyoth1, dyoths1 = Dep(), Dep()
                s5_mixer(1, yoth1, dyoth1, dyoths1)
                stage(8)
                yh01 = sb("yh01", [128, 2, L], BF16, stack=stl)
                yhs = sb("yhs", [128, 4, NS], BF16, stack=stl)
                dyh01, dyhs = Dep(), Dep()
                hgrn_mixer(1, yh01, dyh01, yhs, dyhs)

                def ysrc1(fc):
                    if fc < 4:
                        return (lambda tt, fc=fc: yoth1[:, fc, tt * 128:(tt + 1) * 128], dyoth1, yoth1[:, fc, L:TW], dyoths1)
                    if fc < 6:
                        return (lambda tt, fc=fc: yh01[:, fc - 4, tt * 128:(tt + 1) * 128], dyh01, yhs[:, fc - 4, :], dyhs)
                    return (lambda tt, fc=fc: hT[:, fc, tt * 128:(tt + 1) * 128], dhT, yhs[:, fc - 4, :], dyhs)
                stage(9)
                out_proj_ln(1, od_w_out, ysrc1)
            stage(10)
            ffn(1)
        _DEAD[0] = False
        P.barrier()
        ypv = y_p.rearrange("(n p) d -> p n d", p=128)
        for tt in range(NT):
            P.dma("sp", ypv[:, tt, :], X[:, tt, :], s_out, reads=[dX[tt]])
        P.dma("sp", y_sT[:, :, :], XsT[:], s_out, reads=[dXs])
        P.barrier()
    return nc


_CACHE = {}


def _fm(v, nch):
    return np.ascontiguousarray(np.asarray(v, np.float32).reshape(nch, 128).T)


def _prepare(inp):
    f32 = np.float32
    g = {k: np.asarray(v) for k, v in inp.items()}
    shared = {
        "w_ada": g["w_ada"], "ev_w_in": g["ev_w_in"][0], "ev_w_out": g["ev_w_out"][0],
        "od_w_in": g["od_w_in"][0], "od_w_out": g["od_w_out"][0], "ffn_w_in": g["ffn_w_in"], "ffn_w_out": g["ffn_w_out"],
        "w_glu": g["od_s5_w_glu"][0], "gla_w_lr": g["ev_gla_w_lr"][0],
    }
    shared["b_adaT"] = np.ascontiguousarray(g["b_ada"].reshape(2, 48, 128).transpose(2, 0, 1))
    shared["ln_gb"] = np.concatenate([g["ln_g"].reshape(-1), g["ln_b"].reshape(-1)]).reshape(1, 8 * D).astype(f32)
    shared["ln_gT"] = np.ascontiguousarray(g["ln_g"].reshape(4, 8, 128).transpose(2, 0, 1))
    shared["ln_bT"] = np.ascontiguousarray(g["ln_b"].reshape(4, 8, 128).transpose(2, 0, 1))
    shared["gla_b_lrT"] = _fm(g["ev_gla_b_lr"][0], 2)
    shared["gla_ng"] = np.tile(g["ev_gla_norm_g"][0], 4).reshape(1, 512).astype(f32)
    shared["hg_ng"] = np.tile(g["od_hg_norm_g"][0], 4).reshape(1, 512).astype(f32)
    shared["conv_wT"] = np.ascontiguousarray(g["ev_conv_w"][0].reshape(4, 4, 128).transpose(2, 1, 0))
    shared["lru_vecT"] = np.ascontiguousarray(np.stack([_fm(g["ev_conv_b"][0], 4), _fm(g["ev_lru_b_r"][0], 4),
                                                        _fm(g["ev_lru_b_i"][0], 4), _fm(g["ev_lru_lam"][0], 4)], axis=2))
    wbd = np.zeros((128, 8, 128), f32)
    for gi, wname in enumerate(("ev_lru_w_r", "ev_lru_w_i")):
        w = g[wname][0]
        for h in range(8):
            grp, hh = h // 2, h % 2
            wbd[hh * 64:(hh + 1) * 64, gi * 4 + grp, hh * 64:(hh + 1) * 64] = w[h]
    shared["lru_wbd"] = wbd
    def chT(a):
        return np.ascontiguousarray(np.asarray(a, f32).reshape(16, 2, 64).transpose(1, 2, 0).reshape(128, 16))
    shared["s5_vecT"] = np.ascontiguousarray(np.stack([chT(g["od_s5_lam_re"][0]), chT(g["od_s5_lam_im"][0]),
                                                       chT(np.repeat(g["od_s5_log_dt"][0][:, None], 64, axis=1))], axis=1))
    bbd = np.zeros((128, 32, 128), f32)
    cbd = np.zeros((128, 32, 128), f32)
    for ri, (bn, cn) in enumerate((("od_s5_b_re", "od_s5_c_re"), ("od_s5_b_im", "od_s5_c_im"))):
        bsrc, csrc = g[bn][0], g[cn][0]
        for gg_ in range(32):
            cg, two = gg_ // 2, gg_ % 2
            r0 = (gg_ % 8) * 16
            bbd[r0:r0 + 16, ri * 16 + cg, two * 64:(two + 1) * 64] = bsrc[gg_].T
            cbd[two * 64:(two + 1) * 64, ri * 16 + cg, r0:r0 + 16] = csrc[gg_].T
    shared["s5_bbd"] = bbd
    shared["s5_cbd"] = cbd
    shared["s5_dT"] = np.ascontiguousarray(np.stack([_fm(g["od_s5_d"][0], 4), _fm(g["od_s5_b_glu"][0], 4)], axis=1))
    shared["hg_lbT"] = np.ascontiguousarray(g["hg_lb_logits"].reshape(2, 4, 128).transpose(2, 0, 1))
    in_maps = []
    for b in range(NCORES):
        sl = slice(b * NS, (b + 1) * NS)
        m = dict(shared)
        m["xp"] = np.ascontiguousarray(g["x_prompt"][b])
        xs = g["x_sample"][sl, 0, :]
        m["xsT"] = np.ascontiguousarray(xs.T.reshape(8, 128, NS).transpose(1, 0, 2))
        c17 = np.concatenate([g["c_prompt"][b:b + 1], g["c_sample"][sl]], axis=0)
        m["cT"] = np.ascontiguousarray(c17.T.reshape(8, 128, 17).transpose(1, 0, 2))
        sg = g["state_gla"][0, sl]
        m["st_gla"] = np.ascontiguousarray(sg.reshape(NS, 2, 2, 64, 128).transpose(2, 3, 0, 1, 4).reshape(128, NS, 2, 128))
        sc = g["state_rglru_conv"][0, sl]
        m["st_conv"] = np.ascontiguousarray(sc.reshape(NS, 3, 4, 128).transpose(3, 2, 1, 0))
        m["st_lru"] = np.ascontiguousarray(g["state_rglru_h"][0, sl].reshape(NS, 4, 128).transpose(2, 1, 0))
        s5 = np.stack([g["state_s5_re"][0, sl], g["state_s5_im"][0, sl]], axis=0)
        m["st_s5"] = np.ascontiguousarray(s5.reshape(2, NS, 16, 2, 64).transpose(3, 4, 0, 2, 1).reshape(128, 2, 16, NS))
        m["st_hg"] = np.ascontiguousarray(g["state_hgrn"][0, sl].transpose(2, 0, 1, 3))
        in_maps.append({k: np.ascontiguousarray(v, dtype=f32) for k, v in m.items()})
    return in_maps


def kernel(**inp):
    if "nc" not in _CACHE:
        _CACHE["nc"] = build_program()
    nc = _CACHE["nc"]
    in_maps = _prepare(inp)
    res = run_bass_kernel_spmd(nc, in_maps, core_ids=list(range(NCORES)))
    return _assemble(res.results)


def _assemble(R):
    f32 = np.float32
    y_p = np.stack([R[b]["y_p"] for b in range(NCORES)], axis=0)
    y_s = np.concatenate([R[b]["y_sT"].transpose(2, 1, 0).reshape(NS, 1, D) for b in range(NCORES)], axis=0)
    gla_p = np.stack([R[b]["o_gla_p"].reshape(2, 64, 2, 128).transpose(2, 0, 1, 3).reshape(4, 64, 128) for b in range(NCORES)], axis=0)[None]
    gla_s = np.concatenate([R[b]["o_gla_s"].reshape(2, 64, NS, 2, 128).transpose(2, 3, 0, 1, 4).reshape(NS, 4, 64, 128)
                            for b in range(NCORES)], axis=0)[None]
    conv_p = np.stack([R[b]["o_conv_p"].transpose(2, 1, 0).reshape(3, 512) for b in range(NCORES)], axis=0)[None]
    conv_s = np.concatenate([R[b]["o_conv_s"].transpose(3, 2, 1, 0).reshape(NS, 3, 512) for b in range(NCORES)], axis=0)[None]
    lru_p = np.stack([R[b]["o_lru_p"].T.reshape(512) for b in range(NCORES)], axis=0)[None]
    lru_s = np.concatenate([R[b]["o_lru_s"].transpose(2, 1, 0).reshape(NS, 512) for b in range(NCORES)], axis=0)[None]

    def s5p(b, i):
        return R[b]["o_s5_p"][:, i, :].reshape(2, 64, 16).transpose(2, 0, 1).reshape(32, 64)

    def s5s(b, i):
        return R[b]["o_s5_s"][:, i].reshape(2, 64, 16, NS).transpose(3, 2, 0, 1).reshape(NS, 32, 64)
    re_p = np.stack([s5p(b, 0) for b in range(NCORES)], axis=0)[None]
    im_p = np.stack([s5p(b, 1) for b in range(NCORES)], axis=0)[None]
    re_s = np.concatenate([s5s(b, 0) for b in range(NCORES)], axis=0)[None]
    im_s = np.concatenate([s5s(b, 1) for b in range(NCORES)], axis=0)[None]
    hg_p = np.stack([R[b]["o_hg_p"].transpose(1, 0, 2) for b in range(NCORES)], axis=0)[None]
    hg_s = np.concatenate([R[b]["o_hg_s"].transpose(1, 2, 0, 3) for b in range(NCORES)], axis=0)[None]
    outs = (y_p, y_s, gla_p, gla_s, conv_p, conv_s, lru_p, lru_s, re_p, re_s, im_p, im_s, hg_p, hg_s)
    return tuple(np.ascontiguousarray(o, dtype=f32) for o in outs)
```

```python
import numpy as np
import concourse.bass as bass
import concourse.mybir as mybir
from concourse.bass_utils import run_bass_kernel_spmd
from contextlib import ExitStack

F32 = mybir.dt.float32
BF16 = mybir.dt.bfloat16
ALU = mybir.AluOpType
AF = mybir.ActivationFunctionType
AX = mybir.AxisListType

NCORES = 8
D = 1024
L = 2048
NT = 16
NS = 16
TW = L + NS
DFF = 2816
ALPHA = 4.0 ** 0.25
LN_EPS = 1e-5
RMS_EPS = 1e-6
GELU_C = 1.5957691216057308


class StopBuild(Exception):
    pass


import os as _os
KSTOP = float(_os.environ.get("KSTOP", "99"))


_DEAD = [False]


def stage(n):
    if n > KSTOP:
        _DEAD[0] = True


class Dep:
    __slots__ = ("w", "re", "rd")

    def __init__(self):
        self.w = None
        self.re = {}
        self.rd = []


class DSem:
    def __init__(self, sem, i):
        self.sem = sem
        self.cnt = 0
        self.id = i


class Prog:
    def __init__(self, nc, es):
        self.nc = nc
        self.es = es
        self.engs = {"pe": nc.tensor, "act": nc.scalar, "dve": nc.vector, "pool": nc.gpsimd, "sp": nc.sync}
        self.sem = {k: es.enter_context(nc.semaphore("sem_" + k)) for k in self.engs}
        self.cnt = {k: 0 for k in self.engs}
        self.known = {k: {} for k in self.engs}
        self.dsems = []

    def dsem(self):
        s = DSem(self.es.enter_context(self.nc.semaphore("dsem%d" % len(self.dsems))), len(self.dsems))
        self.dsems.append(s)
        return s

    def _wait(self, e, tok):
        if tok[0] == "e":
            _, src, val = tok
            if self.known[e].get(src, 0) >= val:
                return
            self.known[e][src] = val
            self.engs[e].wait_ge(self.sem[src], val)
        else:
            ds = tok[1]
            val = ds.cnt
            key = ("d", ds.id)
            if self.known[e].get(key, 0) >= val:
                return
            self.known[e][key] = val
            self.engs[e].wait_ge(ds.sem, val)

    def _deps(self, e, reads, writes, dma_group=False):
        for d in reads:
            for t in self._wl(d):
                self._wait(e, t)
        for d in writes:
            if dma_group and d.w is not None and all(t[0] == "d" for t in self._wl(d)) and not d.re and not d.rd:
                continue
            for t in self._wl(d):
                self._wait(e, t)
            for src, val in d.re.items():
                self._wait(e, ("e", src, val))
            for t in d.rd:
                self._wait(e, t)

    @staticmethod
    def _wl(d):
        if d.w is None:
            return []
        return d.w if isinstance(d.w, list) else [d.w]

    def _done(self, tok, reads, writes, dma_group=False):
        for d in reads:
            if tok[0] == "e":
                d.re[tok[1]] = tok[2]
            else:
                d.rd = [t for t in d.rd if t[1] is not tok[1]] + [tok]
        for d in writes:
            if dma_group and d.w is not None and all(t[0] == "d" for t in self._wl(d)) and not d.re and not d.rd:
                d.w = [t for t in self._wl(d) if t[1] is not tok[1]] + [tok]
                continue
            d.w = tok
            d.re = {}
            d.rd = []

    def op(self, e, fn, reads=(), writes=()):
        if _DEAD[0]:
            return
        self._deps(e, reads, writes)
        ins = fn(self.engs[e])
        self.cnt[e] += 1
        ins.then_inc(self.sem[e], 1)
        self._done(("e", e, self.cnt[e]), reads, writes)

    def dma(self, q, out, in_, ds, reads=(), writes=()):
        if _DEAD[0]:
            return
        if ds is None or ds == "in" or ds == "out":
            if not hasattr(self, "pool_sems"):
                self.pool_sems = [self.dsem() for _ in range(24)]
                self.pool_i = 0
            ds = self.pool_sems[self.pool_i % len(self.pool_sems)]
            self.pool_i += 1
            if ds.cnt > 0:
                self._wait(q, ("d", ds))
        self._deps(q, reads, writes, dma_group=True)
        ins = self.engs[q].dma_start(out=out, in_=in_)
        ds.cnt += 16
        ins.then_inc(ds.sem, 16)
        self._done(("d", ds), reads, writes, dma_group=True)

    def barrier(self):
        for e in self.engs:
            for src in self.engs:
                if src != e and self.cnt[src] > 0:
                    self._wait(e, ("e", src, self.cnt[src]))
            for ds in self.dsems:
                if ds.cnt > 0:
                    self._wait(e, ("d", ds))


def build_program():
    nc = bass.Bass("TRN2", target_bir_lowering=False)

    def din(name, shape):
        return nc.dram_tensor(name, list(shape), F32, kind="ExternalInput").ap()

    def dout(name, shape):
        return nc.dram_tensor(name, list(shape), F32, kind="ExternalOutput").ap()

    xp = din("xp", [L, D])
    xsT_d = din("xsT", [128, 8, NS])
    cT_d = din("cT", [128, 8, 17])
    w_ada = din("w_ada", [2, D, 6 * D])
    b_adaT_d = din("b_adaT", [128, 2, 48])
    ln_gb_d = din("ln_gb", [1, 8 * D])
    ln_gT_d = din("ln_gT", [128, 4, 8])
    ln_bT_d = din("ln_bT", [128, 4, 8])
    ev_w_in = din("ev_w_in", [D, 2576])
    ev_w_out = din("ev_w_out", [D, D])
    od_w_in = din("od_w_in", [D, 2560])
    od_w_out = din("od_w_out", [D, D])
    ffn_w_in = din("ffn_w_in", [2, D, 2 * DFF])
    ffn_w_out = din("ffn_w_out", [2, DFF, D])
    w_glu = din("w_glu", [512, 512])
    gla_w_lr_d = din("gla_w_lr", [16, 256])
    gla_b_lrT_d = din("gla_b_lrT", [128, 2])
    gla_ng_d = din("gla_ng", [1, 512])
    hg_ng_d = din("hg_ng", [1, 512])
    conv_wT_d = din("conv_wT", [128, 4, 4])
    lru_vecT_d = din("lru_vecT", [128, 4, 4])
    lru_wbd_d = din("lru_wbd", [128, 8, 128])
    s5_vecT_d = din("s5_vecT", [128, 3, 16])
    s5_bbd_d = din("s5_bbd", [128, 32, 128])
    s5_cbd_d = din("s5_cbd", [128, 32, 128])
    s5_dT_d = din("s5_dT", [128, 2, 4])
    hg_lbT_d = din("hg_lbT", [128, 2, 4])
    st_gla_d = din("st_gla", [128, NS, 2, 128])
    st_conv_d = din("st_conv", [128, 4, 3, NS])
    st_lru_d = din("st_lru", [128, 4, NS])
    st_s5_d = din("st_s5", [128, 2, 16, NS])
    st_hg_d = din("st_hg", [128, NS, 4, 128])

    y_p = dout("y_p", [L, D])
    y_sT = dout("y_sT", [128, 8, NS])
    o_gla_p = dout("o_gla_p", [128, 2, 128])
    o_gla_s = dout("o_gla_s", [128, NS, 2, 128])
    o_conv_p = dout("o_conv_p", [128, 4, 3])
    o_conv_s = dout("o_conv_s", [128, 4, 3, NS])
    o_lru_p = dout("o_lru_p", [128, 4])
    o_lru_s = dout("o_lru_s", [128, 4, NS])
    o_s5_p = dout("o_s5_p", [128, 2, 16])
    o_s5_s = dout("o_s5_s", [128, 2, 16, NS])
    o_hg_p = dout("o_hg_p", [128, 4, 128])
    o_hg_s = dout("o_hg_s", [128, NS, 4, 128])

    es = ExitStack()
    with es:
        P = Prog(nc, es)

        _nm = [0]

        def sb(name, shape, dt=F32, stack=es):
            _nm[0] += 1
            return stack.enter_context(nc.sbuf_tensor("t%d_%s" % (_nm[0], name), list(shape), dt))

        X = sb("X", [128, NT, D])
        XsT = sb("XsT", [128, 8, NS])
        hT = sb("hT", [128, 8, TW], BF16)
        modT = sb("modT", [128, 2, 48, 17])
        ident = sb("ident", [128, 128])
        identb = sb("identb", [128, 128], BF16)
        onesf = sb("onesf", [128, 128])
        cmask = sb("cmask", [128, 4, 64], BF16)
        pp2 = [es.enter_context(nc.psum_tensor("pp%d" % i, [128, 1024], F32)) for i in range(2)]
        ps = [None] * 4 + [es.enter_context(nc.psum_tensor("ps%d" % i, [128, 512], F32)) for i in range(4, 8)]
        dpp = [Dep(), Dep()]

        dX = [Dep() for _ in range(NT)]
        dXs = Dep()
        dhT = Dep()
        dhTs = Dep()
        dyT = [Dep() for _ in range(8)]
        dyTs = Dep()
        dmod = Dep()
        dconst = Dep()
        dbc = [Dep() for _ in range(3)]
        dps = [Dep() for _ in range(8)]
        s_out = "out"
        s_in = "in"

        def c_ident(e):
            e.memset(ident[:], 0.0)
            e.memset(onesf[:], 1.0)
            return e.memset(cmask[:], 1.0)
        P.op("pool", c_ident, writes=[dconst])

        def c_sel(e):
            e.affine_select(out=ident[:], in_=onesf[:], pattern=[[-1, 128]], compare_op=ALU.is_equal,
                            fill=0.0, base=0, channel_multiplier=1)
            e.affine_select(out=cmask[0:64], in_=cmask[0:64], pattern=[[0, 4], [1, 64]], compare_op=ALU.is_ge,
                            fill=0.0, base=0, channel_multiplier=-1)
            return e.affine_select(out=cmask[64:128], in_=cmask[64:128], pattern=[[0, 4], [1, 64]], compare_op=ALU.is_ge,
                                   fill=0.0, base=0, channel_multiplier=-1)
        P.op("pool", c_sel, reads=[dconst], writes=[dconst])
        P.op("dve", lambda e: e.tensor_copy(out=identb[:], in_=ident[:]), reads=[dconst], writes=[dconst])

        xpv = xp.rearrange("(n p) d -> p n d", p=128)
        for tt in range(NT):
            P.dma("sp", X[:, tt, :], xpv[:, tt, :], s_in, writes=[dX[tt]])
        P.dma("sp", XsT[:], xsT_d[:, :, :], s_in, writes=[dXs])

        def setup_mod(stack):
            cT = sb("cT", [128, 8, 17], stack=stack)
            condT = sb("condT", [128, 8, 17], BF16, stack=stack)
            b_adaT = sb("b_adaT", [128, 2, 48], stack=stack)
            wab = [sb("wab%d" % i, [128, 8, 512], BF16, stack=stack) for i in range(2)]
            mrow = [sb("mrow%d" % i, [17, 512], stack=stack) for i in range(2)]
            dwab = [Dep(), Dep()]
            swab = [P.dsem(), P.dsem()]
            dc = Dep()
            dmrow = [Dep(), Dep()]
            P.dma("sp", cT[:], cT_d[:, :, :], s_in, writes=[dc])
            P.dma("sp", b_adaT[:], b_adaT_d[:, :, :], s_in, writes=[dc])
            P.op("act", lambda e: e.activation(out=condT[:], in_=cT[:], func=AF.Silu), reads=[dc], writes=[dc])
            blocks = [(l, blk) for l in range(2) for blk in range(12)]
            nb_ = len(blocks)
            stt = {"dma": 0, "A": 0, "B": 0, "C": 0, "D": 0}

            def dma(i):
                l, blk = blocks[i]
                wv = w_ada[l].rearrange("(c p) f -> p c f", p=128)
                sl_ = i % 2
                for c in range(8):
                    P.dma("pool", wab[sl_][:, c, :], wv[:, c, blk * 512:(blk + 1) * 512], swab[sl_], writes=[dwab[sl_]])

            def stageA(i):
                sl_ = i % 2

                def mmf(e):
                    ins = None
                    for c in range(8):
                        ins = e.matmul(ps[5][0:17, 0:512], lhsT=condT[:, c, :], rhs=wab[sl_][:, c, :], start=(c == 0), stop=(c == 7))
                    return ins
                P.op("pe", mmf, reads=[dwab[sl_], dc], writes=[dps[5]])

            def stageB(i):
                k = i % 2
                P.op("act", lambda e: e.activation(out=mrow[k][:], in_=ps[5][0:17, 0:512], func=AF.Copy), reads=[dps[5]], writes=[dmrow[k]])

            def stageC(i):
                k = i % 2
                pb = 6 + k

                def trf(e):
                    ins = None
                    for q in range(4):
                        ins = e.transpose(out=ps[pb][:, q * 17:(q + 1) * 17], in_=mrow[k][0:17, q * 128:(q + 1) * 128], identity=ident[0:17, 0:17])
                    return ins
                P.op("pe", trf, reads=[dmrow[k], dconst], writes=[dps[pb]])

            def stageD(i):
                l, blk = blocks[i]
                pb = 6 + i % 2
                P.op("dve", lambda e: e.tensor_tensor(
                    out=modT[:, l, blk * 4:(blk + 1) * 4, :],
                    in0=ps[pb][:, 0:68].rearrange("p (j n) -> p j n", n=17),
                    in1=b_adaT[:, l, blk * 4:(blk + 1) * 4].unsqueeze(2).broadcast_to([128, 4, 17]), op=ALU.add),
                    reads=[dps[pb], dc], writes=[dmod])

            def slot():
                if stt["D"] < stt["C"]:
                    stageD(stt["D"])
                    stt["D"] += 1
                if stt["C"] < stt["B"]:
                    stageC(stt["C"])
                    stt["C"] += 1
                if stt["B"] < stt["A"]:
                    stageB(stt["B"])
                    stt["B"] += 1
                if stt["A"] < nb_:
                    i = stt["A"]
                    while stt["dma"] <= min(i + 1, nb_ - 1):
                        dma(stt["dma"])
                        stt["dma"] += 1
                    stageA(i)
                    stt["A"] += 1

            def issue(n):
                for _ in range(n):
                    if stt["D"] >= nb_:
                        return
                    slot()

            def drain():
                while stt["D"] < stt["A"]:
                    if stt["D"] < stt["C"]:
                        stageD(stt["D"])
                        stt["D"] += 1
                    if stt["C"] < stt["B"]:
                        stageC(stt["C"])
                        stt["C"] += 1
                    if stt["B"] < stt["A"]:
                        stageB(stt["B"])
                        stt["B"] += 1
            issue.drain = drain
            return issue

        def modp(l, g):
            return modT[:, l, g * 8:(g + 1) * 8, 0]

        def mods(l, g):
            return modT[:, l, g * 8:(g + 1) * 8, 1:17]

        small = sb("small", [128, 64])
        dsmall = Dep()
        hs_tmp = sb("hs_tmp", [128, 8, NS])
        dhs_tmp = Dep()
        wfm = [sb("wfm%d" % i, [128, 8, 128], BF16) for i in range(3)]
        dwfm = [Dep() for _ in range(3)]
        swfm = [P.dsem() for _ in range(3)]
        wcnt = [0]
        pcnt = [0]
        k67 = [0]

        def nxt67():
            k67[0] += 1
            return 6 + (k67[0] % 2)

        def load_w(buf, dep, ds, wsrc, c0, n, nk=8):
            wv = wsrc.rearrange("(c p) f -> p c f", p=128)
            P.dma("pool", buf[:, 0:nk, 0:n], wv[:, :, c0:c0 + n], ds, writes=[dep])

        def make_hT(l, g_sc, g_sh):
            P.op("dve", lambda e: e.tensor_scalar(out=small[:, 0:8], in0=modp(l, g_sc), scalar1=1.0, scalar2=None, op0=ALU.add),
                 reads=[dmod], writes=[dsmall])
            for tb in range(4):
                for dc_ in range(8):
                    pb = nxt67()

                    def tr(e, tb=tb, dc_=dc_, pb=pb):
                        ins = None
                        for q in range(4):
                            ins = e.transpose(out=ps[pb][:, q * 128:(q + 1) * 128], in_=X[:, tb * 4 + q, dc_ * 128:(dc_ + 1) * 128],
                                              identity=ident[:])
                        return ins
                    P.op("pe", tr, reads=[dX[tb * 4 + q] for q in range(4)] + [dconst], writes=[dps[pb]])
                    P.op("act", lambda e, tb=tb, dc_=dc_, pb=pb: e.activation(
                        out=hT[:, dc_, tb * 512:(tb + 1) * 512], in_=ps[pb][:], func=AF.Identity,
                        scale=small[:, dc_:dc_ + 1], bias=modp(l, g_sh)[:, dc_:dc_ + 1]),
                        reads=[dps[pb], dsmall, dmod], writes=[dhT])
            P.op("dve", lambda e: e.scalar_tensor_tensor(out=hs_tmp[:], in0=mods(l, g_sc), scalar=1.0, in1=XsT[:],
                                                         op0=ALU.add, op1=ALU.mult),
                 reads=[dmod, dXs], writes=[dhs_tmp])
            P.op("dve", lambda e: e.tensor_tensor(out=hT[:, :, L:TW], in0=hs_tmp[:], in1=mods(l, g_sh), op=ALU.add),
                 reads=[dhs_tmp, dmod], writes=[dhTs])

        def proj_fm(wsrc, c0, n, src, dsrc_p, dsrc_s, cons_p, cons_s, nk=8, wbuf=None):
            if wbuf is None:
                i = wcnt[0] % 3
                wcnt[0] += 1
                load_w(wfm[i], dwfm[i], swfm[i], wsrc, c0, n, nk)
                wb, dwb = wfm[i], dwfm[i]
                lw = lambda c: wb[:, c, 0:n]
            else:
                lw, dwb = wbuf
            if cons_p is not None:
                for th in range(2):
                    j = pcnt[0] % 2
                    pcnt[0] += 1

                    def mm(e, th=th, j=j):
                        ins = None
                        for tb in range(2):
                            for c in range(nk):
                                ins = e.matmul(pp2[j][0:n, tb * 512:(tb + 1) * 512], lhsT=lw(c),
                                               rhs=src[:, c, th * 1024 + tb * 512: th * 1024 + (tb + 1) * 512],
                                               start=(c == 0), stop=(c == nk - 1))
                        return ins
                    P.op("pe", mm, reads=[dwb, dsrc_p], writes=[dpp[j]])
                    cons_p(th, pp2[j][0:n, :], dpp[j])
            if cons_s is not None:
                def mms(e):
                    ins = None
                    for c in range(nk):
                        ins = e.matmul(ps[4][0:n, 0:NS], lhsT=lw(c), rhs=src[:, c, L:TW], start=(c == 0), stop=(c == nk - 1))
                    return ins
                P.op("pe", mms, reads=[dwb, dsrc_s], writes=[dps[4]])
                cons_s(ps[4][0:n, 0:NS], dps[4])

        def gelu_from(src_ap, dsrc, out_ap, dout, tmp, dtmp, n):
            P.op("act", lambda e: e.activation(out=tmp, in_=src_ap, func=AF.Square), reads=[dsrc], writes=[dtmp])
            P.op("dve", lambda e: e.tensor_scalar(out=tmp, in0=tmp, scalar1=0.044715, scalar2=1.0, op0=ALU.mult, op1=ALU.add),
                 reads=[dtmp], writes=[dtmp])
            P.op("dve", lambda e: e.tensor_tensor(out=tmp, in0=src_ap, in1=tmp, op=ALU.mult), reads=[dsrc, dtmp], writes=[dtmp])
            P.op("act", lambda e: e.activation(out=tmp, in_=tmp, func=AF.Sigmoid, scale=GELU_C), reads=[dtmp], writes=[dtmp])
            P.op("dve", lambda e: e.tensor_tensor(out=out_ap, in0=src_ap, in1=tmp, op=ALU.mult), reads=[dsrc, dtmp], writes=[dout])

        I16 = sb("I16", [128, NS, NS], BF16)
        P.op("pool", lambda e: e.memset(I16[:], 1.0), writes=[dconst], reads=[dconst])
        P.op("pool", lambda e: e.affine_select(out=I16[:], in_=I16[:], pattern=[[1, NS], [-1, NS]], compare_op=ALU.is_equal,
                                                fill=0.0, base=0, channel_multiplier=0), reads=[dconst], writes=[dconst])
        stg = sb("stg", [128, 64])
        dstg = Dep()

        def gla_chunks(dk, nh, qsel, kt, kd, vtok, gg, el, S, Sb_unused, ydst, dyT, deps_in, S_out_dram):
            hp = 128 // dk
            nfq = nh // hp
            NV = nh * 128
            NC = 2 * NT
            dS_ = Dep()
            with ExitStack() as st:
                scT = [sb("scT%d" % i, [128, nh, 64], BF16, stack=st) for i in range(2)]
                dscT = [Dep(), Dep()]
                Sb2 = [sb("Sb2_%d" % i, [128, nfq, 128], BF16, stack=st) for i in range(2)]
                dSb2 = [Dep(), Dep()]
                osq = sb("osq", [128, NV], stack=st)
                rs = sb("rs", [128, 2 * nh], stack=st)
                ytok = [sb("ytok%d" % i, [128, NV], BF16, stack=st) for i in range(2)]
                dytok = [Dep(), Dep()]
                dloc = Dep()
                dppB = [Dep(), Dep()]
                P.op("dve", lambda e: e.memset(S[:], 0.0), writes=[dS_])
                for i in range(2):
                    P.op("dve", lambda e, i=i: e.memset(Sb2[i][:], 0.0), writes=[dSb2[i]])
                    P.op("dve", lambda e, i=i: e.memset(scT[i][:], 0.0), writes=[dscT[i]])

                def issue_scores(c):
                    tt, cc = divmod(c, 2)
                    po, t0, j = cc * 64, c * 64, cc

                    def sc_mm(e):
                        ins = None
                        for hl in range(nh):
                            fq = hl // hp
                            ins = e.matmul(pp2[j][po:po + 64, hl * 64:(hl + 1) * 64], lhsT=kt[:, fq, t0:t0 + 64],
                                           rhs=qsel(hl)[:, t0:t0 + 64], start=True, stop=True)
                        return ins
                    P.op("pe", sc_mm, reads=deps_in, writes=[dpp[j]])
                    P.op("dve", lambda e: e.tensor_tensor(
                        out=scT[j][po:po + 64], in0=pp2[j][po:po + 64, 0:nh * 64].rearrange("p (h t) -> p h t", t=64),
                        in1=cmask[po:po + 64, 0:nh, :], op=ALU.mult), reads=[dpp[j], dconst], writes=[dscT[j]])

                def issue_state(c):
                    tt, cc = divmod(c, 2)
                    po = cc * 64
                    db = 7 if cc == 0 else 4

                    def ds_mm(e):
                        ins = None
                        for hl in range(nh):
                            pr = (hl % hp) * dk
                            fq = hl // hp
                            ins = e.matmul(ps[db][pr:pr + dk, fq * 128:(fq + 1) * 128], lhsT=kd[po:po + 64, tt, hl * dk:(hl + 1) * dk],
                                           rhs=vtok[po:po + 64, tt, hl * 128:(hl + 1) * 128], start=True, stop=True)
                        return ins
                    P.op("pe", ds_mm, reads=deps_in, writes=[dps[db]])
                    for fq in range(nfq):
                        P.op("dve", lambda e, fq=fq: e.scalar_tensor_tensor(
                            out=S[:, fq, :], in0=S[:, fq, :], scalar=el[:, fq, c:c + 1], in1=ps[db][:, fq * 128:(fq + 1) * 128],
                            op0=ALU.mult, op1=ALU.add), reads=[dps[db]] + deps_in, writes=[dS_])
                    P.op("act", lambda e: e.activation(out=Sb2[c % 2][:], in_=S[:], func=AF.Copy), reads=[dS_], writes=[dSb2[c % 2]])

                def issue_out(c):
                    tt, cc = divmod(c, 2)
                    po, t0, j = cc * 64, c * 64, cc
                    ob = pp2[tt % 2]
                    sprev = Sb2[(c - 1) % 2]

                    def o_mm(e):
                        ins = None
                        for hl in range(nh):
                            fq = hl // hp
                            e.matmul(ob[po:po + 64, 512 + hl * 128:512 + (hl + 1) * 128], lhsT=scT[j][:, hl, :],
                                     rhs=vtok[:, tt, hl * 128:(hl + 1) * 128], start=True, stop=False)
                            ins = e.matmul(ob[po:po + 64, 512 + hl * 128:512 + (hl + 1) * 128], lhsT=qsel(hl)[:, t0:t0 + 64],
                                           rhs=sprev[:, fq, 0:128], start=False, stop=True)
                        return ins
                    P.op("pe", o_mm, reads=deps_in + [dscT[j], dSb2[(c - 1) % 2]], writes=[dppB[tt % 2]])

                def issue_epilogue(tt):
                    ob = pp2[tt % 2][:, 512:512 + NV]
                    dob = dppB[tt % 2]
                    yk = ytok[tt % 2]
                    P.op("act", lambda e: e.activation(out=osq[:], in_=ob, func=AF.Square), reads=[dob], writes=[dloc])
                    P.op("dve", lambda e: e.tensor_reduce(out=rs[:, 0:nh], in_=osq[:].rearrange("p (h v) -> p h v", v=128),
                                                          axis=AX.X, op=ALU.add), reads=[dloc], writes=[dloc])
                    P.op("act", lambda e: e.activation(out=rs[:, 0:nh], in_=rs[:, 0:nh], func=AF.Sqrt, scale=1.0 / 128, bias=RMS_EPS),
                         reads=[dloc], writes=[dloc])
                    P.op("dve", lambda e: e.reciprocal(out=rs[:, nh:2 * nh], in_=rs[:, 0:nh]), reads=[dloc], writes=[dloc])
                    P.op("dve", lambda e: e.tensor_tensor(out=osq[:].rearrange("p (h v) -> p h v", v=128),
                                                          in0=ob.rearrange("p (h v) -> p h v", v=128),
                                                          in1=rs[:, nh:2 * nh].unsqueeze(2).broadcast_to([128, nh, 128]), op=ALU.mult),
                         reads=[dob, dloc], writes=[dloc])
                    P.op("dve", lambda e: e.tensor_tensor(out=yk[:], in0=osq[:], in1=gg[:, tt, :], op=ALU.mult),
                         reads=[dloc] + deps_in, writes=[dytok[tt % 2]])

                def issue_ytr(tt):
                    yk = ytok[tt % 2]
                    pb = 6 if tt % 2 == 0 else 5

                    def ytr(e):
                        ins = None
                        for q in range(nh):
                            ins = e.transpose(out=ps[pb][:].bitcast(BF16)[:, q * 128:(q + 1) * 128], in_=yk[:, q * 128:(q + 1) * 128],
                                              identity=identb[:])
                        return ins
                    P.op("pe", ytr, reads=[dytok[tt % 2], dconst], writes=[dps[pb]])
                    P.op("act", lambda e: e.activation(
                        out=ydst(tt), in_=ps[pb][:].bitcast(BF16)[:, 0:NV].rearrange("p (q t) -> p q t", t=128), func=AF.Copy),
                        reads=[dps[pb]], writes=[dyT])

                issue_scores(0)
                for c in range(NC):
                    tt, cc = divmod(c, 2)
                    issue_state(c)
                    if c + 1 < NC:
                        issue_scores(c + 1)
                    issue_out(c)
                    if cc == 1:
                        issue_epilogue(tt)
                        if tt >= 1:
                            issue_ytr(tt - 1)
                issue_ytr(NT - 1)
                P.dma("sp", S_out_dram, S[:], s_out, reads=[dS_])
                P.barrier()

        def transpose_kd(kdT, dkdT, kd, dkd, fq, nf):
            for tb in range(4):
                pb = nxt67()

                def tr(e, tb=tb, pb=pb):
                    ins = None
                    for q in range(4):
                        ins = e.transpose(out=ps[pb][:].bitcast(BF16)[:, q * 128:(q + 1) * 128],
                                          in_=kdT[:, (tb * 4 + q) * 128:(tb * 4 + q + 1) * 128], identity=identb[:])
                    return ins
                P.op("pe", tr, reads=[dkdT, dconst], writes=[dps[pb]])
                P.op("act", lambda e, tb=tb, pb=pb: e.activation(
                    out=kd[:, tb * 4:(tb + 1) * 4, fq * 128:(fq + 1) * 128],
                    in_=ps[pb][:].bitcast(BF16)[:, 0:512].rearrange("p (q t) -> p q t", t=128), func=AF.Copy),
                    reads=[dps[pb]], writes=[dkd])

        def proj_tm(wsrc, c0, wt, dwt, swt, cons_p, cons_s, n=512):
            load_w(wt, dwt, swt, wsrc, c0, n)
            for tt in range(NT):
                pb = 5 + (tt % 3)

                def mm(e, tt=tt, pb=pb):
                    ins = None
                    for c in range(8):
                        ins = e.matmul(ps[pb][:, 0:n], lhsT=hT[:, c, tt * 128:(tt + 1) * 128], rhs=wt[:, c, 0:n], start=(c == 0), stop=(c == 7))
                    return ins
                P.op("pe", mm, reads=[dwt, dhT], writes=[dps[pb]])
                cons_p(tt, ps[pb][:, 0:n], dps[pb])

            def mms(e):
                ins = None
                for c in range(8):
                    ins = e.matmul(ps[5][0:NS, 0:n], lhsT=hT[:, c, L:TW], rhs=wt[:, c, 0:n], start=(c == 0), stop=(c == 7))
                return ins
            P.op("pe", mms, reads=[dwt, dhTs], writes=[dps[5]])
            cons_s(ps[5][0:NS, 0:n], dps[5])

        def sample_state_update(dk, q_s, k_tok, v_tok, a_s, Sst, dSst, dq, st, tag):
            hp = 128 // dk
            nfq = 4 // hp
            KM = sb("KM" + tag, [NS, NS, 4 * dk], BF16, stack=st)
            QM = sb("QM" + tag, [128, 4, NS, NS], stack=st)
            qz = sb("qz" + tag, [128, 4, NS], stack=st)
            dkm = Dep()
            P.op("dve", lambda e: e.tensor_tensor(out=KM[:], in0=k_tok.unsqueeze(1).broadcast_to([NS, NS, 4 * dk]),
                                                  in1=ident[0:NS, 0:NS].unsqueeze(2).broadcast_to([NS, NS, 4 * dk]), op=ALU.mult),
                 reads=dq + [dconst], writes=[dkm])
            P.op("dve", lambda e: e.memset(qz[:], 0.0), writes=[dkm])
            for h in range(4):
                pr = (h % hp) * dk
                P.op("dve", lambda e, h=h, pr=pr: e.tensor_copy(out=qz[pr:pr + dk, h, :], in_=q_s[pr:pr + dk, h // hp, :]), reads=dq + [dkm], writes=[dkm])
            P.op("dve", lambda e: e.tensor_tensor(out=QM[:], in0=qz[:].unsqueeze(2).broadcast_to([128, 4, NS, NS]),
                                                  in1=I16[:].unsqueeze(1).broadcast_to([128, 4, NS, NS]), op=ALU.mult),
                 reads=dq + [dconst, dkm], writes=[dkm])
            grp = 0
            for h in range(4):
                pr = (h % hp) * dk
                fq = h // hp
                for b0 in range(0, NS, 4):
                    db = 7 if grp % 2 == 0 else 4
                    grp += 1

                    def dsm(e, h=h, b0=b0, pr=pr, db=db):
                        ins = None
                        for q in range(4):
                            ins = e.matmul(ps[db][pr:pr + dk, q * 128:(q + 1) * 128], lhsT=KM[0:NS, b0 + q, h * dk:(h + 1) * dk],
                                           rhs=v_tok[0:NS, h * 128:(h + 1) * 128], start=True, stop=True)
                        return ins
                    P.op("pe", dsm, reads=[dkm] + dq, writes=[dps[db]])
                    for q in range(4):
                        b = b0 + q
                        P.op("dve", lambda e, b=b, q=q, pr=pr, fq=fq, db=db: e.scalar_tensor_tensor(
                            out=Sst[pr:pr + dk, b, fq, :], in0=Sst[pr:pr + dk, b, fq, :], scalar=a_s[pr:pr + dk, fq, b:b + 1],
                            in1=ps[db][pr:pr + dk, q * 128:(q + 1) * 128], op0=ALU.mult, op1=ALU.add), reads=[dps[db]] + dq, writes=[dSst])

                    def omm(e, h=h, b0=b0, fq=fq):
                        ins = None
                        for q in range(4):
                            b = b0 + q
                            ins = e.matmul(ps[5][0:NS, h * 128:(h + 1) * 128], lhsT=QM[:, h, b, :], rhs=Sst[:, b, fq, :],
                                           start=(b == 0), stop=(b == NS - 1))
                        return ins
                    P.op("pe", omm, reads=[dkm, dSst], writes=[dps[5]])

        def rms_gate_sample(gg_s, dgg, ydst_s, dyTs, st, tag):
            osq = sb("osq_s" + tag, [NS, 512], stack=st)
            rs = sb("rs_s" + tag, [NS, 8], stack=st)
            d = Dep()
            P.op("act", lambda e: e.activation(out=osq[:], in_=ps[5][0:NS, :], func=AF.Square), reads=[dps[5]], writes=[d])
            P.op("dve", lambda e: e.tensor_reduce(out=rs[:, 0:4], in_=osq[:].rearrange("p (h v) -> p h v", v=128), axis=AX.X, op=ALU.add),
                 reads=[d], writes=[d])
            P.op("act", lambda e: e.activation(out=rs[:, 0:4], in_=rs[:, 0:4], func=AF.Sqrt, scale=1.0 / 128, bias=RMS_EPS), reads=[d], writes=[d])
            P.op("dve", lambda e: e.reciprocal(out=rs[:, 4:8], in_=rs[:, 0:4]), reads=[d], writes=[d])
            P.op("dve", lambda e: e.tensor_tensor(out=osq[:].rearrange("p (h v) -> p h v", v=128),
                                                  in0=ps[5][0:NS, :].rearrange("p (h v) -> p h v", v=128),
                                                  in1=rs[:, 4:8].unsqueeze(2).broadcast_to([NS, 4, 128]), op=ALU.mult),
                 reads=[dps[5], d], writes=[d])
            P.op("dve", lambda e: e.tensor_tensor(out=osq[:], in0=osq[:], in1=gg_s, op=ALU.mult), reads=[d, dgg], writes=[d])

            def tr(e):
                ins = None
                for q in range(4):
                    ins = e.transpose(out=ps[6][:, q * NS:(q + 1) * NS], in_=osq[0:NS, q * 128:(q + 1) * 128], identity=ident[0:NS, 0:NS])
                return ins
            P.op("pe", tr, reads=[d, dconst], writes=[dps[6]])
            P.op("act", lambda e: e.activation(out=ydst_s, in_=ps[6][:, 0:4 * NS].rearrange("p (q t) -> p q t", t=NS), func=AF.Copy),
                 reads=[dps[6]], writes=[dyTs])
        def make_rmask(rmask, drm):
            P.op("pool", lambda e: e.memset(rmask[:], 1.0), writes=[drm])
            P.op("pool", lambda e: e.affine_select(out=rmask[:].rearrange("p (c j) -> p c j", j=64),
                                                    in_=rmask[:].rearrange("p (c j) -> p c j", j=64),
                                                    pattern=[[0, 32], [1, 64]], compare_op=ALU.is_gt, fill=0.0, base=0,
                                                    channel_multiplier=0), reads=[drm], writes=[drm])

        def acopy(out, in_, reads, writes, eng="act"):
            if eng == "act":
                P.op("act", lambda e: e.activation(out=out, in_=in_, func=AF.Copy), reads=reads, writes=writes)
            else:
                P.op(eng, lambda e: e.tensor_copy(out=out, in_=in_), reads=reads, writes=writes)

        def even_mixer(l, yoth, dyoth, dyoths, bg=None, bg_stack=None):
            W = ev_w_in
            stage(3)
            with ExitStack() as st:
                lruv = sb("lruv", [128, 4, 4], stack=st)
                convw = sb("convw", [128, 4, 4], stack=st)
                wbd = sb("wbd", [128, 8, 128], BF16, stack=st)
                stconv = sb("stconv", [128, 4, 3, NS], stack=st)
                stlru = sb("stlru", [128, 4, NS], stack=st)
                oconv = sb("oconv", [128, 4, 3, NS], stack=st)
                olru = sb("olru", [128, 4, NS], stack=st)
                xrp = sb("xrp", [128, L + 3], stack=st)
                xc = sb("xc", [128, L], stack=st)
                xcb = sb("xcb", [128, L], BF16, stack=st)
                rr = sb("rr", [128, L], stack=st)
                ig = sb("ig", [128, L], stack=st)
                aa = sb("aa", [128, L], stack=st)
                gg = sb("gg", [128, L], stack=st)
                sm = sb("lru_sm", [128, 16, NS], stack=st)
                smb = sb("lru_smb", [128, NS], BF16, stack=st)
                dpar, dxr, dxc, drr, dig, daa, dgg, dsm, dost = Dep(), Dep(), Dep(), Dep(), Dep(), Dep(), Dep(), Dep(), Dep()
                swbd = P.dsem()
                P.dma("sp", lruv[:], lru_vecT_d[:, :, :], s_in, writes=[dpar])
                P.dma("sp", convw[:], conv_wT_d[:, :, :], s_in, writes=[dpar])
                P.dma("sp", stconv[:], st_conv_d[:, :, :, :], s_in, writes=[dpar])
                P.dma("sp", stlru[:], st_lru_d[:, :, :], s_in, writes=[dpar])
                P.dma("pool", wbd[:], lru_wbd_d[:, :, :], swbd, writes=[dpar])
                c8 = small[:, 16:20]
                c16 = small[:, 20:24]
                P.op("act", lambda e: e.activation(out=small[:, 24:28], in_=lruv[:, :, 3], func=AF.Exp, scale=-1.0), reads=[dpar], writes=[dsmall])
                P.op("act", lambda e: e.activation(out=small[:, 24:28], in_=small[:, 24:28], func=AF.Ln, bias=1.0), reads=[dsmall], writes=[dsmall])
                P.op("dve", lambda e: e.tensor_scalar(out=c8, in0=small[:, 24:28], scalar1=-8.0, scalar2=None, op0=ALU.mult), reads=[dsmall], writes=[dsmall])
                P.op("dve", lambda e: e.tensor_scalar(out=c16, in0=small[:, 24:28], scalar1=-16.0, scalar2=None, op0=ALU.mult), reads=[dsmall], writes=[dsmall])
                P.op("dve", lambda e: e.memset(xrp[:, 0:3], 0.0), writes=[dxr])
                for g in range(4):
                    def cons_xr(th, pap, dp):
                        acopy(xrp[:, 3 + th * 1024: 3 + (th + 1) * 1024], pap, [dp], [dxr])

                    def cons_xr_s(pap, dp):
                        acopy(sm[:, 0, :], pap, [dp], [dsm])
                    proj_fm(W, 1552 + g * 128, 128, hT, dhT, dhTs, cons_xr, cons_xr_s)
                    if bg is not None:
                        bg(1)
                    def cons_gr(th, pap, dp):
                        sl = slice(th * 1024, (th + 1) * 1024)
                        gelu_from(pap, dp, gg[:, sl], dgg, aa[:, sl], daa, 1024)

                    def cons_gr_s(pap, dp):
                        gelu_from(pap, dp, sm[:, 1, :], dsm, sm[:, 2, :], dsm, NS)
                    proj_fm(W, 2064 + g * 128, 128, hT, dhT, dhTs, cons_gr, cons_gr_s)
                    if bg is not None:
                        bg(1)
                    P.op("dve", lambda e, g=g: e.tensor_scalar(out=xc[:], in0=xrp[:, 3:3 + L], scalar1=convw[:, g, 3:4], scalar2=lruv[:, g, 0:1],
                                                              op0=ALU.mult, op1=ALU.add), reads=[dxr, dpar], writes=[dxc])
                    for j in range(3):
                        P.op("dve", lambda e, g=g, j=j: e.scalar_tensor_tensor(out=xc[:], in0=xrp[:, j:j + L], scalar=convw[:, g, j:j + 1], in1=xc[:],
                                                                           op0=ALU.mult, op1=ALU.add), reads=[dxr, dpar, dxc], writes=[dxc])
                    acopy(xcb[:], xc[:], [dxc], [dxc])
                    acopy(stg[:, g * 3:(g + 1) * 3], xrp[:, L:L + 3], [dxr], [dstg], eng="dve")
                    if bg is not None:
                        bg(1)
                    P.op("dve", lambda e, g=g: e.tensor_scalar(out=sm[:, 3, :], in0=sm[:, 0, :], scalar1=convw[:, g, 3:4], scalar2=lruv[:, g, 0:1],
                                                              op0=ALU.mult, op1=ALU.add), reads=[dsm, dpar], writes=[dsm])
                    for j in range(3):
                        P.op("dve", lambda e, g=g, j=j: e.scalar_tensor_tensor(out=sm[:, 3, :], in0=stconv[:, g, j, :], scalar=convw[:, g, j:j + 1],
                                                                           in1=sm[:, 3, :], op0=ALU.mult, op1=ALU.add), reads=[dsm, dpar], writes=[dsm])
                    acopy(smb[:], sm[:, 3, :], [dsm], [dsm])
                    acopy(oconv[:, g, 0:2, :], stconv[:, g, 1:3, :], [dpar], [dost], eng="dve")
                    acopy(oconv[:, g, 2, :], sm[:, 0, :], [dsm], [dost], eng="dve")
                    for gi, (dst, ddst) in enumerate(((rr, drr), (ig, dig))):
                        for th in range(2):
                            j = pcnt[0] % 2
                            pcnt[0] += 1

                            def gmm(e, th=th, j=j, gi=gi, g=g):
                                ins = None
                                for tb in range(2):
                                    ins = e.matmul(pp2[j][:, tb * 512:(tb + 1) * 512], lhsT=wbd[:, gi * 4 + g, :],
                                                   rhs=xcb[:, th * 1024 + tb * 512: th * 1024 + (tb + 1) * 512], start=True, stop=True)
                                return ins
                            P.op("pe", gmm, reads=[dpar, dxc], writes=[dpp[j]])
                            P.op("act", lambda e, th=th, j=j, gi=gi, g=g, dst=dst: e.activation(
                                out=dst[:, th * 1024:(th + 1) * 1024], in_=pp2[j][:, :], func=AF.Sigmoid, bias=lruv[:, g, 1 + gi:2 + gi]),
                                reads=[dpp[j], dpar], writes=[ddst])
                        P.op("pe", lambda e, gi=gi, g=g: e.matmul(ps[4][:, 0:NS], lhsT=wbd[:, gi * 4 + g, :], rhs=smb[:], start=True, stop=True),
                             reads=[dpar, dsm], writes=[dps[4]])
                        P.op("act", lambda e, gi=gi, g=g: e.activation(out=sm[:, 4 + gi, :], in_=ps[4][:, 0:NS], func=AF.Sigmoid,
                                                                      bias=lruv[:, g, 1 + gi:2 + gi]), reads=[dps[4], dpar], writes=[dsm])
                    if bg is not None:
                        bg(1)
                    P.op("act", lambda e, g=g: e.activation(out=aa[:], in_=rr[:], func=AF.Exp, scale=c8[:, g:g + 1]), reads=[drr, dsmall, dgg], writes=[daa])
                    P.op("act", lambda e, g=g: e.activation(out=rr[:], in_=rr[:], func=AF.Exp, scale=c16[:, g:g + 1]), reads=[dsmall], writes=[drr])
                    P.op("act", lambda e: e.activation(out=rr[:], in_=rr[:], func=AF.Sqrt, scale=-1.0, bias=1.0), reads=[], writes=[drr])
                    P.op("dve", lambda e: e.tensor_tensor(out=ig[:], in0=ig[:], in1=xc[:], op=ALU.mult), reads=[dxc], writes=[dig])
                    P.op("dve", lambda e: e.tensor_tensor(out=ig[:], in0=ig[:], in1=rr[:], op=ALU.mult), reads=[drr], writes=[dig])
                    P.op("dve", lambda e: e.tensor_tensor_scan(out=xc[:], data0=aa[:], data1=ig[:], initial=0.0, op0=ALU.mult, op1=ALU.add),
                         reads=[daa, dig], writes=[dxc])
                    P.op("dve", lambda e, g=g: e.tensor_tensor(out=yoth[:, g, 0:L], in0=xc[:], in1=gg[:], op=ALU.mult), reads=[dxc, dgg], writes=[dyoth])
                    acopy(stg[:, 12 + g:13 + g], xc[:, L - 1:L], [dxc], [dstg], eng="dve")
                    if bg is not None:
                        bg(1)
                    P.op("act", lambda e, g=g: e.activation(out=sm[:, 6, :], in_=sm[:, 4, :], func=AF.Exp, scale=c8[:, g:g + 1]), reads=[dsm, dsmall], writes=[dsm])
                    P.op("act", lambda e, g=g: e.activation(out=sm[:, 7, :], in_=sm[:, 4, :], func=AF.Exp, scale=c16[:, g:g + 1]), reads=[dsm, dsmall], writes=[dsm])
                    P.op("act", lambda e: e.activation(out=sm[:, 7, :], in_=sm[:, 7, :], func=AF.Sqrt, scale=-1.0, bias=1.0), reads=[dsm], writes=[dsm])
                    P.op("dve", lambda e: e.tensor_tensor(out=sm[:, 5, :], in0=sm[:, 5, :], in1=sm[:, 3, :], op=ALU.mult), reads=[dsm], writes=[dsm])
                    P.op("dve", lambda e: e.tensor_tensor(out=sm[:, 5, :], in0=sm[:, 5, :], in1=sm[:, 7, :], op=ALU.mult), reads=[dsm], writes=[dsm])
                    P.op("dve", lambda e, g=g: e.tensor_tensor(out=sm[:, 6, :], in0=sm[:, 6, :], in1=stlru[:, g, :], op=ALU.mult), reads=[dsm, dpar], writes=[dsm])
                    P.op("dve", lambda e, g=g: e.tensor_tensor(out=olru[:, g, :], in0=sm[:, 6, :], in1=sm[:, 5, :], op=ALU.add), reads=[dsm], writes=[dost])
                    P.op("dve", lambda e, g=g: e.tensor_tensor(out=yoth[:, g, L:TW], in0=olru[:, g, :], in1=sm[:, 1, :], op=ALU.mult), reads=[dsm, dost], writes=[dyoths])
                P.dma("sp", o_conv_s[:, :, :, :], oconv[:], s_out, reads=[dost])
                P.dma("sp", o_lru_s[:, :, :], olru[:], s_out, reads=[dost])
                P.dma("sp", o_conv_p.rearrange("p a b -> p (a b)"), stg[:, 0:12], s_out, reads=[dstg])
                P.dma("sp", o_lru_p[:, :], stg[:, 12:16], s_out, reads=[dstg])
                P.barrier()
            if bg is not None:
                bg(24)
                P.barrier()
                bg_stack.close()
            stage(4)
            with ExitStack() as st:
                qt = sb("qt", [128, 2, 2, L], BF16, stack=st)
                kt = sb("kt", [128, 2, L], BF16, stack=st)
                kd = sb("kd", [128, NT, 256], BF16, stack=st)
                elT = sb("elT", [128, 2, 32], stack=st)
                S = sb("S", [128, 2, 128], stack=st)
                Sb = sb("Sb", [128, 2, 128], BF16, stack=st)
                q_s = sb("q_s", [128, 2, NS], stack=st)
                a_s = sb("a_s", [128, 2, NS], stack=st)
                k_tok = sb("k_tok", [NS, 256], BF16, stack=st)
                v_tok = sb("v_tok", [NS, 512], BF16, stack=st)
                gg_s = sb("gg_s", [NS, 512], stack=st)
                gnB = sb("gnB", [128, 512], stack=st)
                dprep, dSst, dgs = Dep(), Dep(), Dep()
                P.op("pool", lambda e: e.memset(qt[:], 0.0), writes=[dprep])
                P.dma("sp", gnB[:], gla_ng_d.partition_broadcast(128), s_in, writes=[dgs])
                with ExitStack() as st2:
                    rmask = sb("rmask", [128, L], stack=st2)
                    csp = sb("csp", [128, L], stack=st2)
                    ee = sb("ee", [128, L], stack=st2)
                    kdT = sb("kdT", [128, L], BF16, stack=st2)
                    lrT = sb("lrT", [16, 1, TW], BF16, stack=st2)
                    wlr = sb("wlr", [16, 256], BF16, stack=st2)
                    blr = sb("blr", [128, 2], stack=st2)
                    sps = sb("sps", [128, NS], stack=st2)
                    drm, dcs, dee, ddd, dkdT, dlr, dw = Dep(), Dep(), Dep(), Dep(), Dep(), Dep(), Dep()
                    swl = P.dsem()
                    make_rmask(rmask, drm)
                    P.dma("pool", wlr[:], gla_w_lr_d[:, :], swl, writes=[dw])
                    P.dma("sp", blr[:], gla_b_lrT_d[:, :], s_in, writes=[dw])
                    P.op("dve", lambda e: e.tensor_scalar(out=blr[:], in0=blr[:], scalar1=-1.0, scalar2=None, op0=ALU.mult), reads=[dw], writes=[dw])
                    proj_fm(W, 1536, 16, hT, dhT, dhTs,
                            lambda th, pap, dp: acopy(lrT[:, 0, th * 1024:(th + 1) * 1024], pap, [dp], [dlr]),
                            lambda pap, dp: acopy(lrT[:, 0, L:TW], pap, [dp], [dlr]))
                    for fq in range(2):
                        def cons_g(th, pap, dp):
                            sl = slice(th * 1024, (th + 1) * 1024)
                            P.op("act", lambda e: e.activation(out=ee[:, sl], in_=pap, func=AF.Exp, scale=-1.0, bias=blr[:, fq:fq + 1]), reads=[dp, dw], writes=[dee])
                            P.op("act", lambda e: e.activation(out=ee[:, sl], in_=ee[:, sl], func=AF.Ln, bias=1.0), reads=[dee], writes=[dee])

                        def cons_g_s(pap, dp):
                            P.op("act", lambda e: e.activation(out=sps[:], in_=pap, func=AF.Exp, scale=-1.0, bias=blr[:, fq:fq + 1]), reads=[dp, dw], writes=[dee])
                            P.op("act", lambda e: e.activation(out=sps[:], in_=sps[:], func=AF.Ln, bias=1.0), reads=[dee], writes=[dee])
                            P.op("act", lambda e: e.activation(out=a_s[:, fq, :], in_=sps[:], func=AF.Exp, scale=-1.0 / 16), reads=[dee], writes=[dprep])
                        proj_fm(None, 0, 128, lrT, dlr, dlr, cons_g, cons_g_s, nk=1,
                                wbuf=(lambda c, fq=fq: wlr[0:16, fq * 128:(fq + 1) * 128], dw))
                        P.op("dve", lambda e: e.tensor_tensor_scan(out=csp[:], data0=rmask[:], data1=ee[:], initial=0.0, op0=ALU.mult, op1=ALU.add),
                             reads=[drm, dee, dkdT], writes=[dcs])
                        P.op("act", lambda e, fq=fq: e.activation(out=elT[:, fq, :], in_=csp[:].rearrange("p (c j) -> p c j", j=64)[:, :, 63],
                                                                   func=AF.Exp, scale=-1.0 / 16), reads=[dcs], writes=[dprep])
                        P.op("act", lambda e: e.activation(out=ee[:], in_=csp[:], func=AF.Exp, scale=-1.0 / 16), reads=[dcs], writes=[dee])

                        def cons_q(th, pap, dp):
                            sl = slice(th * 1024, (th + 1) * 1024)
                            for par in range(2):
                                pr_ = slice(par * 64, (par + 1) * 64)
                                P.op("dve", lambda e, par=par, pr_=pr_: e.scalar_tensor_tensor(
                                    out=qt[pr_, par, fq, sl], in0=pap[pr_], scalar=0.125, in1=ee[pr_, sl], op0=ALU.mult, op1=ALU.mult),
                                    reads=[dp, dee, dprep], writes=[dprep])

                        def cons_q_s(pap, dp):
                            P.op("dve", lambda e: e.tensor_scalar(out=q_s[:, fq, :], in0=pap, scalar1=0.125, scalar2=None, op0=ALU.mult), reads=[dp], writes=[dprep])
                        proj_fm(W, fq * 128, 128, hT, dhT, dhTs, cons_q, cons_q_s)
                        P.op("act", lambda e: e.activation(out=ee[:], in_=csp[:], func=AF.Exp, scale=1.0 / 16), reads=[dcs, dprep], writes=[dee])

                        def cons_k(th, pap, dp):
                            sl = slice(th * 1024, (th + 1) * 1024)
                            P.op("dve", lambda e: e.tensor_tensor(out=kt[:, fq, sl], in0=pap, in1=ee[:, sl], op=ALU.mult), reads=[dp, dee], writes=[dprep])
                            P.op("dve", lambda e: e.tensor_tensor(out=ee[:, sl].rearrange("p (c j) -> p c j", j=64), in0=ee[:, sl].rearrange("p (c j) -> p c j", j=64),
                                                                  in1=elT[:, fq, th * 16:(th + 1) * 16].unsqueeze(2).broadcast_to([128, 16, 64]), op=ALU.mult),
                                 reads=[dprep], writes=[dee])
                            P.op("dve", lambda e: e.tensor_tensor(out=kdT[:, sl], in0=pap, in1=ee[:, sl], op=ALU.mult), reads=[dp, dee], writes=[dkdT])
                        i = wcnt[0] % 3
                        proj_fm(W, 256 + fq * 128, 128, hT, dhT, dhTs, cons_k, None)
                        def mks(e, i=i):
                            ins = None
                            for c in range(8):
                                ins = e.matmul(ps[5][0:NS, 0:128], lhsT=hT[:, c, L:TW], rhs=wfm[i][:, c, 0:128], start=(c == 0), stop=(c == 7))
                            return ins
                        P.op("pe", mks, reads=[dwfm[i], dhTs], writes=[dps[5]])
                        acopy(k_tok[:, fq * 128:(fq + 1) * 128], ps[5][0:NS, 0:128], [dps[5]], [dprep])
                        transpose_kd(kdT, dkdT, kd, dprep, fq, 2)
                    P.barrier()
                stage(4.1)
                with ExitStack() as st3:
                    vtok = sb("vtok", [128, NT, 512], BF16, stack=st3)
                    ggt = sb("ggt", [128, NT, 512], BF16, stack=st3)
                    dv, dg, dgt = Dep(), Dep(), Dep()
                    with ExitStack() as st4:
                        wv = sb("wv", [128, 8, 256], BF16, stack=st4)
                        gtmp = sb("gtmp", [128, 256], stack=st4)
                        dwv = Dep()
                        sv_ = P.dsem()
                        for hv in range(2):
                            cs = slice(hv * 256, (hv + 1) * 256)
                            proj_tm(W, 512 + hv * 256, wv, dwv, sv_,
                                    lambda tt, pap, dp, cs=cs: acopy(vtok[:, tt, cs], pap, [dp], [dv]),
                                    lambda pap, dp, cs=cs: acopy(v_tok[:, cs], pap, [dp], [dprep]), n=256)

                            def cons_gt(tt, pap, dp, cs=cs):
                                P.op("act", lambda e: e.activation(out=gtmp[:], in_=pap, func=AF.Silu), reads=[dp], writes=[dgt])
                                P.op("dve", lambda e: e.tensor_tensor(out=ggt[:, tt, cs], in0=gtmp[:], in1=gnB[:, cs], op=ALU.mult), reads=[dgt, dgs], writes=[dg])

                            def cons_gt_s(pap, dp, cs=cs):
                                P.op("act", lambda e: e.activation(out=gtmp[0:NS, :], in_=pap, func=AF.Silu), reads=[dp], writes=[dgt])
                                P.op("dve", lambda e: e.tensor_tensor(out=gg_s[:, cs], in0=gtmp[0:NS, :], in1=gnB[0:NS, cs], op=ALU.mult), reads=[dgt, dgs], writes=[dgs])
                            proj_tm(W, 1024 + hv * 256, wv, dwv, sv_, cons_gt, cons_gt_s, n=256)
                        P.barrier()
                    stage(4.2)
                    gla_chunks(64, 4, lambda h: qt[:, h % 2, h // 2, :], kt, kd, vtok, ggt, elT, S, Sb,
                               lambda tt: hT[:, 0:4, tt * 128:(tt + 1) * 128], dhT, [dprep, dv, dg], o_gla_p[:, :, :])
                    stage(4.3)
                with ExitStack() as st5:
                    Sst = sb("Sst", [128, NS, 2, 128], stack=st5)
                    P.dma("sp", Sst[:], st_gla_d[:, :, :, :], s_in, writes=[dSst])
                    sample_state_update(64, q_s[:], k_tok[:], v_tok[:], a_s, Sst, dSst, [dprep], st5, "e")
                    rms_gate_sample(gg_s[:], dgs, hT[:, 0:4, L:TW], dhTs, st5, "e")
                    P.dma("sp", o_gla_s[:, :, :, :], Sst[:], s_out, reads=[dSst])
                    P.barrier()

        def build_bcast(dst, ddst, col_ap, add_one):
            for half in range(2):
                pb = nxt67()
                for q in range(4):
                    dc_ = half * 4 + q
                    P.op("dve", lambda e, dc_=dc_: e.tensor_scalar(out=hs_tmp[:].rearrange("p a b -> p (a b)")[:, 0:128], in0=ident[:],
                                                                    scalar1=col_ap[:, dc_:dc_ + 1], scalar2=None, op0=ALU.mult),
                         reads=[dmod, dconst, dsmall], writes=[dhs_tmp])
                    P.op("pe", lambda e, q=q, pb=pb: e.matmul(ps[pb][:, q * 128:(q + 1) * 128], lhsT=onesf[:],
                                                             rhs=hs_tmp[:].rearrange("p a b -> p (a b)")[:, 0:128], start=True, stop=True),
                         reads=[dhs_tmp, dconst], writes=[dps[pb]])
                if add_one:
                    P.op("act", lambda e, half=half, pb=pb: e.activation(out=dst[:, half * 512:(half + 1) * 512], in_=ps[pb][:], func=AF.Identity, bias=1.0),
                         reads=[dps[pb]], writes=[ddst])
                else:
                    acopy(dst[:, half * 512:(half + 1) * 512], ps[pb][:], [dps[pb]], [ddst])

        def ln_bufs(st, tag):
            stats = [sb("ln_stats%s%d" % (tag, i), [128, 2, 6], stack=st) for i in range(2)]
            mv = [sb("ln_mv%s%d" % (tag, i), [128, 8], stack=st) for i in range(2)]
            tmpn = [sb("ln_tmpn%s%d" % (tag, i), [128, D], stack=st) for i in range(2)]
            return stats, mv, tmpn, [Dep(), Dep()]

        def ln_s1(tt, lb):
            stats, mv, tmpn, dls = lb
            k = tt % 2
            for hh in range(2):
                P.op("dve", lambda e, hh=hh: e.bn_stats(out=stats[k][:, hh, :], in_=X[:, tt, hh * 512:(hh + 1) * 512]), reads=[dX[tt]], writes=[dls[k]])
            P.op("dve", lambda e: e.bn_aggr(out=mv[k][:, 0:2], in_=stats[k][:].rearrange("p a b -> p (a b)")), reads=[dls[k]], writes=[dls[k]])
            P.op("act", lambda e: e.activation(out=mv[k][:, 2:3], in_=mv[k][:, 1:2], func=AF.Sqrt, bias=LN_EPS), reads=[dls[k]], writes=[dls[k]])

        def ln_s2(tt, lb):
            stats, mv, tmpn, dls = lb
            k = tt % 2
            P.op("dve", lambda e: e.reciprocal(out=mv[k][:, 3:4], in_=mv[k][:, 2:3]), reads=[dls[k]], writes=[dls[k]])
            P.op("dve", lambda e: e.scalar_tensor_tensor(out=mv[k][:, 4:5], in0=mv[k][:, 0:1], scalar=-1.0, in1=mv[k][:, 3:4], op0=ALU.mult, op1=ALU.mult),
                 reads=[dls[k]], writes=[dls[k]])
            P.op("act", lambda e: e.activation(out=tmpn[k][:], in_=X[:, tt, :], func=AF.Identity, scale=mv[k][:, 3:4], bias=mv[k][:, 4:5]),
                 reads=[dls[k], dX[tt]], writes=[dls[k]])

        def ln_s3(tt, lb, bcs, dbcs):
            stats, mv, tmpn, dls = lb
            k = tt % 2
            P.op("dve", lambda e: e.tensor_tensor(out=tmpn[k][:], in0=tmpn[k][:], in1=bcs[1][:], op=ALU.mult), reads=[dls[k], dbcs[1]], writes=[dls[k]])
            P.op("dve", lambda e: e.tensor_tensor(out=X[:, tt, :], in0=tmpn[k][:], in1=bcs[2][:], op=ALU.add), reads=[dls[k], dbcs[2]], writes=[dX[tt]])

        def ln_sample(l, lni, g_gt, outT_ps_view, dpsv, st):
            v = sb("lns_v%d%d" % (l, lni), [128, 8, 2, NS], stack=st)
            mom = sb("lns_m%d%d" % (l, lni), [128, 4, NS], stack=st)
            lgb = sb("lns_g%d%d" % (l, lni), [128, 2, 8], stack=st)
            d = Dep()
            P.dma("sp", lgb[:, 0, :], ln_gT_d[:, l * 2 + lni, :], s_in, writes=[d])
            P.dma("sp", lgb[:, 1, :], ln_bT_d[:, l * 2 + lni, :], s_in, writes=[d])
            P.op("dve", lambda e: e.scalar_tensor_tensor(out=v[:, :, 0, :], in0=mods(l, g_gt), scalar=1.0, in1=outT_ps_view, op0=ALU.add, op1=ALU.mult),
                 reads=[dmod, dpsv], writes=[d])
            P.op("dve", lambda e: e.scalar_tensor_tensor(out=v[:, :, 0, :], in0=XsT[:], scalar=ALPHA, in1=v[:, :, 0, :], op0=ALU.mult, op1=ALU.add),
                 reads=[dXs, d], writes=[d])
            P.op("dve", lambda e: e.tensor_tensor(out=v[:, :, 1, :], in0=v[:, :, 0, :], in1=v[:, :, 0, :], op=ALU.mult), reads=[d], writes=[d])

            def mm(e):
                ins = None
                for c in range(8):
                    ins = e.matmul(ps[6][:, 0:2 * NS], lhsT=onesf[:], rhs=v[:, c, :, :].rearrange("p a b -> p (a b)"), start=(c == 0), stop=(c == 7))
                return ins
            P.op("pe", mm, reads=[d, dconst], writes=[dps[6]])
            P.op("dve", lambda e: e.tensor_scalar(out=mom[:, 0:2, :], in0=ps[6][:, 0:2 * NS].rearrange("p (a b) -> p a b", b=NS), scalar1=1.0 / D, scalar2=None,
                                                  op0=ALU.mult), reads=[dps[6]], writes=[d])
            P.op("dve", lambda e: e.tensor_tensor(out=mom[:, 2, :], in0=mom[:, 0, :], in1=mom[:, 0, :], op=ALU.mult), reads=[d], writes=[d])
            P.op("dve", lambda e: e.tensor_tensor(out=mom[:, 1, :], in0=mom[:, 1, :], in1=mom[:, 2, :], op=ALU.subtract), reads=[d], writes=[d])
            P.op("act", lambda e: e.activation(out=mom[:, 1, :], in_=mom[:, 1, :], func=AF.Sqrt, bias=LN_EPS), reads=[d], writes=[d])
            P.op("dve", lambda e: e.reciprocal(out=mom[:, 3, :], in_=mom[:, 1, :]), reads=[d], writes=[d])
            P.op("dve", lambda e: e.tensor_tensor(out=v[:, :, 0, :], in0=v[:, :, 0, :], in1=mom[:, 0, :].unsqueeze(1).broadcast_to([128, 8, NS]), op=ALU.subtract),
                 reads=[d], writes=[d])
            P.op("dve", lambda e: e.tensor_tensor(out=v[:, :, 0, :], in0=v[:, :, 0, :], in1=mom[:, 3, :].unsqueeze(1).broadcast_to([128, 8, NS]), op=ALU.mult),
                 reads=[d], writes=[d])
            P.op("dve", lambda e: e.tensor_tensor(out=v[:, :, 0, :], in0=v[:, :, 0, :], in1=lgb[:, 0, :].unsqueeze(2).broadcast_to([128, 8, NS]), op=ALU.mult),
                 reads=[d], writes=[d])
            P.op("dve", lambda e: e.tensor_tensor(out=XsT[:], in0=v[:, :, 0, :], in1=lgb[:, 1, :].unsqueeze(2).broadcast_to([128, 8, NS]), op=ALU.add),
                 reads=[d], writes=[dXs])

        def load_bcs(l, lni, g_gt, st):
            bcs = [sb("bc%d_%d%d" % (i, l, lni), [128, D], stack=st) for i in range(3)]
            dbcs = [Dep(), Dep(), Dep()]
            build_bcast(bcs[0], dbcs[0], modp(l, g_gt), True)
            P.dma("sp", bcs[1][:], ln_gb_d[:, (l * 2 + lni) * D:(l * 2 + lni + 1) * D].partition_broadcast(128), s_in, writes=[dbcs[1]])
            P.dma("sp", bcs[2][:], ln_gb_d[:, (4 + l * 2 + lni) * D:(4 + l * 2 + lni + 1) * D].partition_broadcast(128), s_in, writes=[dbcs[2]])
            return bcs, dbcs

        def out_proj_ln(l, wsrc, ysrc):
            with ExitStack() as st:
                wo = sb("wo", [128, 8, D], BF16, stack=st)
                lb = ln_bufs(st, "o%d" % l)
                dwo = Dep()
                swo = P.dsem()
                for c in range(8):
                    P.dma("pool", wo[:, c, :], wsrc[c * 128:(c + 1) * 128, :], swo, writes=[dwo])
                bcs, dbcs = load_bcs(l, 0, 2, st)
                for tt in range(NT):
                    if tt >= 1:
                        ln_s2(tt - 1, lb)
                    j = pcnt[0] % 2
                    pcnt[0] += 1

                    def mm(e, tt=tt, j=j):
                        ins = None
                        for dh in range(2):
                            for fc in range(8):
                                ins = e.matmul(pp2[j][:, dh * 512:(dh + 1) * 512], lhsT=ysrc(fc)[0](tt),
                                               rhs=wo[:, fc, dh * 512:(dh + 1) * 512], start=(fc == 0), stop=(fc == 7))
                        return ins
                    P.op("pe", mm, reads=[dwo] + [ysrc(fc)[1] for fc in range(8)], writes=[dpp[j]])
                    P.op("dve", lambda e, j=j: e.tensor_tensor(out=pp2[j][:, :], in0=pp2[j][:, :], in1=bcs[0][:], op=ALU.mult), reads=[dbcs[0]], writes=[dpp[j]])
                    P.op("dve", lambda e, tt=tt, j=j: e.scalar_tensor_tensor(out=X[:, tt, :], in0=X[:, tt, :], scalar=ALPHA, in1=pp2[j][:, :], op0=ALU.mult, op1=ALU.add),
                         reads=[dpp[j]], writes=[dX[tt]])
                    ln_s1(tt, lb)
                    if tt >= 1:
                        ln_s3(tt - 1, lb, bcs, dbcs)
                ln_s2(NT - 1, lb)
                ln_s3(NT - 1, lb, bcs, dbcs)
                def mms(e):
                    ins = None
                    for dc_ in range(8):
                        for fc in range(8):
                            ins = e.matmul(ps[4][:, dc_ * NS:(dc_ + 1) * NS], lhsT=wo[:, fc, dc_ * 128:(dc_ + 1) * 128], rhs=ysrc(fc)[2],
                                           start=(fc == 0), stop=(fc == 7))
                    return ins
                P.op("pe", mms, reads=[dwo] + [ysrc(fc)[3] for fc in range(8)], writes=[dps[4]])
                ln_sample(l, 0, 2, ps[4][:, 0:8 * NS].rearrange("p (c n) -> p c n", n=NS), dps[4], st)
                P.barrier()

        def ffn(l):
            make_hT(l, 4, 3)
            Wi = ffn_w_in[l]
            Wo = ffn_w_out[l]
            with ExitStack() as st:
                aT = sb("aT", [128, 22, 1024 + NS], BF16, stack=st)
                woh = sb("woh", [128, 22, 512], BF16, stack=st)
                sg = sb("ffn_sg", [128, 1024], stack=st)
                lb = ln_bufs(st, "f%d" % l)
                if len(wfm) < 4:
                    wfm.append(sb("wfm3_%d" % l, [128, 8, 128], BF16, stack=st))
                    dwfm.append(Dep())
                    swfm.append(P.dsem())
                fcnt = [0]
                daT, dwoh, dsg, daTs = Dep(), Dep(), Dep(), Dep()
                swoh = P.dsem()
                bcs, dbcs = load_bcs(l, 1, 5, st)
                for tblk in range(2):
                    for jf in range(22):
                        P.dma("pool", woh[:, jf, :], Wo[jf * 128:(jf + 1) * 128, 0:512], swoh, writes=[dwoh])
                    for jf in range(22):
                        ig_ = fcnt[0] % 4
                        fcnt[0] += 1
                        load_w(wfm[ig_], dwfm[ig_], swfm[ig_], Wi, jf * 128, 128)
                        iu_ = fcnt[0] % 4
                        fcnt[0] += 1
                        load_w(wfm[iu_], dwfm[iu_], swfm[iu_], Wi, DFF + jf * 128, 128)
                        for (wi_, j) in ((ig_, 0), (iu_, 1)):
                            def mm(e, wi_=wi_, j=j):
                                ins = None
                                for tb in range(2):
                                    for c in range(8):
                                        ins = e.matmul(pp2[j][:, tb * 512:(tb + 1) * 512], lhsT=wfm[wi_][:, c, :],
                                                       rhs=hT[:, c, tblk * 1024 + tb * 512: tblk * 1024 + (tb + 1) * 512], start=(c == 0), stop=(c == 7))
                                return ins
                            P.op("pe", mm, reads=[dwfm[wi_], dhT], writes=[dpp[j]])
                        P.op("act", lambda e: e.activation(out=sg[:], in_=pp2[0][:, :], func=AF.Silu), reads=[dpp[0]], writes=[dsg])
                        P.op("dve", lambda e, jf=jf: e.tensor_tensor(out=aT[:, jf, 0:1024], in0=pp2[1][:, :], in1=sg[:], op=ALU.mult), reads=[dpp[1], dsg], writes=[daT])
                        if tblk == 0:
                            for (wi_, col) in ((ig_, 0), (iu_, NS)):
                                def mms(e, wi_=wi_, col=col):
                                    ins = None
                                    for c in range(8):
                                        ins = e.matmul(ps[4][:, col:col + NS], lhsT=wfm[wi_][:, c, :], rhs=hT[:, c, L:TW], start=(c == 0), stop=(c == 7))
                                    return ins
                                P.op("pe", mms, reads=[dwfm[wi_], dhTs], writes=[dps[4]])
                            P.op("act", lambda e: e.activation(out=sg[:, 0:NS], in_=ps[4][:, 0:NS], func=AF.Silu), reads=[dps[4], dsg], writes=[dsg])
                            P.op("dve", lambda e, jf=jf: e.tensor_tensor(out=aT[:, jf, 1024:1024 + NS], in0=ps[4][:, NS:2 * NS], in1=sg[:, 0:NS], op=ALU.mult),
                                 reads=[dps[4], dsg], writes=[daTs])
                    for dh in range(2):
                        if dh == 1:
                            for jf in range(22):
                                P.dma("pool", woh[:, jf, :], Wo[jf * 128:(jf + 1) * 128, 512:1024], swoh, writes=[dwoh])
                        for t8 in range(8):
                            tt = tblk * 8 + t8
                            if dh == 1 and t8 >= 1:
                                ln_s2(tt - 1, lb)
                            pb = nxt67()

                            def mm2(e, t8=t8, pb=pb):
                                ins = None
                                for jf in range(22):
                                    ins = e.matmul(ps[pb][:, :], lhsT=aT[:, jf, t8 * 128:(t8 + 1) * 128], rhs=woh[:, jf, :], start=(jf == 0), stop=(jf == 21))
                                return ins
                            P.op("pe", mm2, reads=[daT, dwoh], writes=[dps[pb]])
                            P.op("dve", lambda e, pb=pb, dh=dh: e.tensor_tensor(out=ps[pb][:, :], in0=ps[pb][:, :], in1=bcs[0][:, dh * 512:(dh + 1) * 512], op=ALU.mult),
                                 reads=[dbcs[0]], writes=[dps[pb]])
                            P.op("dve", lambda e, tt=tt, dh=dh, pb=pb: e.scalar_tensor_tensor(out=X[:, tt, dh * 512:(dh + 1) * 512], in0=X[:, tt, dh * 512:(dh + 1) * 512],
                                                                                    scalar=ALPHA, in1=ps[pb][:, :], op0=ALU.mult, op1=ALU.add),
                                 reads=[dps[pb]], writes=[dX[tt]])
                            if dh == 1:
                                ln_s1(tt, lb)
                                if t8 >= 1:
                                    ln_s3(tt - 1, lb, bcs, dbcs)
                        if dh == 1:
                            ln_s2(tblk * 8 + 7, lb)
                            ln_s3(tblk * 8 + 7, lb, bcs, dbcs)
                        if tblk == 0:
                            def mms2(e, dh=dh):
                                ins = None
                                for q in range(4):
                                    for jf in range(22):
                                        ins = e.matmul(ps[5][:, (dh * 4 + q) * NS:(dh * 4 + q + 1) * NS], lhsT=woh[:, jf, q * 128:(q + 1) * 128],
                                                       rhs=aT[:, jf, 1024:1024 + NS], start=(jf == 0), stop=(jf == 21))
                                return ins
                            P.op("pe", mms2, reads=[daTs, dwoh], writes=[dps[5]])
                    if tblk == 0:
                        ln_sample(l, 1, 5, ps[5][:, 0:8 * NS].rearrange("p (c n) -> p c n", n=NS), dps[5], st)
                P.barrier()
                wfm.pop()
                dwfm.pop()
                swfm.pop()

        def s5_mixer(l, yoth, dyoth, dyoths):
            W = od_w_in
            PI = float(np.pi)
            with ExitStack() as st:
                sv = sb("s5v", [128, 3, 16], stack=st)
                pr = sb("s5pr", [128, 18, 16], stack=st)
                ce = sb("s5ce", [128, 11, 16], stack=st)
                dTt = sb("s5dT", [128, 2, 4], stack=st)
                Bb = sb("s5Bb", [128, 32, 128], BF16, stack=st)
                Cp = sb("s5Cp", [128, 32, 128], BF16, stack=st)
                s0 = sb("s5s0", [128, 2, 16, NS], stack=st)
                hl = sb("s5hl", [128, 2, 16], stack=st)
                ygb = sb("s5ygb", [128, 4, TW], BF16, stack=st)
                uT = yoth
                sms = sb("s5sm", [128, 1, NS], stack=st)
                ysa = sb("s5ysa", [128, 4, NS], stack=st)
                dpar, dB, dC, dCs, ds0, dos, dhl, dyg, dygs, du, dus = (Dep() for _ in range(11))
                dtab, dxr, dxi, dh, dt1_, dt2_, dsm, dysa = (Dep() for _ in range(8))
                dos_all = [Dep(), Dep()]
                dCst = [Dep(), Dep()]
                sB = P.dsem()
                P.dma("sp", sv[:], s5_vecT_d[:, :, :], s_in, writes=[dpar])
                P.dma("sp", dTt[:], s5_dT_d[:, :, :], s_in, writes=[dpar])
                P.dma("sp", s0[:], st_s5_d[:, :, :, :], s_in, writes=[ds0])
                for hf in range(2):
                    P.dma("pool", Bb[:, hf * 16:(hf + 1) * 16, :], s5_bbd_d[:, hf * 16:(hf + 1) * 16, :], sB, writes=[dB])
                LR, LI, DT, RHO, TH, C0, S0_, ABR, ABI, FR, FI, FIR, FII, T0, T1_, T2_ = (pr[:, i, :] for i in range(16))

                def dv(fn, reads=(dpar,), writes=(dpar,)):
                    P.op("dve", fn, reads=list(reads), writes=list(writes))

                def ac(fn):
                    P.op("act", fn, reads=[dpar], writes=[dpar])
                ac(lambda e: e.activation(out=DT, in_=sv[:, 2, :], func=AF.Exp))
                dv(lambda e: e.tensor_copy(out=LR, in_=sv[:, 0, :]))
                dv(lambda e: e.tensor_copy(out=LI, in_=sv[:, 1, :]))
                dv(lambda e: e.tensor_tensor(out=T0, in0=LR, in1=DT, op=ALU.mult))
                ac(lambda e: e.activation(out=RHO, in_=T0, func=AF.Exp))
                dv(lambda e: e.tensor_tensor(out=TH, in0=LI, in1=DT, op=ALU.mult))
                TWO_PI = 2 * PI

                def wrap_small(ap, tmp):
                    dv(lambda e: e.tensor_scalar(out=tmp, in0=ap, scalar1=1.0, scalar2=-1.0, op0=ALU.is_ge, op1=ALU.mult))
                    dv(lambda e: e.tensor_tensor(out=ap, in0=ap, in1=tmp, op=ALU.add))
                dv(lambda e: e.tensor_scalar(out=T0, in0=TH, scalar1=1.0 / TWO_PI, scalar2=None, op0=ALU.mult))
                for k in range(8):
                    wrap_small(T0, T1_)
                dv(lambda e: e.tensor_scalar(out=T1_, in0=T0, scalar1=0.0, scalar2=1.0, op0=ALU.is_lt, op1=ALU.mult))
                dv(lambda e: e.tensor_tensor(out=T0, in0=T0, in1=T1_, op=ALU.add))
                dv(lambda e: e.tensor_copy(out=ce[:, 0, :], in_=T0))
                for k in range(1, 11):
                    dv(lambda e, k=k: e.tensor_scalar(out=ce[:, k, :], in0=ce[:, k - 1, :], scalar1=2.0, scalar2=None, op0=ALU.mult))
                    wrap_small(ce[:, k, :], T1_)
                ac(lambda e: e.activation(out=S0_, in_=ce[:, 0, :], func=AF.Sin, scale=TWO_PI, bias=-PI))
                ac(lambda e: e.activation(out=T2_, in_=ce[:, 0, :], func=AF.Abs, scale=TWO_PI, bias=-PI))
                ac(lambda e: e.activation(out=C0, in_=T2_, func=AF.Sin, scale=-1.0, bias=PI / 2))
                dv(lambda e: e.scalar_tensor_tensor(out=ABR, in0=RHO, scalar=-1.0, in1=C0, op0=ALU.mult, op1=ALU.mult))
                dv(lambda e: e.scalar_tensor_tensor(out=ABI, in0=RHO, scalar=-1.0, in1=S0_, op0=ALU.mult, op1=ALU.mult))
                dv(lambda e: e.tensor_scalar(out=T0, in0=ABR, scalar1=-1.0, scalar2=None, op0=ALU.add))
                dv(lambda e: e.tensor_tensor(out=T1_, in0=LR, in1=LR, op=ALU.mult))
                dv(lambda e: e.tensor_tensor(out=T2_, in0=LI, in1=LI, op=ALU.mult))
                dv(lambda e: e.tensor_tensor(out=T1_, in0=T1_, in1=T2_, op=ALU.add))
                dv(lambda e: e.reciprocal(out=T1_, in_=T1_))
                dv(lambda e: e.tensor_tensor(out=FR, in0=T0, in1=LR, op=ALU.mult))
                dv(lambda e: e.tensor_tensor(out=T2_, in0=ABI, in1=LI, op=ALU.mult))
                dv(lambda e: e.tensor_tensor(out=FR, in0=FR, in1=T2_, op=ALU.add))
                dv(lambda e: e.tensor_tensor(out=FR, in0=FR, in1=T1_, op=ALU.mult))
                dv(lambda e: e.tensor_tensor(out=FI, in0=ABI, in1=LR, op=ALU.mult))
                dv(lambda e: e.tensor_tensor(out=T2_, in0=T0, in1=LI, op=ALU.mult))
                dv(lambda e: e.tensor_tensor(out=FI, in0=FI, in1=T2_, op=ALU.subtract))
                dv(lambda e: e.tensor_tensor(out=FI, in0=FI, in1=T1_, op=ALU.mult))
                dv(lambda e: e.tensor_tensor(out=T0, in0=FR, in1=FR, op=ALU.mult))
                dv(lambda e: e.tensor_tensor(out=T2_, in0=FI, in1=FI, op=ALU.mult))
                dv(lambda e: e.tensor_tensor(out=T0, in0=T0, in1=T2_, op=ALU.add))
                dv(lambda e: e.reciprocal(out=T0, in_=T0))
                dv(lambda e: e.tensor_tensor(out=FIR, in0=FR, in1=T0, op=ALU.mult))
                dv(lambda e: e.scalar_tensor_tensor(out=FII, in0=FI, scalar=-1.0, in1=T0, op0=ALU.mult, op1=ALU.mult))
                stc = ExitStack()
                Cst = [sb("s5Cst%d" % i, [128, 2, 128], stack=stc) for i in range(2)]
                ctmp = sb("s5ctmp", [128, 2, 128], stack=stc)
                for cg in range(16):
                    b_ = cg % 2
                    P.dma("sp", Cst[b_][:, 0, :], s5_cbd_d[:, cg, :], s_in, writes=[dCst[b_]])
                    P.dma("sp", Cst[b_][:, 1, :], s5_cbd_d[:, 16 + cg, :], s_in, writes=[dCst[b_]])
                    fr, fi = pr[:, 9, cg:cg + 1], pr[:, 10, cg:cg + 1]
                    P.op("dve", lambda e, b_=b_, fi=fi: e.tensor_scalar(out=ctmp[:, 0, :], in0=Cst[b_][:, 1, :], scalar1=fi, scalar2=None, op0=ALU.mult),
                         reads=[dCst[b_], dpar], writes=[dCs])
                    P.op("dve", lambda e, b_=b_, fr=fr, cg=cg: e.scalar_tensor_tensor(out=Cp[:, cg, :], in0=Cst[b_][:, 0, :], scalar=fr, in1=ctmp[:, 0, :],
                                                                                 op0=ALU.mult, op1=ALU.subtract), reads=[dCst[b_], dpar, dCs], writes=[dC])
                    P.op("dve", lambda e, b_=b_, fr=fr: e.tensor_scalar(out=ctmp[:, 1, :], in0=Cst[b_][:, 1, :], scalar1=fr, scalar2=-1.0, op0=ALU.mult, op1=ALU.mult),
                         reads=[dCst[b_], dpar], writes=[dCs])
                    P.op("dve", lambda e, b_=b_, fi=fi, cg=cg: e.scalar_tensor_tensor(out=ctmp[:, 0, :], in0=Cst[b_][:, 0, :], scalar=fi, in1=ctmp[:, 1, :],
                                                                                 op0=ALU.mult, op1=ALU.subtract), reads=[dCst[b_], dpar, dCs], writes=[dCs])
                    P.op("dve", lambda e, cg=cg: e.tensor_scalar(out=Cp[:, 16 + cg, :], in0=ctmp[:, 0, :], scalar1=-1.0, scalar2=None, op0=ALU.mult),
                         reads=[dCs], writes=[dC])
                P.barrier()
                stc.close()
                ctab = sb("s5c", [128, L], stack=st)
                stab = sb("s5s", [128, L], stack=st)
                xr = sb("s5xr", [128, L], stack=st)
                xi = sb("s5xi", [128, L], stack=st)
                hre = sb("s5hre", [128, L], BF16, stack=st)
                him = sb("s5him", [128, L], BF16, stack=st)
                t1 = xr[:, 0:512]
                t2 = xr[:, 512:1024]
                lo_t = sb("s5lo", [128, 4, 64], stack=st)
                hi_t = sb("s5hi", [128, 4, 32], stack=st)
                lh_tmp = sb("s5lht", [128, 4, 32], stack=st)
                dlohi = Dep()

                def build_lohi(cg0):
                    def lv(fn):
                        P.op("dve", fn, reads=[dlohi, dpar], writes=[dlohi])
                    for tab, k0, nlev in ((lo_t, 0, 6), (hi_t, 6, 5)):
                        lv(lambda e, tab=tab: e.memset(tab[:, :, 0:1], 0.0))
                        for k in range(nlev):
                            n = 1 << k
                            inc = ce[:, k0 + k, cg0:cg0 + 4].unsqueeze(2).broadcast_to([128, 4, n])
                            lv(lambda e, tab=tab, n=n, inc=inc: e.tensor_tensor(out=tab[:, :, n:2 * n], in0=tab[:, :, 0:n], in1=inc, op=ALU.add))
                            lv(lambda e, tab=tab, n=n: e.tensor_scalar(out=lh_tmp[:, :, 0:n], in0=tab[:, :, n:2 * n], scalar1=1.0, scalar2=-1.0,
                                                                       op0=ALU.is_ge, op1=ALU.mult))
                            lv(lambda e, tab=tab, n=n: e.tensor_tensor(out=tab[:, :, n:2 * n], in0=tab[:, :, n:2 * n], in1=lh_tmp[:, :, 0:n], op=ALU.add))
                dt1 = dxr
                dt2 = dxr
                for fo in range(4):
                    proj_fm(W, fo * 128, 128, hT, dhT, dhTs,
                            lambda th, pap, dp, fo=fo: acopy(uT[:, fo, th * 1024:(th + 1) * 1024], pap, [dp], [du]),
                            lambda pap, dp, fo=fo: acopy(uT[:, fo, L:TW], pap, [dp], [dus]))
                def rs_mm(e):
                    ins = None
                    for ri in range(2):
                        for cg_ in range(16):
                            ins = e.matmul(pp2[ri][:, cg_ * NS:(cg_ + 1) * NS], lhsT=Bb[:, ri * 16 + cg_, :], rhs=uT[:, cg_ // 4, L:TW], start=True, stop=True)
                    return ins
                P.op("pe", rs_mm, reads=[dB, dus], writes=[dpp[0], dpp[1]])
                RR = pp2[0][:, 0:16 * NS].rearrange("p (c n) -> p c n", n=NS)
                RI = pp2[1][:, 0:16 * NS].rearrange("p (c n) -> p c n", n=NS)
                tq = lambda i: xr[:, i * 256:(i + 1) * 256].rearrange("p (c n) -> p c n", n=NS)
                OS = xi[:, 0:512].rearrange("p (r c n) -> p r c n", r=2, n=NS)
                SMB = hre[:, 0:512].rearrange("p (r c n) -> p r c n", r=2, n=NS)
                bcp = lambda i: pr[:, i, :].unsqueeze(2).broadcast_to([128, 16, NS])

                def tt_(out, in0, in1, op, reads, writes):
                    P.op("dve", lambda e: e.tensor_tensor(out=out, in0=in0, in1=in1, op=op), reads=reads, writes=writes)
                rd = [dxr, dpar, ds0]
                tt_(tq(0), RI, bcp(10), ALU.mult, rd + [dpp[1]], [dxr])
                tt_(tq(1), RR, bcp(9), ALU.mult, rd + [dpp[0]], [dxr])
                tt_(tq(1), tq(1), tq(0), ALU.subtract, rd, [dxr])
                tt_(tq(0), RR, bcp(10), ALU.mult, rd + [dpp[0]], [dxr])
                tt_(tq(2), RI, bcp(9), ALU.mult, rd + [dpp[1]], [dxr])
                tt_(tq(2), tq(2), tq(0), ALU.add, rd, [dxr])
                tt_(tq(0), s0[:, 0, :, :], bcp(7), ALU.mult, rd, [dxr])
                tt_(tq(1), tq(1), tq(0), ALU.add, rd, [dxr])
                tt_(tq(0), s0[:, 1, :, :], bcp(8), ALU.mult, rd, [dxr])
                tt_(OS[:, 0], tq(1), tq(0), ALU.subtract, rd + [dxi], [dxi])
                tt_(tq(0), s0[:, 1, :, :], bcp(7), ALU.mult, rd, [dxr])
                tt_(tq(2), tq(2), tq(0), ALU.add, rd, [dxr])
                tt_(tq(0), s0[:, 0, :, :], bcp(8), ALU.mult, rd, [dxr])
                tt_(OS[:, 1], tq(2), tq(0), ALU.add, rd + [dxi], [dxi])
                P.dma("sp", o_s5_s[:, :, :, :], OS, s_out, reads=[dxi])
                tt_(tq(0), OS[:, 1], bcp(12), ALU.mult, rd + [dxi], [dxr])
                tt_(tq(1), OS[:, 0], bcp(11), ALU.mult, rd + [dxi], [dxr])
                tt_(SMB[:, 0], tq(1), tq(0), ALU.subtract, rd + [dh], [dh])
                tt_(tq(0), OS[:, 0], bcp(12), ALU.mult, rd + [dxi], [dxr])
                tt_(tq(1), OS[:, 1], bcp(11), ALU.mult, rd + [dxi], [dxr])
                tt_(SMB[:, 1], tq(1), tq(0), ALU.add, rd + [dh], [dh])

                def ys_mm(e):
                    ins = None
                    for fo_ in range(4):
                        out = pp2[0][:, 512 + fo_ * NS:512 + (fo_ + 1) * NS]
                        for k_ in range(4):
                            cg_ = fo_ * 4 + k_
                            e.matmul(out, lhsT=Cp[:, cg_, :], rhs=SMB[:, 0, cg_, :], start=(k_ == 0), stop=False)
                            ins = e.matmul(out, lhsT=Cp[:, 16 + cg_, :], rhs=SMB[:, 1, cg_, :], start=False, stop=(k_ == 3))
                    return ins
                P.op("pe", ys_mm, reads=[dC, dh], writes=[dpp[0]])
                P.op("dve", lambda e: e.tensor_copy(out=ysa[:], in_=pp2[0][:, 512:512 + 4 * NS].rearrange("p (f n) -> p f n", n=NS)), reads=[dpp[0]], writes=[dysa])
                for cg in range(16):
                    fo = cg // 4
                    first, last = (cg % 4 == 0), (cg % 4 == 3)
                    sc = lambda i: pr[:, i, cg:cg + 1]
                    if cg % 4 == 0:
                        build_lohi(cg)
                    cl = cg % 4
                    xr3 = xr[:].rearrange("p (a b) -> p a b", b=64)
                    P.op("dve", lambda e: e.tensor_tensor(out=xr3, in0=lo_t[:, cl, :].unsqueeze(1).broadcast_to([128, 32, 64]),
                                                          in1=hi_t[:, cl, :].unsqueeze(2).broadcast_to([128, 32, 64]), op=ALU.add),
                         reads=[dxr, dh, dlohi], writes=[dxr])
                    P.op("dve", lambda e: e.scalar_tensor_tensor(out=xr[:], in0=xr[:], scalar=1.0, in1=xr[:], op0=ALU.is_ge, op1=ALU.subtract),
                         reads=[dxr], writes=[dxr])
                    P.op("act", lambda e: e.activation(out=stab[:], in_=xr[:], func=AF.Sin, scale=-TWO_PI, bias=-PI), reads=[dxr, dh], writes=[dtab])
                    P.op("act", lambda e: e.activation(out=xi[:], in_=xr[:], func=AF.Abs, scale=-TWO_PI, bias=-PI), reads=[dxr, dxi, dh], writes=[dxi])
                    P.op("act", lambda e: e.activation(out=ctab[:], in_=xi[:], func=AF.Sin, scale=-1.0, bias=PI / 2), reads=[dxi], writes=[dtab])
                    for th in range(2):
                        sl = slice(th * 1024, (th + 1) * 1024)
                        for ri in range(2):
                            def rmm(e, ri=ri, th=th):
                                ins = None
                                for tb in range(2):
                                    ins = e.matmul(pp2[ri][:, tb * 512:(tb + 1) * 512], lhsT=Bb[:, ri * 16 + cg, :],
                                                   rhs=uT[:, fo, th * 1024 + tb * 512: th * 1024 + (tb + 1) * 512], start=True, stop=True)
                                return ins
                            P.op("pe", rmm, reads=[dB, du], writes=[dpp[ri]])
                        P0, P1 = pp2[0][:, :], pp2[1][:, :]
                        P.op("dve", lambda e, sl=sl: e.tensor_tensor(out=xr[:, sl], in0=P0, in1=ctab[:, sl], op=ALU.mult), reads=[dpp[0], dtab, dxr], writes=[dxr])
                        P.op("dve", lambda e, sl=sl: e.tensor_tensor(out=xi[:, sl], in0=P1, in1=ctab[:, sl], op=ALU.mult), reads=[dpp[1], dtab, dh], writes=[dxi])
                        P.op("dve", lambda e, sl=sl: e.tensor_tensor(out=P0, in0=P0, in1=stab[:, sl], op=ALU.mult), reads=[dtab, dxr], writes=[dpp[0]])
                        P.op("dve", lambda e, sl=sl: e.tensor_tensor(out=P1, in0=P1, in1=stab[:, sl], op=ALU.mult), reads=[dtab, dxi], writes=[dpp[1]])
                        P.op("dve", lambda e, sl=sl: e.tensor_tensor(out=xr[:, sl], in0=P1, in1=xr[:, sl], op=ALU.add), reads=[dpp[1]], writes=[dxr])
                        P.op("dve", lambda e, sl=sl: e.tensor_tensor(out=xi[:, sl], in0=xi[:, sl], in1=P0, op=ALU.subtract), reads=[dpp[0]], writes=[dxi])
                    rho_b = pr[:, 3, cg:cg + 1].broadcast_to([128, L])
                    P.op("dve", lambda e: e.tensor_tensor_scan(out=xr[:], data0=rho_b, data1=xr[:], initial=0.0, op0=ALU.mult, op1=ALU.add), reads=[dpar], writes=[dxr])
                    P.op("dve", lambda e: e.tensor_tensor_scan(out=xi[:], data0=rho_b, data1=xi[:], initial=0.0, op0=ALU.mult, op1=ALU.add), reads=[dpar], writes=[dxi])
                    for th in range(2):
                        sl = slice(th * 1024, (th + 1) * 1024)
                        P0, P1 = pp2[0][:, :], pp2[1][:, :]
                        P.op("dve", lambda e, sl=sl: e.tensor_tensor(out=P0, in0=xr[:, sl], in1=ctab[:, sl], op=ALU.mult), reads=[dxr, dtab], writes=[dpp[0]])
                        P.op("dve", lambda e, sl=sl: e.tensor_tensor(out=P1, in0=xi[:, sl], in1=ctab[:, sl], op=ALU.mult), reads=[dxi, dtab], writes=[dpp[1]])
                        P.op("dve", lambda e, sl=sl: e.tensor_tensor(out=xr[:, sl], in0=xr[:, sl], in1=stab[:, sl], op=ALU.mult), reads=[dtab, dpp[0]], writes=[dxr])
                        P.op("dve", lambda e, sl=sl: e.tensor_tensor(out=xi[:, sl], in0=xi[:, sl], in1=stab[:, sl], op=ALU.mult), reads=[dtab, dpp[1]], writes=[dxi])
                        if th == 1:
                            P.op("dve", lambda e: e.tensor_tensor(out=hl[:, 0, cg:cg + 1], in0=pp2[0][:, 1023:1024], in1=xi[:, L - 1:L], op=ALU.subtract),
                                 reads=[dpp[0], dxi], writes=[dhl])
                            P.op("dve", lambda e: e.tensor_tensor(out=hl[:, 1, cg:cg + 1], in0=pp2[1][:, 1023:1024], in1=xr[:, L - 1:L], op=ALU.add),
                                 reads=[dpp[1], dxr], writes=[dhl])
                        P.op("dve", lambda e, sl=sl: e.tensor_tensor(out=hre[:, sl], in0=P0, in1=xi[:, sl], op=ALU.subtract), reads=[dpp[0], dxi], writes=[dh])
                        P.op("dve", lambda e, sl=sl: e.tensor_tensor(out=him[:, sl], in0=P1, in1=xr[:, sl], op=ALU.add), reads=[dpp[1], dxr], writes=[dh])
                    for tb in range(4):
                        def ymm(e, tb=tb):
                            e.matmul(ps[4 + tb][:, :], lhsT=Cp[:, cg, :], rhs=hre[:, tb * 512:(tb + 1) * 512], start=first, stop=False)
                            return e.matmul(ps[4 + tb][:, :], lhsT=Cp[:, 16 + cg, :], rhs=him[:, tb * 512:(tb + 1) * 512], start=False, stop=last)
                        P.op("pe", ymm, reads=[dC, dh], writes=[dps[4 + tb]])
                    if last:
                        dtb = [Dep() for _ in range(4)]
                        T1 = [xr[:, tb * 512:(tb + 1) * 512] for tb in range(4)]
                        T2 = [xi[:, tb * 512:(tb + 1) * 512] for tb in range(4)]
                        SL = [slice(tb * 512, (tb + 1) * 512) for tb in range(4)]
                        for tb in range(4):
                            P.op("dve", lambda e, tb=tb: e.scalar_tensor_tensor(out=T1[tb], in0=uT[:, fo, SL[tb]], scalar=dTt[:, 0, fo:fo + 1], in1=ps[4 + tb][:, :],
                                                                             op0=ALU.mult, op1=ALU.add), reads=[du, dpar, dps[4 + tb], dh], writes=[dtb[tb], dxr, dxi])
                        for tb in range(4):
                            P.op("act", lambda e, tb=tb: e.activation(out=T2[tb], in_=T1[tb], func=AF.Square), reads=[dtb[tb]], writes=[dtb[tb]])
                        for tb in range(4):
                            P.op("dve", lambda e, tb=tb: e.tensor_scalar(out=T2[tb], in0=T2[tb], scalar1=0.044715, scalar2=1.0, op0=ALU.mult, op1=ALU.add),
                                 reads=[dtb[tb]], writes=[dtb[tb]])
                        for tb in range(4):
                            P.op("dve", lambda e, tb=tb: e.tensor_tensor(out=T2[tb], in0=T1[tb], in1=T2[tb], op=ALU.mult), reads=[dtb[tb]], writes=[dtb[tb]])
                        for tb in range(4):
                            P.op("act", lambda e, tb=tb: e.activation(out=T2[tb], in_=T2[tb], func=AF.Sigmoid, scale=GELU_C), reads=[dtb[tb]], writes=[dtb[tb]])
                        for tb in range(4):
                            P.op("dve", lambda e, tb=tb: e.tensor_tensor(out=ygb[:, fo, SL[tb]], in0=T1[tb], in1=T2[tb], op=ALU.mult),
                                 reads=[dtb[tb], dxr, dxi], writes=[dyg])
                        P.op("dve", lambda e: e.scalar_tensor_tensor(out=t1[:, 0:NS], in0=uT[:, fo, L:TW], scalar=dTt[:, 0, fo:fo + 1], in1=ysa[:, fo, :],
                                                                     op0=ALU.mult, op1=ALU.add), reads=[dus, dpar, dysa, dt1], writes=[dt1])
                        gelu_from(t1[:, 0:NS], dt1, ygb[:, fo, L:TW], dygs, t2[:, 0:NS], dt2, NS)
                P.op("dve", lambda e: e.tensor_tensor(out=pr[:, 16, :], in0=hl[:, 1, :], in1=FI, op=ALU.mult), reads=[dhl, dpar], writes=[dpar])
                P.op("dve", lambda e: e.tensor_tensor(out=pr[:, 17, :], in0=hl[:, 0, :], in1=FR, op=ALU.mult), reads=[dhl, dpar], writes=[dpar])
                P.op("dve", lambda e: e.tensor_tensor(out=stg[:, 16:32], in0=pr[:, 17, :], in1=pr[:, 16, :], op=ALU.subtract), reads=[dpar], writes=[dstg])
                P.op("dve", lambda e: e.tensor_tensor(out=pr[:, 16, :], in0=hl[:, 0, :], in1=FI, op=ALU.mult), reads=[dhl, dpar], writes=[dpar])
                P.op("dve", lambda e: e.tensor_tensor(out=pr[:, 17, :], in0=hl[:, 1, :], in1=FR, op=ALU.mult), reads=[dhl, dpar], writes=[dpar])
                P.op("dve", lambda e: e.tensor_tensor(out=stg[:, 32:48], in0=pr[:, 17, :], in1=pr[:, 16, :], op=ALU.add), reads=[dpar], writes=[dstg])
                P.dma("sp", o_s5_p.rearrange("p a b -> p (a b)"), stg[:, 16:48], s_out, reads=[dstg])
                P.barrier()
                for fo2 in range(4):
                    def cons_g(th, pap, dp, fo2=fo2):
                        sl = slice(th * 1024, (th + 1) * 1024)
                        P.op("act", lambda e: e.activation(out=xr[:, sl], in_=pap, func=AF.Sigmoid, bias=dTt[:, 1, fo2:fo2 + 1]), reads=[dp, dpar], writes=[dxr])
                        P.op("dve", lambda e: e.tensor_tensor(out=yoth[:, fo2, sl], in0=xr[:, sl], in1=ygb[:, fo2, sl], op=ALU.mult), reads=[dxr, dyg], writes=[dyoth])

                    def cons_g_s(pap, dp, fo2=fo2):
                        P.op("act", lambda e: e.activation(out=sms[:, 0, :], in_=pap, func=AF.Sigmoid, bias=dTt[:, 1, fo2:fo2 + 1]), reads=[dp, dpar], writes=[dsm])
                        P.op("dve", lambda e: e.tensor_tensor(out=yoth[:, fo2, L:TW], in0=sms[:, 0, :], in1=ygb[:, fo2, L:TW], op=ALU.mult), reads=[dsm, dygs], writes=[dyoths])
                    proj_fm(w_glu, fo2 * 128, 128, ygb, dyg, dygs, cons_g, cons_g_s, nk=4)
                P.barrier()

        def hgrn_mixer(l, yh01, dyh01, yhs, dyhs):
            W = od_w_in
            with ExitStack() as st:
                lbt = sb("hg_lbt", [128, 2, 4], stack=st)
                lbv = sb("hg_lbv", [128, 2, 4], stack=st)
                q_s = sb("hq_s", [128, 4, NS], stack=st)
                a_s = sb("ha_s", [128, 4, NS], stack=st)
                kks = sb("hkks", [128, 4, NS], stack=st)
                k_tok = sb("hk_tok", [NS, 512], BF16, stack=st)
                v_tok = sb("hv_tok", [NS, 512], BF16, stack=st)
                gg_s = sb("hgg_s", [NS, 512], stack=st)
                gnB = sb("hgnB", [128, 512], stack=st)
                dpar, dsm, dgs, dSst = Dep(), Dep(), Dep(), Dep()
                P.dma("sp", lbt[:], hg_lbT_d[:, :, :], s_in, writes=[dpar])
                P.dma("sp", gnB[:], hg_ng_d.partition_broadcast(128), s_in, writes=[dgs])
                P.op("dve", lambda e: e.tensor_tensor(out=lbv[:, 0, :], in0=lbt[:, 1, :], in1=lbt[:, 0, :], op=ALU.subtract), reads=[dpar], writes=[dpar])
                P.op("act", lambda e: e.activation(out=lbv[:, 0, :], in_=lbv[:, 0, :], func=AF.Sigmoid), reads=[dpar], writes=[dpar])
                P.op("dve", lambda e: e.tensor_scalar(out=lbv[:, 1, :], in0=lbv[:, 0, :], scalar1=-1.0, scalar2=1.0, op0=ALU.mult, op1=ALU.add), reads=[dpar], writes=[dpar])
                for pss in range(2):
                    with ExitStack() as sp_:
                        qt = sb("hqt", [128, 2, L], BF16, stack=sp_)
                        kt = sb("hkt", [128, 2, L], BF16, stack=sp_)
                        kd = sb("hkd", [128, NT, 256], BF16, stack=sp_)
                        elT = sb("helT", [128, 2, 32], stack=sp_)
                        S = sb("hS", [128, 2, 128], stack=sp_)
                        Sb = sb("hSb", [128, 2, 128], BF16, stack=sp_)
                        dprep = Dep()
                        with ExitStack() as st2:
                            A = sb("hA", [128, L], stack=st2)
                            C = sb("hC", [128, L], stack=st2)
                            kdT = sb("hkdT", [128, 1024], BF16, stack=st2)
                            rmask = sb("hrmask", [128, 1024], stack=st2)
                            dA, dC, dkdT, drm = Dep(), Dep(), Dep(), Dep()
                            P.op("pool", lambda e: e.memset(rmask[:], 1.0), writes=[drm])
                            P.op("pool", lambda e: e.affine_select(out=rmask[:].rearrange("p (c j) -> p c j", j=64), in_=rmask[:].rearrange("p (c j) -> p c j", j=64),
                                                                    pattern=[[0, 16], [1, 64]], compare_op=ALU.is_gt, fill=0.0, base=0, channel_multiplier=0),
                                 reads=[drm], writes=[drm])
                            for hl in range(2):
                                h = pss * 2 + hl
                                lb_, oml_ = lbv[:, 0, h:h + 1], lbv[:, 1, h:h + 1]

                                def cons_f(th, pap, dp):
                                    sl = slice(th * 1024, (th + 1) * 1024)
                                    P.op("act", lambda e: e.activation(out=A[:, sl], in_=pap, func=AF.Sigmoid), reads=[dp, dprep, dkdT], writes=[dA])
                                    P.op("dve", lambda e: e.tensor_scalar(out=A[:, sl], in0=A[:, sl], scalar1=oml_, scalar2=lb_, op0=ALU.mult, op1=ALU.add),
                                         reads=[dpar], writes=[dA])

                                def cons_f_s(pap, dp):
                                    P.op("act", lambda e: e.activation(out=a_s[:, h, :], in_=pap, func=AF.Sigmoid), reads=[dp], writes=[dsm])
                                    P.op("dve", lambda e: e.tensor_scalar(out=a_s[:, h, :], in0=a_s[:, h, :], scalar1=oml_, scalar2=lb_, op0=ALU.mult, op1=ALU.add),
                                         reads=[dpar, dsm], writes=[dsm])
                                    P.op("dve", lambda e: e.tensor_scalar(out=kks[:, h, :], in0=a_s[:, h, :], scalar1=-1.0, scalar2=1.0, op0=ALU.mult, op1=ALU.add),
                                         reads=[dsm], writes=[dsm])
                                proj_fm(W, 1024 + h * 128, 128, hT, dhT, dhTs, cons_f, cons_f_s)
                                P.op("act", lambda e: e.activation(out=C[:], in_=A[:], func=AF.Ln), reads=[dA, dprep, dkdT], writes=[dC])
                                P.op("dve", lambda e: e.tensor_scalar(out=A[:], in0=A[:], scalar1=-1.0, scalar2=1.0, op0=ALU.mult, op1=ALU.add), reads=[dC], writes=[dA])
                                for th in range(2):
                                    sl = slice(th * 1024, (th + 1) * 1024)
                                    P.op("dve", lambda e, sl=sl: e.tensor_tensor_scan(out=C[:, sl], data0=rmask[:], data1=C[:, sl], initial=0.0, op0=ALU.mult, op1=ALU.add),
                                         reads=[drm], writes=[dC])
                                P.op("act", lambda e, hl=hl: e.activation(out=elT[:, hl, :], in_=C[:].rearrange("p (c j) -> p c j", j=64)[:, :, 63], func=AF.Exp),
                                     reads=[dC], writes=[dprep])
                                P.op("act", lambda e: e.activation(out=C[:], in_=C[:], func=AF.Exp), reads=[dprep], writes=[dC])

                                def cons_q(th, pap, dp, hl=hl):
                                    sl = slice(th * 1024, (th + 1) * 1024)
                                    P.op("act", lambda e: e.activation(out=pap, in_=pap, func=AF.Silu), reads=[], writes=[dp])
                                    P.op("dve", lambda e: e.tensor_tensor(out=qt[:, hl, sl], in0=pap, in1=C[:, sl], op=ALU.mult), reads=[dp, dC], writes=[dprep])

                                def cons_q_s(pap, dp):
                                    P.op("act", lambda e: e.activation(out=q_s[:, h, :], in_=pap, func=AF.Silu), reads=[dp], writes=[dsm])
                                proj_fm(W, 512 + h * 128, 128, hT, dhT, dhTs, cons_q, cons_q_s)
                                P.op("dve", lambda e: e.reciprocal(out=C[:], in_=C[:]), reads=[dprep], writes=[dC])
                                P.op("dve", lambda e, hl=hl: e.tensor_tensor(out=kt[:, hl, :], in0=A[:], in1=C[:], op=ALU.mult), reads=[dA, dC], writes=[dprep])
                                for th in range(2):
                                    sl = slice(th * 1024, (th + 1) * 1024)
                                    P.op("dve", lambda e, sl=sl, th=th, hl=hl: e.tensor_tensor(
                                        out=C[:, sl].rearrange("p (c j) -> p c j", j=64), in0=C[:, sl].rearrange("p (c j) -> p c j", j=64),
                                        in1=elT[:, hl, th * 16:(th + 1) * 16].unsqueeze(2).broadcast_to([128, 16, 64]), op=ALU.mult), reads=[dprep], writes=[dC])
                                    P.op("dve", lambda e, sl=sl: e.tensor_tensor(out=kdT[:], in0=A[:, sl], in1=C[:, sl], op=ALU.mult), reads=[dA, dC, dprep], writes=[dkdT])
                                    for tb in range(2):
                                        pb = nxt67()

                                        def tr(e, tb=tb, pb=pb):
                                            ins = None
                                            for q in range(4):
                                                ins = e.transpose(out=ps[pb][:].bitcast(BF16)[:, q * 128:(q + 1) * 128],
                                                                  in_=kdT[:, (tb * 4 + q) * 128:(tb * 4 + q + 1) * 128], identity=identb[:])
                                            return ins
                                        P.op("pe", tr, reads=[dkdT, dconst], writes=[dps[pb]])
                                        t_base = th * 8 + tb * 4
                                        P.op("act", lambda e, pb=pb, t_base=t_base, hl=hl: e.activation(
                                            out=kd[:, t_base:t_base + 4, hl * 128:(hl + 1) * 128],
                                            in_=ps[pb][:].bitcast(BF16)[:, 0:512].rearrange("p (q t) -> p q t", t=128), func=AF.Copy),
                                            reads=[dps[pb]], writes=[dprep])
                            P.barrier()
                        with ExitStack() as st3:
                            vtok = sb("hvtok", [128, NT, 256], BF16, stack=st3)
                            ggt = sb("hggt", [128, NT, 256], BF16, stack=st3)
                            dv_, dg_, dgt = Dep(), Dep(), Dep()
                            cs = slice(pss * 256, (pss + 1) * 256)
                            with ExitStack() as st4:
                                wv = sb("hwv", [128, 8, 256], BF16, stack=st4)
                                gtmp = sb("hgtmp", [128, 256], stack=st4)
                                dwv = Dep()
                                sv_ = P.dsem()
                                proj_tm(W, 1536 + pss * 256, wv, dwv, sv_,
                                        lambda tt, pap, dp: acopy(vtok[:, tt, :], pap, [dp], [dv_]),
                                        lambda pap, dp: acopy(v_tok[:, cs], pap, [dp], [dsm]), n=256)

                                def cons_gt(tt, pap, dp):
                                    P.op("act", lambda e: e.activation(out=gtmp[:], in_=pap, func=AF.Silu), reads=[dp], writes=[dgt])
                                    P.op("dve", lambda e: e.tensor_tensor(out=ggt[:, tt, :], in0=gtmp[:], in1=gnB[:, cs], op=ALU.mult), reads=[dgt, dgs], writes=[dg_])

                                def cons_gt_s(pap, dp):
                                    P.op("act", lambda e: e.activation(out=gtmp[0:NS, :], in_=pap, func=AF.Silu), reads=[dp], writes=[dgt])
                                    P.op("dve", lambda e: e.tensor_tensor(out=gg_s[:, cs], in0=gtmp[0:NS, :], in1=gnB[0:NS, cs], op=ALU.mult), reads=[dgt, dgs], writes=[dgs])
                                proj_tm(W, 2048 + pss * 256, wv, dwv, sv_, cons_gt, cons_gt_s, n=256)
                                P.barrier()
                            if pss == 0:
                                ydst = lambda tt: yh01[:, 0:2, tt * 128:(tt + 1) * 128]
                                dy = dyh01
                            else:
                                ydst = lambda tt: hT[:, 6:8, tt * 128:(tt + 1) * 128]
                                dy = dhT
                            gla_chunks(128, 2, lambda hl: qt[:, hl, :], kt, kd, vtok, ggt, elT, S, Sb, ydst, dy, [dprep, dv_, dg_],
                                       o_hg_p[:, pss * 2:(pss + 1) * 2, :])
                def ktr(e):
                    ins = None
                    for h in range(4):
                        ins = e.transpose(out=ps[6][0:NS, h * 128:(h + 1) * 128], in_=kks[:, h, :], identity=ident[:])
                    return ins
                P.op("pe", ktr, reads=[dsm, dconst], writes=[dps[6]])
                acopy(k_tok[:], ps[6][0:NS, :], [dps[6]], [dsm])
                with ExitStack() as st5:
                    Sst = sb("hSst", [128, NS, 4, 128], stack=st5)
                    P.dma("sp", Sst[:], st_hg_d[:, :, :, :], s_in, writes=[dSst])
                    sample_state_update(128, q_s[:], k_tok[:], v_tok[:], a_s, Sst, dSst, [dsm], st5, "o")
                    rms_gate_sample(gg_s[:], dgs, yhs[:], dyhs, st5, "o")
                    P.dma("sp", o_hg_s[:, :, :, :], Sst[:], s_out, reads=[dSst])
                    P.barrier()


        _DEAD[0] = False
        if True:
            stage(2)
            with ExitStack() as stl:
                yoth = sb("yoth", [128, 4, TW], BF16, stack=stl)
                dyoth, dyoths = Dep(), Dep()
                pa = ExitStack()
                mod_issue = setup_mod(pa)
                mod_issue(4)
                mod_issue.drain()
                make_hT(0, 1, 0)
                even_mixer(0, yoth, dyoth, dyoths, bg=mod_issue, bg_stack=pa)

                def ysrc0(fc):
                    if fc < 4:
                        return (lambda tt, fc=fc: hT[:, fc, tt * 128:(tt + 1) * 128], dhT, hT[:, fc, L:TW], dhTs)
                    return (lambda tt, fc=fc: yoth[:, fc - 4, tt * 128:(tt + 1) * 128], dyoth, yoth[:, fc - 4, L:TW], dyoths)
                stage(5)
                out_proj_ln(0, ev_w_out, ysrc0)
            stage(6)
            ffn(0)
            stage(7)
            make_hT(1, 1, 0)
            with ExitStack() as stl:
                yoth1 = sb("yoth1", [128, 4, TW], BF16, stack=stl)
                dyoth1, dyoths1 = Dep(), Dep()
                s5_mixer(1, yoth1, dyoth1, dyoths1)
                stage(8)
                yh01 = sb("yh01", [128, 2, L], BF16, stack=stl)
                yhs = sb("yhs", [128, 4, NS], BF16, stack=stl)
                dyh01, dyhs = Dep(), Dep()
                hgrn_mixer(1, yh01, dyh01, yhs, dyhs)

                def ysrc1(fc):
                    if fc < 4:
                        return (lambda tt, fc=fc: yoth1[:, fc, tt * 128:(tt + 1) * 128], dyoth1, yoth1[:, fc, L:TW], dyoths1)
                    if fc < 6:
                        return (lambda tt, fc=fc: yh01[:, fc - 4, tt * 128:(tt + 1) * 128], dyh01, yhs[:, fc - 4, :], dyhs)
                    return (lambda tt, fc=fc: hT[:, fc, tt * 128:(tt + 1) * 128], dhT, yhs[:, fc - 4, :], dyhs)
                stage(9)
                out_proj_ln(1, od_w_out, ysrc1)
            stage(10)
            ffn(1)
        _DEAD[0] = False
        P.barrier()
        ypv = y_p.rearrange("(n p) d -> p n d", p=128)
        for tt in range(NT):
            P.dma("sp", ypv[:, tt, :], X[:, tt, :], s_out, reads=[dX[tt]])
        P.dma("sp", y_sT[:, :, :], XsT[:], s_out, reads=[dXs])
        P.barrier()
    return nc


_CACHE = {}


def _fm(v, nch):
    return np.ascontiguousarray(np.asarray(v, np.float32).reshape(nch, 128).T)


def _prepare(inp):
    f32 = np.float32
    g = {k: np.asarray(v) for k, v in inp.items()}
    shared = {
        "w_ada": g["w_ada"], "ev_w_in": g["ev_w_in"][0], "ev_w_out": g["ev_w_out"][0],
        "od_w_in": g["od_w_in"][0], "od_w_out": g["od_w_out"][0], "ffn_w_in": g["ffn_w_in"], "ffn_w_out": g["ffn_w_out"],
        "w_glu": g["od_s5_w_glu"][0], "gla_w_lr": g["ev_gla_w_lr"][0],
    }
    shared["b_adaT"] = np.ascontiguousarray(g["b_ada"].reshape(2, 48, 128).transpose(2, 0, 1))
    shared["ln_gb"] = np.concatenate([g["ln_g"].reshape(-1), g["ln_b"].reshape(-1)]).reshape(1, 8 * D).astype(f32)
    shared["ln_gT"] = np.ascontiguousarray(g["ln_g"].reshape(4, 8, 128).transpose(2, 0, 1))
    shared["ln_bT"] = np.ascontiguousarray(g["ln_b"].reshape(4, 8, 128).transpose(2, 0, 1))
    shared["gla_b_lrT"] = _fm(g["ev_gla_b_lr"][0], 2)
    shared["gla_ng"] = np.tile(g["ev_gla_norm_g"][0], 4).reshape(1, 512).astype(f32)
    shared["hg_ng"] = np.tile(g["od_hg_norm_g"][0], 4).reshape(1, 512).astype(f32)
    shared["conv_wT"] = np.ascontiguousarray(g["ev_conv_w"][0].reshape(4, 4, 128).transpose(2, 1, 0))
    shared["lru_vecT"] = np.ascontiguousarray(np.stack([_fm(g["ev_conv_b"][0], 4), _fm(g["ev_lru_b_r"][0], 4),
                                                        _fm(g["ev_lru_b_i"][0], 4), _fm(g["ev_lru_lam"][0], 4)], axis=2))
    wbd = np.zeros((128, 8, 128), f32)
    for gi, wname in enumerate(("ev_lru_w_r", "ev_lru_w_i")):
        w = g[wname][0]
        for h in range(8):
            grp, hh = h // 2, h % 2
            wbd[hh * 64:(hh + 1) * 64, gi * 4 + grp, hh * 64:(hh + 1) * 64] = w[h]
    shared["lru_wbd"] = wbd
    def chT(a):
        return np.ascontiguousarray(np.asarray(a, f32).reshape(16, 2, 64).transpose(1, 2, 0).reshape(128, 16))
    shared["s5_vecT"] = np.ascontiguousarray(np.stack([chT(g["od_s5_lam_re"][0]), chT(g["od_s5_lam_im"][0]),
                                                       chT(np.repeat(g["od_s5_log_dt"][0][:, None], 64, axis=1))], axis=1))
    bbd = np.zeros((128, 32, 128), f32)
    cbd = np.zeros((128, 32, 128), f32)
    for ri, (bn, cn) in enumerate((("od_s5_b_re", "od_s5_c_re"), ("od_s5_b_im", "od_s5_c_im"))):
        bsrc, csrc = g[bn][0], g[cn][0]
        for gg_ in range(32):
            cg, two = gg_ // 2, gg_ % 2
            r0 = (gg_ % 8) * 16
            bbd[r0:r0 + 16, ri * 16 + cg, two * 64:(two + 1) * 64] = bsrc[gg_].T
            cbd[two * 64:(two + 1) * 64, ri * 16 + cg, r0:r0 + 16] = csrc[gg_].T
    shared["s5_bbd"] = bbd
    shared["s5_cbd"] = cbd
    shared["s5_dT"] = np.ascontiguousarray(np.stack([_fm(g["od_s5_d"][0], 4), _fm(g["od_s5_b_glu"][0], 4)], axis=1))
    shared["hg_lbT"] = np.ascontiguousarray(g["hg_lb_logits"].reshape(2, 4, 128).transpose(2, 0, 1))
    in_maps = []
    for b in range(NCORES):
        sl = slice(b * NS, (b + 1) * NS)
        m = dict(shared)
        m["xp"] = np.ascontiguousarray(g["x_prompt"][b])
        xs = g["x_sample"][sl, 0, :]
        m["xsT"] = np.ascontiguousarray(xs.T.reshape(8, 128, NS).transpose(1, 0, 2))
        c17 = np.concatenate([g["c_prompt"][b:b + 1], g["c_sample"][sl]], axis=0)
        m["cT"] = np.ascontiguousarray(c17.T.reshape(8, 128, 17).transpose(1, 0, 2))
        sg = g["state_gla"][0, sl]
        m["st_gla"] = np.ascontiguousarray(sg.reshape(NS, 2, 2, 64, 128).transpose(2, 3, 0, 1, 4).reshape(128, NS, 2, 128))
        sc = g["state_rglru_conv"][0, sl]
        m["st_conv"] = np.ascontiguousarray(sc.reshape(NS, 3, 4, 128).transpose(3, 2, 1, 0))
        m["st_lru"] = np.ascontiguousarray(g["state_rglru_h"][0, sl].reshape(NS, 4, 128).transpose(2, 1, 0))
        s5 = np.stack([g["state_s5_re"][0, sl], g["state_s5_im"][0, sl]], axis=0)
        m["st_s5"] = np.ascontiguousarray(s5.reshape(2, NS, 16, 2, 64).transpose(3, 4, 0, 2, 1).reshape(128, 2, 16, NS))
        m["st_hg"] = np.ascontiguousarray(g["state_hgrn"][0, sl].transpose(2, 0, 1, 3))
        in_maps.append({k: np.ascontiguousarray(v, dtype=f32) for k, v in m.items()})
    return in_maps


def kernel(**inp):
    if "nc" not in _CACHE:
        _CACHE["nc"] = build_program()
    nc = _CACHE["nc"]
    in_maps = _prepare(inp)
    res = run_bass_kernel_spmd(nc, in_maps, core_ids=list(range(NCORES)))
    return _assemble(res.results)


def _assemble(R):
    f32 = np.float32
    y_p = np.stack([R[b]["y_p"] for b in range(NCORES)], axis=0)
    y_s = np.concatenate([R[b]["y_sT"].transpose(2, 1, 0).reshape(NS, 1, D) for b in range(NCORES)], axis=0)
    gla_p = np.stack([R[b]["o_gla_p"].reshape(2, 64, 2, 128).transpose(2, 0, 1, 3).reshape(4, 64, 128) for b in range(NCORES)], axis=0)[None]
    gla_s = np.concatenate([R[b]["o_gla_s"].reshape(2, 64, NS, 2, 128).transpose(2, 3, 0, 1, 4).reshape(NS, 4, 64, 128)
                            for b in range(NCORES)], axis=0)[None]
    conv_p = np.stack([R[b]["o_conv_p"].transpose(2, 1, 0).reshape(3, 512) for b in range(NCORES)], axis=0)[None]
    conv_s = np.concatenate([R[b]["o_conv_s"].transpose(3, 2, 1, 0).reshape(NS, 3, 512) for b in range(NCORES)], axis=0)[None]
    lru_p = np.stack([R[b]["o_lru_p"].T.reshape(512) for b in range(NCORES)], axis=0)[None]
    lru_s = np.concatenate([R[b]["o_lru_s"].transpose(2, 1, 0).reshape(NS, 512) for b in range(NCORES)], axis=0)[None]

    def s5p(b, i):
        return R[b]["o_s5_p"][:, i, :].reshape(2, 64, 16).transpose(2, 0, 1).reshape(32, 64)

    def s5s(b, i):
        return R[b]["o_s5_s"][:, i].reshape(2, 64, 16, NS).transpose(3, 2, 0, 1).reshape(NS, 32, 64)
    re_p = np.stack([s5p(b, 0) for b in range(NCORES)], axis=0)[None]
    im_p = np.stack([s5p(b, 1) for b in range(NCORES)], axis=0)[None]
    re_s = np.concatenate([s5s(b, 0) for b in range(NCORES)], axis=0)[None]
    im_s = np.concatenate([s5s(b, 1) for b in range(NCORES)], axis=0)[None]
    hg_p = np.stack([R[b]["o_hg_p"].transpose(1, 0, 2) for b in range(NCORES)], axis=0)[None]
    hg_s = np.concatenate([R[b]["o_hg_s"].transpose(1, 2, 0, 3) for b in range(NCORES)], axis=0)[None]
    outs = (y_p, y_s, gla_p, gla_s, conv_p, conv_s, lru_p, lru_s, re_p, re_s, im_p, im_s, hg_p, hg_s)
    return tuple(np.ascontiguousarray(o, dtype=f32) for o in outs)
```

```python
import numpy as np
import concourse.bass as bass
import concourse.mybir as mybir
from concourse.bass_utils import run_bass_kernel_spmd
from contextlib import ExitStack

F32 = mybir.dt.float32
BF16 = mybir.dt.bfloat16
ALU = mybir.AluOpType
AF = mybir.ActivationFunctionType
AX = mybir.AxisListType

NCORES = 8
D = 1024
L = 2048
NT = 16
NS = 16
TW = L + NS
DFF = 2816
ALPHA = 4.0 ** 0.25
LN_EPS = 1e-5
RMS_EPS = 1e-6
GELU_C = 1.5957691216057308


class StopBuild(Exception):
    pass


import os as _os
KSTOP = float(_os.environ.get("KSTOP", "99"))


_DEAD = [False]


def stage(n):
    if n > KSTOP:
        _DEAD[0] = True


class Dep:
    __slots__ = ("w", "re", "rd")

    def __init__(self):
        self.w = None
        self.re = {}
        self.rd = []


class DSem:
    def __init__(self, sem, i):
        self.sem = sem
        self.cnt = 0
        self.id = i


class Prog:
    def __init__(self, nc, es):
        self.nc = nc
        self.es = es
        self.engs = {"pe": nc.tensor, "act": nc.scalar, "dve": nc.vector, "pool": nc.gpsimd, "sp": nc.sync}
        self.sem = {k: es.enter_context(nc.semaphore("sem_" + k)) for k in self.engs}
        self.cnt = {k: 0 for k in self.engs}
        self.known = {k: {} for k in self.engs}
        self.dsems = []

    def dsem(self):
        s = DSem(self.es.enter_context(self.nc.semaphore("dsem%d" % len(self.dsems))), len(self.dsems))
        self.dsems.append(s)
        return s

    def _wait(self, e, tok):
        if tok[0] == "e":
            _, src, val = tok
            if self.known[e].get(src, 0) >= val:
                return
            self.known[e][src] = val
            self.engs[e].wait_ge(self.sem[src], val)
        else:
            ds = tok[1]
            val = ds.cnt
            key = ("d", ds.id)
            if self.known[e].get(key, 0) >= val:
                return
            self.known[e][key] = val
            self.engs[e].wait_ge(ds.sem, val)

    def _deps(self, e, reads, writes, dma_group=False):
        for d in reads:
            for t in self._wl(d):
                self._wait(e, t)
        for d in writes:
            if dma_group and d.w is not None and all(t[0] == "d" for t in self._wl(d)) and not d.re and not d.rd:
                continue
            for t in self._wl(d):
                self._wait(e, t)
            for src, val in d.re.items():
                self._wait(e, ("e", src, val))
            for t in d.rd:
                self._wait(e, t)

    @staticmethod
    def _wl(d):
        if d.w is None:
            return []
        return d.w if isinstance(d.w, list) else [d.w]

    def _done(self, tok, reads, writes, dma_group=False):
        for d in reads:
            if tok[0] == "e":
                d.re[tok[1]] = tok[2]
            else:
                d.rd = [t for t in d.rd if t[1] is not tok[1]] + [tok]
        for d in writes:
            if dma_group and d.w is not None and all(t[0] == "d" for t in self._wl(d)) and not d.re and not d.rd:
                d.w = [t for t in self._wl(d) if t[1] is not tok[1]] + [tok]
                continue
            d.w = tok
            d.re = {}
            d.rd = []

    def op(self, e, fn, reads=(), writes=()):
        if _DEAD[0]:
            return
        self._deps(e, reads, writes)
        ins = fn(self.engs[e])
        self.cnt[e] += 1
        ins.then_inc(self.sem[e], 1)
        self._done(("e", e, self.cnt[e]), reads, writes)

    def dma(self, q, out, in_, ds, reads=(), writes=()):
        if _DEAD[0]:
            return
        if ds is None or ds == "in" or ds == "out":
            if not hasattr(self, "pool_sems"):
                self.pool_sems = [self.dsem() for _ in range(24)]
                self.pool_i = 0
            ds = self.pool_sems[self.pool_i % len(self.pool_sems)]
            self.pool_i += 1
            if ds.cnt > 0:
                self._wait(q, ("d", ds))
        self._deps(q, reads, writes, dma_group=True)
        ins = self.engs[q].dma_start(out=out, in_=in_)
        ds.cnt += 16
        ins.then_inc(ds.sem, 16)
        self._done(("d", ds), reads, writes, dma_group=True)

    def barrier(self):
        for e in self.engs:
            for src in self.engs:
                if src != e and self.cnt[src] > 0:
                    self._wait(e, ("e", src, self.cnt[src]))
            for ds in self.dsems:
                if ds.cnt > 0:
                    self._wait(e, ("d", ds))


def build_program():
    nc = bass.Bass("TRN2", target_bir_lowering=False)

    def din(name, shape):
        return nc.dram_tensor(name, list(shape), F32, kind="ExternalInput").ap()

    def dout(name, shape):
        return nc.dram_tensor(name, list(shape), F32, kind="ExternalOutput").ap()

    xp = din("xp", [L, D])
    xsT_d = din("xsT", [128, 8, NS])
    cT_d = din("cT", [128, 8, 17])
    w_ada = din("w_ada", [2, D, 6 * D])
    b_adaT_d = din("b_adaT", [128, 2, 48])
    ln_gb_d = din("ln_gb", [1, 8 * D])
    ln_gT_d = din("ln_gT", [128, 4, 8])
    ln_bT_d = din("ln_bT", [128, 4, 8])
    ev_w_in = din("ev_w_in", [D, 2576])
    ev_w_out = din("ev_w_out", [D, D])
    od_w_in = din("od_w_in", [D, 2560])
    od_w_out = din("od_w_out", [D, D])
    ffn_w_in = din("ffn_w_in", [2, D, 2 * DFF])
    ffn_w_out = din("ffn_w_out", [2, DFF, D])
    w_glu = din("w_glu", [512, 512])
    gla_w_lr_d = din("gla_w_lr", [16, 256])
    gla_b_lrT_d = din("gla_b_lrT", [128, 2])
    gla_ng_d = din("gla_ng", [1, 512])
    hg_ng_d = din("hg_ng", [1, 512])
    conv_wT_d = din("conv_wT", [128, 4, 4])
    lru_vecT_d = din("lru_vecT", [128, 4, 4])
    lru_wbd_d = din("lru_wbd", [128, 8, 128])
    s5_vecT_d = din("s5_vecT", [128, 3, 16])
    s5_bbd_d = din("s5_bbd", [128, 32, 128])
    s5_cbd_d = din("s5_cbd", [128, 32, 128])
    s5_dT_d = din("s5_dT", [128, 2, 4])
    hg_lbT_d = din("hg_lbT", [128, 2, 4])
    st_gla_d = din("st_gla", [128, NS, 2, 128])
    st_conv_d = din("st_conv", [128, 4, 3, NS])
    st_lru_d = din("st_lru", [128, 4, NS])
    st_s5_d = din("st_s5", [128, 2, 16, NS])
    st_hg_d = din("st_hg", [128, NS, 4, 128])

    y_p = dout("y_p", [L, D])
    y_sT = dout("y_sT", [128, 8, NS])
    o_gla_p = dout("o_gla_p", [128, 2, 128])
    o_gla_s = dout("o_gla_s", [128, NS, 2, 128])
    o_conv_p = dout("o_conv_p", [128, 4, 3])
    o_conv_s = dout("o_conv_s", [128, 4, 3, NS])
    o_lru_p = dout("o_lru_p", [128, 4])
    o_lru_s = dout("o_lru_s", [128, 4, NS])
    o_s5_p = dout("o_s5_p", [128, 2, 16])
    o_s5_s = dout("o_s5_s", [128, 2, 16, NS])
    o_hg_p = dout("o_hg_p", [128, 4, 128])
    o_hg_s = dout("o_hg_s", [128, NS, 4, 128])

    es = ExitStack()
    with es:
        P = Prog(nc, es)

        _nm = [0]

        def sb(name, shape, dt=F32, stack=es):
            _nm[0] += 1
            return stack.enter_context(nc.sbuf_tensor("t%d_%s" % (_nm[0], name), list(shape), dt))

        X = sb("X", [128, NT, D])
        XsT = sb("XsT", [128, 8, NS])
        hT = sb("hT", [128, 8, TW], BF16)
        modT = sb("modT", [128, 2, 48, 17])
        ident = sb("ident", [128, 128])
        identb = sb("identb", [128, 128], BF16)
        onesf = sb("onesf", [128, 128])
        cmask = sb("cmask", [128, 4, 64], BF16)
        pp2 = [es.enter_context(nc.psum_tensor("pp%d" % i, [128, 1024], F32)) for i in range(2)]
        ps = [None] * 4 + [es.enter_context(nc.psum_tensor("ps%d" % i, [128, 512], F32)) for i in range(4, 8)]
        dpp = [Dep(), Dep()]

        dX = [Dep() for _ in range(NT)]
        dXs = Dep()
        dhT = Dep()
        dhTs = Dep()
        dyT = [Dep() for _ in range(8)]
        dyTs = Dep()
        dmod = Dep()
        dconst = Dep()
        dbc = [Dep() for _ in range(3)]
        dps = [Dep() for _ in range(8)]
        s_out = "out"
        s_in = "in"

        def c_ident(e):
            e.memset(ident[:], 0.0)
            e.memset(onesf[:], 1.0)
            return e.memset(cmask[:], 1.0)
        P.op("pool", c_ident, writes=[dconst])

        def c_sel(e):
            e.affine_select(out=ident[:], in_=onesf[:], pattern=[[-1, 128]], compare_op=ALU.is_equal,
                            fill=0.0, base=0, channel_multiplier=1)
            e.affine_select(out=cmask[0:64], in_=cmask[0:64], pattern=[[0, 4], [1, 64]], compare_op=ALU.is_ge,
                            fill=0.0, base=0, channel_multiplier=-1)
            return e.affine_select(out=cmask[64:128], in_=cmask[64:128], pattern=[[0, 4], [1, 64]], compare_op=ALU.is_ge,
                                   fill=0.0, base=0, channel_multiplier=-1)
        P.op("pool", c_sel, reads=[dconst], writes=[dconst])
        P.op("dve", lambda e: e.tensor_copy(out=identb[:], in_=ident[:]), reads=[dconst], writes=[dconst])

        xpv = xp.rearrange("(n p) d -> p n d", p=128)
        for tt in range(NT):
            P.dma("sp", X[:, tt, :], xpv[:, tt, :], s_in, writes=[dX[tt]])
        P.dma("sp", XsT[:], xsT_d[:, :, :], s_in, writes=[dXs])

        def setup_mod(stack):
            cT = sb("cT", [128, 8, 17], stack=stack)
            condT = sb("condT", [128, 8, 17], BF16, stack=stack)
            b_adaT = sb("b_adaT", [128, 2, 48], stack=stack)
            wab = [sb("wab%d" % i, [128, 8, 512], BF16, stack=stack) for i in range(2)]
            mrow = [sb("mrow%d" % i, [17, 512], stack=stack) for i in range(2)]
            dwab = [Dep(), Dep()]
            swab = [P.dsem(), P.dsem()]
            dc = Dep()
            dmrow = [Dep(), Dep()]
            P.dma("sp", cT[:], cT_d[:, :, :], s_in, writes=[dc])
            P.dma("sp", b_adaT[:], b_adaT_d[:, :, :], s_in, writes=[dc])
            P.op("act", lambda e: e.activation(out=condT[:], in_=cT[:], func=AF.Silu), reads=[dc], writes=[dc])
            blocks = [(l, blk) for l in range(2) for blk in range(12)]
            nb_ = len(blocks)
            stt = {"dma": 0, "A": 0, "B": 0, "C": 0, "D": 0}

            def dma(i):
                l, blk = blocks[i]
                wv = w_ada[l].rearrange("(c p) f -> p c f", p=128)
                sl_ = i % 2
                for c in range(8):
                    P.dma("pool", wab[sl_][:, c, :], wv[:, c, blk * 512:(blk + 1) * 512], swab[sl_], writes=[dwab[sl_]])

            def stageA(i):
                sl_ = i % 2

                def mmf(e):
                    ins = None
                    for c in range(8):
                        ins = e.matmul(ps[5][0:17, 0:512], lhsT=condT[:, c, :], rhs=wab[sl_][:, c, :], start=(c == 0), stop=(c == 7))
                    return ins
                P.op("pe", mmf, reads=[dwab[sl_], dc], writes=[dps[5]])

            def stageB(i):
                k = i % 2
                P.op("act", lambda e: e.activation(out=mrow[k][:], in_=ps[5][0:17, 0:512], func=AF.Copy), reads=[dps[5]], writes=[dmrow[k]])

            def stageC(i):
                k = i % 2
                pb = 6 + k

                def trf(e):
                    ins = None
                    for q in range(4):
                        ins = e.transpose(out=ps[pb][:, q * 17:(q + 1) * 17], in_=mrow[k][0:17, q * 128:(q + 1) * 128], identity=ident[0:17, 0:17])
                    return ins
                P.op("pe", trf, reads=[dmrow[k], dconst], writes=[dps[pb]])

            def stageD(i):
                l, blk = blocks[i]
                pb = 6 + i % 2
                P.op("dve", lambda e: e.tensor_tensor(
                    out=modT[:, l, blk * 4:(blk + 1) * 4, :],
                    in0=ps[pb][:, 0:68].rearrange("p (j n) -> p j n", n=17),
                    in1=b_adaT[:, l, blk * 4:(blk + 1) * 4].unsqueeze(2).broadcast_to([128, 4, 17]), op=ALU.add),
                    reads=[dps[pb], dc], writes=[dmod])

            def slot():
                if stt["D"] < stt["C"]:
                    stageD(stt["D"])
                    stt["D"] += 1
                if stt["C"] < stt["B"]:
                    stageC(stt["C"])
                    stt["C"] += 1
                if stt["B"] < stt["A"]:
                    stageB(stt["B"])
                    stt["B"] += 1
                if stt["A"] < nb_:
                    i = stt["A"]
                    while stt["dma"] <= min(i + 1, nb_ - 1):
                        dma(stt["dma"])
                        stt["dma"] += 1
                    stageA(i)
                    stt["A"] += 1

            def issue(n):
                for _ in range(n):
                    if stt["D"] >= nb_:
                        return
                    slot()

            def drain():
                while stt["D"] < stt["A"]:
                    if stt["D"] < stt["C"]:
                        stageD(stt["D"])
                        stt["D"] += 1
                    if stt["C"] < stt["B"]:
                        stageC(stt["C"])
                        stt["C"] += 1
                    if stt["B"] < stt["A"]:
                        stageB(stt["B"])
                        stt["B"] += 1
            issue.drain = drain
            return issue

        def modp(l, g):
            return modT[:, l, g * 8:(g + 1) * 8, 0]

        def mods(l, g):
            return modT[:, l, g * 8:(g + 1) * 8, 1:17]

        small = sb("small", [128, 64])
        dsmall = Dep()
        hs_tmp = sb("hs_tmp", [128, 8, NS])
        dhs_tmp = Dep()
        wfm = [sb("wfm%d" % i, [128, 8, 128], BF16) for i in range(3)]
        dwfm = [Dep() for _ in range(3)]
        swfm = [P.dsem() for _ in range(3)]
        wcnt = [0]
        pcnt = [0]
        k67 = [0]

        def nxt67():
            k67[0] += 1
            return 6 + (k67[0] % 2)

        def load_w(buf, dep, ds, wsrc, c0, n, nk=8):
            wv = wsrc.rearrange("(c p) f -> p c f", p=128)
            P.dma("pool", buf[:, 0:nk, 0:n], wv[:, :, c0:c0 + n], ds, writes=[dep])

        def make_hT(l, g_sc, g_sh):
            P.op("dve", lambda e: e.tensor_scalar(out=small[:, 0:8], in0=modp(l, g_sc), scalar1=1.0, scalar2=None, op0=ALU.add),
                 reads=[dmod], writes=[dsmall])
            for tb in range(4):
                for dc_ in range(8):
                    pb = nxt67()

                    def tr(e, tb=tb, dc_=dc_, pb=pb):
                        ins = None
                        for q in range(4):
                            ins = e.transpose(out=ps[pb][:, q * 128:(q + 1) * 128], in_=X[:, tb * 4 + q, dc_ * 128:(dc_ + 1) * 128],
                                              identity=ident[:])
                        return ins
                    P.op("pe", tr, reads=[dX[tb * 4 + q] for q in range(4)] + [dconst], writes=[dps[pb]])
                    P.op("act", lambda e, tb=tb, dc_=dc_, pb=pb: e.activation(
                        out=hT[:, dc_, tb * 512:(tb + 1) * 512], in_=ps[pb][:], func=AF.Identity,
                        scale=small[:, dc_:dc_ + 1], bias=modp(l, g_sh)[:, dc_:dc_ + 1]),
                        reads=[dps[pb], dsmall, dmod], writes=[dhT])
            P.op("dve", lambda e: e.scalar_tensor_tensor(out=hs_tmp[:], in0=mods(l, g_sc), scalar=1.0, in1=XsT[:],
                                                         op0=ALU.add, op1=ALU.mult),
                 reads=[dmod, dXs], writes=[dhs_tmp])
            P.op("dve", lambda e: e.tensor_tensor(out=hT[:, :, L:TW], in0=hs_tmp[:], in1=mods(l, g_sh), op=ALU.add),
                 reads=[dhs_tmp, dmod], writes=[dhTs])

        def proj_fm(wsrc, c0, n, src, dsrc_p, dsrc_s, cons_p, cons_s, nk=8, wbuf=None):
            if wbuf is None:
                i = wcnt[0] % 3
                wcnt[0] += 1
                load_w(wfm[i], dwfm[i], swfm[i], wsrc, c0, n, nk)
                wb, dwb = wfm[i], dwfm[i]
                lw = lambda c: wb[:, c, 0:n]
            else:
                lw, dwb = wbuf
            if cons_p is not None:
                for th in range(2):
                    j = pcnt[0] % 2
                    pcnt[0] += 1

                    def mm(e, th=th, j=j):
                        ins = None
                        for tb in range(2):
                            for c in range(nk):
                                ins = e.matmul(pp2[j][0:n, tb * 512:(tb + 1) * 512], lhsT=lw(c),
                                               rhs=src[:, c, th * 1024 + tb * 512: th * 1024 + (tb + 1) * 512],
                                               start=(c == 0), stop=(c == nk - 1))
                        return ins
                    P.op("pe", mm, reads=[dwb, dsrc_p], writes=[dpp[j]])
                    cons_p(th, pp2[j][0:n, :], dpp[j])
            if cons_s is not None:
                def mms(e):
                    ins = None
                    for c in range(nk):
                        ins = e.matmul(ps[4][0:n, 0:NS], lhsT=lw(c), rhs=src[:, c, L:TW], start=(c == 0), stop=(c == nk - 1))
                    return ins
                P.op("pe", mms, reads=[dwb, dsrc_s], writes=[dps[4]])
                cons_s(ps[4][0:n, 0:NS], dps[4])

        def gelu_from(src_ap, dsrc, out_ap, dout, tmp, dtmp, n):
            P.op("act", lambda e: e.activation(out=tmp, in_=src_ap, func=AF.Square), reads=[dsrc], writes=[dtmp])
            P.op("dve", lambda e: e.tensor_scalar(out=tmp, in0=tmp, scalar1=0.044715, scalar2=1.0, op0=ALU.mult, op1=ALU.add),
                 reads=[dtmp], writes=[dtmp])
            P.op("dve", lambda e: e.tensor_tensor(out=tmp, in0=src_ap, in1=tmp, op=ALU.mult), reads=[dsrc, dtmp], writes=[dtmp])
            P.op("act", lambda e: e.activation(out=tmp, in_=tmp, func=AF.Sigmoid, scale=GELU_C), reads=[dtmp], writes=[dtmp])
            P.op("dve", lambda e: e.tensor_tensor(out=out_ap, in0=src_ap, in1=tmp, op=ALU.mult), reads=[dsrc, dtmp], writes=[dout])

        I16 = sb("I16", [128, NS, NS], BF16)
        P.op("pool", lambda e: e.memset(I16[:], 1.0), writes=[dconst], reads=[dconst])
        P.op("pool", lambda e: e.affine_select(out=I16[:], in_=I16[:], pattern=[[1, NS], [-1, NS]], compare_op=ALU.is_equal,
                                                fill=0.0, base=0, channel_multiplier=0), reads=[dconst], writes=[dconst])
        stg = sb("stg", [128, 64])
        dstg = Dep()

        def gla_chunks(dk, nh, qsel, kt, kd, vtok, gg, el, S, Sb_unused, ydst, dyT, deps_in, S_out_dram):
            hp = 128 // dk
            nfq = nh // hp
            NV = nh * 128
            NC = 2 * NT
            dS_ = Dep()
            with ExitStack() as st:
                scT = [sb("scT%d" % i, [128, nh, 64], BF16, stack=st) for i in range(2)]
                dscT = [Dep(), Dep()]
                Sb2 = [sb("Sb2_%d" % i, [128, nfq, 128], BF16, stack=st) for i in range(2)]
                dSb2 = [Dep(), Dep()]
                osq = sb("osq", [128, NV], stack=st)
                rs = sb("rs", [128, 2 * nh], stack=st)
                ytok = [sb("ytok%d" % i, [128, NV], BF16, stack=st) for i in range(2)]
                dytok = [Dep(), Dep()]
                dloc = Dep()
                dppB = [Dep(), Dep()]
                P.op("dve", lambda e: e.memset(S[:], 0.0), writes=[dS_])
                for i in range(2):
                    P.op("dve", lambda e, i=i: e.memset(Sb2[i][:], 0.0), writes=[dSb2[i]])
                    P.op("dve", lambda e, i=i: e.memset(scT[i][:], 0.0), writes=[dscT[i]])

                def issue_scores(c):
                    tt, cc = divmod(c, 2)
                    po, t0, j = cc * 64, c * 64, cc

                    def sc_mm(e):
                        ins = None
                        for hl in range(nh):
                            fq = hl // hp
                            ins = e.matmul(pp2[j][po:po + 64, hl * 64:(hl + 1) * 64], lhsT=kt[:, fq, t0:t0 + 64],
                                           rhs=qsel(hl)[:, t0:t0 + 64], start=True, stop=True)
                        return ins
                    P.op("pe", sc_mm, reads=deps_in, writes=[dpp[j]])
                    P.op("dve", lambda e: e.tensor_tensor(
                        out=scT[j][po:po + 64], in0=pp2[j][po:po + 64, 0:nh * 64].rearrange("p (h t) -> p h t", t=64),
                        in1=cmask[po:po + 64, 0:nh, :], op=ALU.mult), reads=[dpp[j], dconst], writes=[dscT[j]])

                def issue_state(c):
                    tt, cc = divmod(c, 2)
                    po = cc * 64
                    db = 7 if cc == 0 else 4

                    def ds_mm(e):
                        ins = None
                        for hl in range(nh):
                            pr = (hl % hp) * dk
                            fq = hl // hp
                            ins = e.matmul(ps[db][pr:pr + dk, fq * 128:(fq + 1) * 128], lhsT=kd[po:po + 64, tt, hl * dk:(hl + 1) * dk],
                                           rhs=vtok[po:po + 64, tt, hl * 128:(hl + 1) * 128], start=True, stop=True)
                        return ins
                    P.op("pe", ds_mm, reads=deps_in, writes=[dps[db]])
                    for fq in range(nfq):
                        P.op("dve", lambda e, fq=fq: e.scalar_tensor_tensor(
                            out=S[:, fq, :], in0=S[:, fq, :], scalar=el[:, fq, c:c + 1], in1=ps[db][:, fq * 128:(fq + 1) * 128],
                            op0=ALU.mult, op1=ALU.add), reads=[dps[db]] + deps_in, writes=[dS_])
                    P.op("act", lambda e: e.activation(out=Sb2[c % 2][:], in_=S[:], func=AF.Copy), reads=[dS_], writes=[dSb2[c % 2]])

                def issue_out(c):
                    tt, cc = divmod(c, 2)
                    po, t0, j = cc * 64, c * 64, cc
                    ob = pp2[tt % 2]
                    sprev = Sb2[(c - 1) % 2]

                    def o_mm(e):
                        ins = None
                        for hl in range(nh):
                            fq = hl // hp
                            e.matmul(ob[po:po + 64, 512 + hl * 128:512 + (hl + 1) * 128], lhsT=scT[j][:, hl, :],
                                     rhs=vtok[:, tt, hl * 128:(hl + 1) * 128], start=True, stop=False)
                            ins = e.matmul(ob[po:po + 64, 512 + hl * 128:512 + (hl + 1) * 128], lhsT=qsel(hl)[:, t0:t0 + 64],
                                           rhs=sprev[:, fq, 0:128], start=False, stop=True)
                        return ins
                    P.op("pe", o_mm, reads=deps_in + [dscT[j], dSb2[(c - 1) % 2]], writes=[dppB[tt % 2]])

                def issue_epilogue(tt):
                    ob = pp2[tt % 2][:, 512:512 + NV]
                    dob = dppB[tt % 2]
                    yk = ytok[tt % 2]
                    P.op("act", lambda e: e.activation(out=osq[:], in_=ob, func=AF.Square), reads=[dob], writes=[dloc])
                    P.op("dve", lambda e: e.tensor_reduce(out=rs[:, 0:nh], in_=osq[:].rearrange("p (h v) -> p h v", v=128),
                                                          axis=AX.X, op=ALU.add), reads=[dloc], writes=[dloc])
                    P.op("act", lambda e: e.activation(out=rs[:, 0:nh], in_=rs[:, 0:nh], func=AF.Sqrt, scale=1.0 / 128, bias=RMS_EPS),
                         reads=[dloc], writes=[dloc])
                    P.op("dve", lambda e: e.reciprocal(out=rs[:, nh:2 * nh], in_=rs[:, 0:nh]), reads=[dloc], writes=[dloc])
                    P.op("dve", lambda e: e.tensor_tensor(out=osq[:].rearrange("p (h v) -> p h v", v=128),
                                                          in0=ob.rearrange("p (h v) -> p h v", v=128),
                                                          in1=rs[:, nh:2 * nh].unsqueeze(2).broadcast_to([128, nh, 128]), op=ALU.mult),
                         reads=[dob, dloc], writes=[dloc])
                    P.op("dve", lambda e: e.tensor_tensor(out=yk[:], in0=osq[:], in1=gg[:, tt, :], op=ALU.mult),
                         reads=[dloc] + deps_in, writes=[dytok[tt % 2]])

                def issue_ytr(tt):
                    yk = ytok[tt % 2]
                    pb = 6 if tt % 2 == 0 else 5

                    def ytr(e):
                        ins = None
                        for q in range(nh):
                            ins = e.transpose(out=ps[pb][:].bitcast(BF16)[:, q * 128:(q + 1) * 128], in_=yk[:, q * 128:(q + 1) * 128],
                                              identity=identb[:])
                        return ins
                    P.op("pe", ytr, reads=[dytok[tt % 2], dconst], writes=[dps[pb]])
                    P.op("act", lambda e: e.activation(
                        out=ydst(tt), in_=ps[pb][:].bitcast(BF16)[:, 0:NV].rearrange("p (q t) -> p q t", t=128), func=AF.Copy),
                        reads=[dps[pb]], writes=[dyT])

                issue_scores(0)
                for c in range(NC):
                    tt, cc = divmod(c, 2)
                    issue_state(c)
                    if c + 1 < NC:
                        issue_scores(c + 1)
                    issue_out(c)
                    if cc == 1:
                        issue_epilogue(tt)
                        if tt >= 1:
                            issue_ytr(tt - 1)
                issue_ytr(NT - 1)
                P.dma("sp", S_out_dram, S[:], s_out, reads=[dS_])
                P.barrier()

        def transpose_kd(kdT, dkdT, kd, dkd, fq, nf):
            for tb in range(4):
                pb = nxt67()

                def tr(e, tb=tb, pb=pb):
                    ins = None
                    for q in range(4):
                        ins = e.transpose(out=ps[pb][:].bitcast(BF16)[:, q * 128:(q + 1) * 128],
                                          in_=kdT[:, (tb * 4 + q) * 128:(tb * 4 + q + 1) * 128], identity=identb[:])
                    return ins
                P.op("pe", tr, reads=[dkdT, dconst], writes=[dps[pb]])
                P.op("act", lambda e, tb=tb, pb=pb: e.activation(
                    out=kd[:, tb * 4:(tb + 1) * 4, fq * 128:(fq + 1) * 128],
                    in_=ps[pb][:].bitcast(BF16)[:, 0:512].rearrange("p (q t) -> p q t", t=128), func=AF.Copy),
                    reads=[dps[pb]], writes=[dkd])

        def proj_tm(wsrc, c0, wt, dwt, swt, cons_p, cons_s, n=512):
            load_w(wt, dwt, swt, wsrc, c0, n)
            for tt in range(NT):
                pb = 5 + (tt % 3)

                def mm(e, tt=tt, pb=pb):
                    ins = None
                    for c in range(8):
                        ins = e.matmul(ps[pb][:, 0:n], lhsT=hT[:, c, tt * 128:(tt + 1) * 128], rhs=wt[:, c, 0:n], start=(c == 0), stop=(c == 7))
                    return ins
                P.op("pe", mm, reads=[dwt, dhT], writes=[dps[pb]])
                cons_p(tt, ps[pb][:, 0:n], dps[pb])

            def mms(e):
                ins = None
                for c in range(8):
                    ins = e.matmul(ps[5][0:NS, 0:n], lhsT=hT[:, c, L:TW], rhs=wt[:, c, 0:n], start=(c == 0), stop=(c == 7))
                return ins
            P.op("pe", mms, reads=[dwt, dhTs], writes=[dps[5]])
            cons_s(ps[5][0:NS, 0:n], dps[5])

        def sample_state_update(dk, q_s, k_tok, v_tok, a_s, Sst, dSst, dq, st, tag):
            hp = 128 // dk
            nfq = 4 // hp
            KM = sb("KM" + tag, [NS, NS, 4 * dk], BF16, stack=st)
            QM = sb("QM" + tag, [128, 4, NS, NS], stack=st)
            qz = sb("qz" + tag, [128, 4, NS], stack=st)
            dkm = Dep()
            P.op("dve", lambda e: e.tensor_tensor(out=KM[:], in0=k_tok.unsqueeze(1).broadcast_to([NS, NS, 4 * dk]),
                                                  in1=ident[0:NS, 0:NS].unsqueeze(2).broadcast_to([NS, NS, 4 * dk]), op=ALU.mult),
                 reads=dq + [dconst], writes=[dkm])
            P.op("dve", lambda e: e.memset(qz[:], 0.0), writes=[dkm])
            for h in range(4):
                pr = (h % hp) * dk
                P.op("dve", lambda e, h=h, pr=pr: e.tensor_copy(out=qz[pr:pr + dk, h, :], in_=q_s[pr:pr + dk, h // hp, :]), reads=dq + [dkm], writes=[dkm])
            P.op("dve", lambda e: e.tensor_tensor(out=QM[:], in0=qz[:].unsqueeze(2).broadcast_to([128, 4, NS, NS]),
                                                  in1=I16[:].unsqueeze(1).broadcast_to([128, 4, NS, NS]), op=ALU.mult),
                 reads=dq + [dconst, dkm], writes=[dkm])
            groups = [(h, b0) for h in range(4) for b0 in range(0, NS, 4)]

            def issue_dsm(gi):
                h, b0 = groups[gi]
                pr = (h % hp) * dk
                db = 7 if gi % 2 == 0 else 4

                def dsm(e):
                    ins = None
                    for q in range(4):
                        ins = e.matmul(ps[db][pr:pr + dk, q * 128:(q + 1) * 128], lhsT=KM[0:NS, b0 + q, h * dk:(h + 1) * dk],
                                       rhs=v_tok[0:NS, h * 128:(h + 1) * 128], start=True, stop=True)
                    return ins
                P.op("pe", dsm, reads=[dkm] + dq, writes=[dps[db]])

            issue_dsm(0)
            for gi, (h, b0) in enumerate(groups):
                pr = (h % hp) * dk
                fq = h // hp
                db = 7 if gi % 2 == 0 else 4
                for q in range(4):
                    b = b0 + q
                    P.op("dve", lambda e, b=b, q=q, pr=pr, fq=fq, db=db: e.scalar_tensor_tensor(
                        out=Sst[pr:pr + dk, b, fq, :], in0=Sst[pr:pr + dk, b, fq, :], scalar=a_s[pr:pr + dk, fq, b:b + 1],
                        in1=ps[db][pr:pr + dk, q * 128:(q + 1) * 128], op0=ALU.mult, op1=ALU.add), reads=[dps[db]] + dq, writes=[dSst])
                if gi + 1 < len(groups):
                    issue_dsm(gi + 1)

                def omm(e, h=h, b0=b0, fq=fq):
                    ins = None
                    for q in range(4):
                        b = b0 + q
                        ins = e.matmul(ps[5][0:NS, h * 128:(h + 1) * 128], lhsT=QM[:, h, b, :], rhs=Sst[:, b, fq, :],
                                       start=(b == 0), stop=(b == NS - 1))
                    return ins
                P.op("pe", omm, reads=[dkm, dSst], writes=[dps[5]])

        def rms_gate_sample(gg_s, dgg, ydst_s, dyTs, st, tag):
            osq = sb("osq_s" + tag, [NS, 512], stack=st)
            rs = sb("rs_s" + tag, [NS, 8], stack=st)
            d = Dep()
            P.op("act", lambda e: e.activation(out=osq[:], in_=ps[5][0:NS, :], func=AF.Square), reads=[dps[5]], writes=[d])
            P.op("dve", lambda e: e.tensor_reduce(out=rs[:, 0:4], in_=osq[:].rearrange("p (h v) -> p h v", v=128), axis=AX.X, op=ALU.add),
                 reads=[d], writes=[d])
            P.op("act", lambda e: e.activation(out=rs[:, 0:4], in_=rs[:, 0:4], func=AF.Sqrt, scale=1.0 / 128, bias=RMS_EPS), reads=[d], writes=[d])
            P.op("dve", lambda e: e.reciprocal(out=rs[:, 4:8], in_=rs[:, 0:4]), reads=[d], writes=[d])
            P.op("dve", lambda e: e.tensor_tensor(out=osq[:].rearrange("p (h v) -> p h v", v=128),
                                                  in0=ps[5][0:NS, :].rearrange("p (h v) -> p h v", v=128),
                                                  in1=rs[:, 4:8].unsqueeze(2).broadcast_to([NS, 4, 128]), op=ALU.mult),
                 reads=[dps[5], d], writes=[d])
            P.op("dve", lambda e: e.tensor_tensor(out=osq[:], in0=osq[:], in1=gg_s, op=ALU.mult), reads=[d, dgg], writes=[d])

            def tr(e):
                ins = None
                for q in range(4):
                    ins = e.transpose(out=ps[6][:, q * NS:(q + 1) * NS], in_=osq[0:NS, q * 128:(q + 1) * 128], identity=ident[0:NS, 0:NS])
                return ins
            P.op("pe", tr, reads=[d, dconst], writes=[dps[6]])
            P.op("act", lambda e: e.activation(out=ydst_s, in_=ps[6][:, 0:4 * NS].rearrange("p (q t) -> p q t", t=NS), func=AF.Copy),
                 reads=[dps[6]], writes=[dyTs])
        def make_rmask(rmask, drm):
            P.op("pool", lambda e: e.memset(rmask[:], 1.0), writes=[drm])
            P.op("pool", lambda e: e.affine_select(out=rmask[:].rearrange("p (c j) -> p c j", j=64),
                                                    in_=rmask[:].rearrange("p (c j) -> p c j", j=64),
                                                    pattern=[[0, 32], [1, 64]], compare_op=ALU.is_gt, fill=0.0, base=0,
                                                    channel_multiplier=0), reads=[drm], writes=[drm])

        def acopy(out, in_, reads, writes, eng="act"):
            if eng == "act":
                P.op("act", lambda e: e.activation(out=out, in_=in_, func=AF.Copy), reads=reads, writes=writes)
            else:
                P.op(eng, lambda e: e.tensor_copy(out=out, in_=in_), reads=reads, writes=writes)

        def even_mixer(l, yoth, dyoth, dyoths, bg=None, bg_stack=None):
            W = ev_w_in
            stage(3)
            with ExitStack() as st:
                lruv = sb("lruv", [128, 4, 4], stack=st)
                convw = sb("convw", [128, 4, 4], stack=st)
                wbd = sb("wbd", [128, 8, 128], BF16, stack=st)
                stconv = sb("stconv", [128, 4, 3, NS], stack=st)
                stlru = sb("stlru", [128, 4, NS], stack=st)
                oconv = sb("oconv", [128, 4, 3, NS], stack=st)
                olru = sb("olru", [128, 4, NS], stack=st)
                xrp = sb("xrp", [128, L + 3], stack=st)
                xc = sb("xc", [128, L], stack=st)
                xcb = sb("xcb", [128, L], BF16, stack=st)
                rr = sb("rr", [128, L], stack=st)
                ig = sb("ig", [128, L], stack=st)
                aa = sb("aa", [128, L], stack=st)
                gg = sb("gg", [128, L], stack=st)
                sm = sb("lru_sm", [128, 16, NS], stack=st)
                smb = sb("lru_smb", [128, NS], BF16, stack=st)
                dpar, dxr, dxc, drr, dig, daa, dgg, dsm, dost = Dep(), Dep(), Dep(), Dep(), Dep(), Dep(), Dep(), Dep(), Dep()
                swbd = P.dsem()
                P.dma("sp", lruv[:], lru_vecT_d[:, :, :], s_in, writes=[dpar])
                P.dma("sp", convw[:], conv_wT_d[:, :, :], s_in, writes=[dpar])
                P.dma("sp", stconv[:], st_conv_d[:, :, :, :], s_in, writes=[dpar])
                P.dma("sp", stlru[:], st_lru_d[:, :, :], s_in, writes=[dpar])
                P.dma("pool", wbd[:], lru_wbd_d[:, :, :], swbd, writes=[dpar])
                c8 = small[:, 16:20]
                c16 = small[:, 20:24]
                P.op("act", lambda e: e.activation(out=small[:, 24:28], in_=lruv[:, :, 3], func=AF.Exp, scale=-1.0), reads=[dpar], writes=[dsmall])
                P.op("act", lambda e: e.activation(out=small[:, 24:28], in_=small[:, 24:28], func=AF.Ln, bias=1.0), reads=[dsmall], writes=[dsmall])
                P.op("dve", lambda e: e.tensor_scalar(out=c8, in0=small[:, 24:28], scalar1=-8.0, scalar2=None, op0=ALU.mult), reads=[dsmall], writes=[dsmall])
                P.op("dve", lambda e: e.tensor_scalar(out=c16, in0=small[:, 24:28], scalar1=-16.0, scalar2=None, op0=ALU.mult), reads=[dsmall], writes=[dsmall])
                P.op("dve", lambda e: e.memset(xrp[:, 0:3], 0.0), writes=[dxr])
                for g in range(4):
                    def cons_xr(th, pap, dp):
                        acopy(xrp[:, 3 + th * 1024: 3 + (th + 1) * 1024], pap, [dp], [dxr])

                    def cons_xr_s(pap, dp):
                        acopy(sm[:, 0, :], pap, [dp], [dsm])
                    proj_fm(W, 1552 + g * 128, 128, hT, dhT, dhTs, cons_xr, cons_xr_s)
                    if bg is not None:
                        bg(1)
                    def cons_gr(th, pap, dp):
                        sl = slice(th * 1024, (th + 1) * 1024)
                        gelu_from(pap, dp, gg[:, sl], dgg, aa[:, sl], daa, 1024)

                    def cons_gr_s(pap, dp):
                        gelu_from(pap, dp, sm[:, 1, :], dsm, sm[:, 2, :], dsm, NS)
                    proj_fm(W, 2064 + g * 128, 128, hT, dhT, dhTs, cons_gr, cons_gr_s)
                    if bg is not None:
                        bg(1)
                    P.op("dve", lambda e, g=g: e.tensor_scalar(out=xc[:], in0=xrp[:, 3:3 + L], scalar1=convw[:, g, 3:4], scalar2=lruv[:, g, 0:1],
                                                              op0=ALU.mult, op1=ALU.add), reads=[dxr, dpar], writes=[dxc])
                    for j in range(3):
                        P.op("dve", lambda e, g=g, j=j: e.scalar_tensor_tensor(out=xc[:], in0=xrp[:, j:j + L], scalar=convw[:, g, j:j + 1], in1=xc[:],
                                                                           op0=ALU.mult, op1=ALU.add), reads=[dxr, dpar, dxc], writes=[dxc])
                    acopy(xcb[:], xc[:], [dxc], [dxc])
                    acopy(stg[:, g * 3:(g + 1) * 3], xrp[:, L:L + 3], [dxr], [dstg], eng="dve")
                    if bg is not None:
                        bg(1)
                    P.op("dve", lambda e, g=g: e.tensor_scalar(out=sm[:, 3, :], in0=sm[:, 0, :], scalar1=convw[:, g, 3:4], scalar2=lruv[:, g, 0:1],
                                                              op0=ALU.mult, op1=ALU.add), reads=[dsm, dpar], writes=[dsm])
                    for j in range(3):
                        P.op("dve", lambda e, g=g, j=j: e.scalar_tensor_tensor(out=sm[:, 3, :], in0=stconv[:, g, j, :], scalar=convw[:, g, j:j + 1],
                                                                           in1=sm[:, 3, :], op0=ALU.mult, op1=ALU.add), reads=[dsm, dpar], writes=[dsm])
                    acopy(smb[:], sm[:, 3, :], [dsm], [dsm])
                    acopy(oconv[:, g, 0:2, :], stconv[:, g, 1:3, :], [dpar], [dost], eng="dve")
                    acopy(oconv[:, g, 2, :], sm[:, 0, :], [dsm], [dost], eng="dve")
                    for gi, (dst, ddst) in enumerate(((rr, drr), (ig, dig))):
                        for th in range(2):
                            j = pcnt[0] % 2
                            pcnt[0] += 1

                            def gmm(e, th=th, j=j, gi=gi, g=g):
                                ins = None
                                for tb in range(2):
                                    ins = e.matmul(pp2[j][:, tb * 512:(tb + 1) * 512], lhsT=wbd[:, gi * 4 + g, :],
                                                   rhs=xcb[:, th * 1024 + tb * 512: th * 1024 + (tb + 1) * 512], start=True, stop=True)
                                return ins
                            P.op("pe", gmm, reads=[dpar, dxc], writes=[dpp[j]])
                            P.op("act", lambda e, th=th, j=j, gi=gi, g=g, dst=dst: e.activation(
                                out=dst[:, th * 1024:(th + 1) * 1024], in_=pp2[j][:, :], func=AF.Sigmoid, bias=lruv[:, g, 1 + gi:2 + gi]),
                                reads=[dpp[j], dpar], writes=[ddst])
                        P.op("pe", lambda e, gi=gi, g=g: e.matmul(ps[4][:, 0:NS], lhsT=wbd[:, gi * 4 + g, :], rhs=smb[:], start=True, stop=True),
                             reads=[dpar, dsm], writes=[dps[4]])
                        P.op("act", lambda e, gi=gi, g=g: e.activation(out=sm[:, 4 + gi, :], in_=ps[4][:, 0:NS], func=AF.Sigmoid,
                                                                      bias=lruv[:, g, 1 + gi:2 + gi]), reads=[dps[4], dpar], writes=[dsm])
                    if bg is not None:
                        bg(1)
                    P.op("act", lambda e, g=g: e.activation(out=aa[:], in_=rr[:], func=AF.Exp, scale=c8[:, g:g + 1]), reads=[drr, dsmall, dgg], writes=[daa])
                    P.op("act", lambda e, g=g: e.activation(out=rr[:], in_=rr[:], func=AF.Exp, scale=c16[:, g:g + 1]), reads=[dsmall], writes=[drr])
                    P.op("act", lambda e: e.activation(out=rr[:], in_=rr[:], func=AF.Sqrt, scale=-1.0, bias=1.0), reads=[], writes=[drr])
                    P.op("dve", lambda e: e.tensor_tensor(out=ig[:], in0=ig[:], in1=xc[:], op=ALU.mult), reads=[dxc], writes=[dig])
                    P.op("dve", lambda e: e.tensor_tensor(out=ig[:], in0=ig[:], in1=rr[:], op=ALU.mult), reads=[drr], writes=[dig])
                    P.op("dve", lambda e: e.tensor_tensor_scan(out=xc[:], data0=aa[:], data1=ig[:], initial=0.0, op0=ALU.mult, op1=ALU.add),
                         reads=[daa, dig], writes=[dxc])
                    P.op("dve", lambda e, g=g: e.tensor_tensor(out=yoth[:, g, 0:L], in0=xc[:], in1=gg[:], op=ALU.mult), reads=[dxc, dgg], writes=[dyoth])
                    acopy(stg[:, 12 + g:13 + g], xc[:, L - 1:L], [dxc], [dstg], eng="dve")
                    if bg is not None:
                        bg(1)
                    P.op("act", lambda e, g=g: e.activation(out=sm[:, 6, :], in_=sm[:, 4, :], func=AF.Exp, scale=c8[:, g:g + 1]), reads=[dsm, dsmall], writes=[dsm])
                    P.op("act", lambda e, g=g: e.activation(out=sm[:, 7, :], in_=sm[:, 4, :], func=AF.Exp, scale=c16[:, g:g + 1]), reads=[dsm, dsmall], writes=[dsm])
                    P.op("act", lambda e: e.activation(out=sm[:, 7, :], in_=sm[:, 7, :], func=AF.Sqrt, scale=-1.0, bias=1.0), reads=[dsm], writes=[dsm])
                    P.op("dve", lambda e: e.tensor_tensor(out=sm[:, 5, :], in0=sm[:, 5, :], in1=sm[:, 3, :], op=ALU.mult), reads=[dsm], writes=[dsm])
                    P.op("dve", lambda e: e.tensor_tensor(out=sm[:, 5, :], in0=sm[:, 5, :], in1=sm[:, 7, :], op=ALU.mult), reads=[dsm], writes=[dsm])
                    P.op("dve", lambda e, g=g: e.tensor_tensor(out=sm[:, 6, :], in0=sm[:, 6, :], in1=stlru[:, g, :], op=ALU.mult), reads=[dsm, dpar], writes=[dsm])
                    P.op("dve", lambda e, g=g: e.tensor_tensor(out=olru[:, g, :], in0=sm[:, 6, :], in1=sm[:, 5, :], op=ALU.add), reads=[dsm], writes=[dost])
                    P.op("dve", lambda e, g=g: e.tensor_tensor(out=yoth[:, g, L:TW], in0=olru[:, g, :], in1=sm[:, 1, :], op=ALU.mult), reads=[dsm, dost], writes=[dyoths])
                P.dma("sp", o_conv_s[:, :, :, :], oconv[:], s_out, reads=[dost])
                P.dma("sp", o_lru_s[:, :, :], olru[:], s_out, reads=[dost])
                P.dma("sp", o_conv_p.rearrange("p a b -> p (a b)"), stg[:, 0:12], s_out, reads=[dstg])
                P.dma("sp", o_lru_p[:, :], stg[:, 12:16], s_out, reads=[dstg])
                P.barrier()
            if bg is not None:
                bg(24)
                P.barrier()
                bg_stack.close()
            stage(4)
            with ExitStack() as st:
                qt = sb("qt", [128, 2, 2, L], BF16, stack=st)
                kt = sb("kt", [128, 2, L], BF16, stack=st)
                kd = sb("kd", [128, NT, 256], BF16, stack=st)
                elT = sb("elT", [128, 2, 32], stack=st)
                S = sb("S", [128, 2, 128], stack=st)
                Sb = sb("Sb", [128, 2, 128], BF16, stack=st)
                q_s = sb("q_s", [128, 2, NS], stack=st)
                a_s = sb("a_s", [128, 2, NS], stack=st)
                k_tok = sb("k_tok", [NS, 256], BF16, stack=st)
                v_tok = sb("v_tok", [NS, 512], BF16, stack=st)
                gg_s = sb("gg_s", [NS, 512], stack=st)
                gnB = sb("gnB", [128, 512], stack=st)
                dprep, dSst, dgs = Dep(), Dep(), Dep()
                P.op("pool", lambda e: e.memset(qt[:], 0.0), writes=[dprep])
                P.dma("sp", gnB[:], gla_ng_d.partition_broadcast(128), s_in, writes=[dgs])
                with ExitStack() as st2:
                    rmask = sb("rmask", [128, L], stack=st2)
                    csp = sb("csp", [128, L], stack=st2)
                    ee = sb("ee", [128, L], stack=st2)
                    kdT = sb("kdT", [128, L], BF16, stack=st2)
                    lrT = sb("lrT", [16, 1, TW], BF16, stack=st2)
                    wlr = sb("wlr", [16, 256], BF16, stack=st2)
                    blr = sb("blr", [128, 2], stack=st2)
                    sps = sb("sps", [128, NS], stack=st2)
                    drm, dcs, dee, ddd, dkdT, dlr, dw = Dep(), Dep(), Dep(), Dep(), Dep(), Dep(), Dep()
                    swl = P.dsem()
                    make_rmask(rmask, drm)
                    P.dma("pool", wlr[:], gla_w_lr_d[:, :], swl, writes=[dw])
                    P.dma("sp", blr[:], gla_b_lrT_d[:, :], s_in, writes=[dw])
                    P.op("dve", lambda e: e.tensor_scalar(out=blr[:], in0=blr[:], scalar1=-1.0, scalar2=None, op0=ALU.mult), reads=[dw], writes=[dw])
                    proj_fm(W, 1536, 16, hT, dhT, dhTs,
                            lambda th, pap, dp: acopy(lrT[:, 0, th * 1024:(th + 1) * 1024], pap, [dp], [dlr]),
                            lambda pap, dp: acopy(lrT[:, 0, L:TW], pap, [dp], [dlr]))
                    for fq in range(2):
                        def cons_g(th, pap, dp):
                            sl = slice(th * 1024, (th + 1) * 1024)
                            P.op("act", lambda e: e.activation(out=ee[:, sl], in_=pap, func=AF.Exp, scale=-1.0, bias=blr[:, fq:fq + 1]), reads=[dp, dw], writes=[dee])
                            P.op("act", lambda e: e.activation(out=ee[:, sl], in_=ee[:, sl], func=AF.Ln, bias=1.0), reads=[dee], writes=[dee])

                        def cons_g_s(pap, dp):
                            P.op("act", lambda e: e.activation(out=sps[:], in_=pap, func=AF.Exp, scale=-1.0, bias=blr[:, fq:fq + 1]), reads=[dp, dw], writes=[dee])
                            P.op("act", lambda e: e.activation(out=sps[:], in_=sps[:], func=AF.Ln, bias=1.0), reads=[dee], writes=[dee])
                            P.op("act", lambda e: e.activation(out=a_s[:, fq, :], in_=sps[:], func=AF.Exp, scale=-1.0 / 16), reads=[dee], writes=[dprep])
                        proj_fm(None, 0, 128, lrT, dlr, dlr, cons_g, cons_g_s, nk=1,
                                wbuf=(lambda c, fq=fq: wlr[0:16, fq * 128:(fq + 1) * 128], dw))
                        P.op("dve", lambda e: e.tensor_tensor_scan(out=csp[:], data0=rmask[:], data1=ee[:], initial=0.0, op0=ALU.mult, op1=ALU.add),
                             reads=[drm, dee, dkdT], writes=[dcs])
                        P.op("act", lambda e, fq=fq: e.activation(out=elT[:, fq, :], in_=csp[:].rearrange("p (c j) -> p c j", j=64)[:, :, 63],
                                                                   func=AF.Exp, scale=-1.0 / 16), reads=[dcs], writes=[dprep])
                        P.op("act", lambda e: e.activation(out=ee[:], in_=csp[:], func=AF.Exp, scale=-1.0 / 16), reads=[dcs], writes=[dee])

                        def cons_q(th, pap, dp):
                            sl = slice(th * 1024, (th + 1) * 1024)
                            for par in range(2):
                                pr_ = slice(par * 64, (par + 1) * 64)
                                P.op("dve", lambda e, par=par, pr_=pr_: e.scalar_tensor_tensor(
                                    out=qt[pr_, par, fq, sl], in0=pap[pr_], scalar=0.125, in1=ee[pr_, sl], op0=ALU.mult, op1=ALU.mult),
                                    reads=[dp, dee, dprep], writes=[dprep])

                        def cons_q_s(pap, dp):
                            P.op("dve", lambda e: e.tensor_scalar(out=q_s[:, fq, :], in0=pap, scalar1=0.125, scalar2=None, op0=ALU.mult), reads=[dp], writes=[dprep])
                        proj_fm(W, fq * 128, 128, hT, dhT, dhTs, cons_q, cons_q_s)
                        P.op("act", lambda e: e.activation(out=ee[:], in_=csp[:], func=AF.Exp, scale=1.0 / 16), reads=[dcs, dprep], writes=[dee])

                        def cons_k(th, pap, dp):
                            sl = slice(th * 1024, (th + 1) * 1024)
                            P.op("dve", lambda e: e.tensor_tensor(out=kt[:, fq, sl], in0=pap, in1=ee[:, sl], op=ALU.mult), reads=[dp, dee], writes=[dprep])
                            P.op("dve", lambda e: e.tensor_tensor(out=ee[:, sl].rearrange("p (c j) -> p c j", j=64), in0=ee[:, sl].rearrange("p (c j) -> p c j", j=64),
                                                                  in1=elT[:, fq, th * 16:(th + 1) * 16].unsqueeze(2).broadcast_to([128, 16, 64]), op=ALU.mult),
                                 reads=[dprep], writes=[dee])
                            P.op("dve", lambda e: e.tensor_tensor(out=kdT[:, sl], in0=pap, in1=ee[:, sl], op=ALU.mult), reads=[dp, dee], writes=[dkdT])
                        i = wcnt[0] % 3
                        proj_fm(W, 256 + fq * 128, 128, hT, dhT, dhTs, cons_k, None)
                        def mks(e, i=i):
                            ins = None
                            for c in range(8):
                                ins = e.matmul(ps[5][0:NS, 0:128], lhsT=hT[:, c, L:TW], rhs=wfm[i][:, c, 0:128], start=(c == 0), stop=(c == 7))
                            return ins
                        P.op("pe", mks, reads=[dwfm[i], dhTs], writes=[dps[5]])
                        acopy(k_tok[:, fq * 128:(fq + 1) * 128], ps[5][0:NS, 0:128], [dps[5]], [dprep])
                        transpose_kd(kdT, dkdT, kd, dprep, fq, 2)
                    P.barrier()
                stage(4.1)
                with ExitStack() as st3:
                    vtok = sb("vtok", [128, NT, 512], BF16, stack=st3)
                    ggt = sb("ggt", [128, NT, 512], BF16, stack=st3)
                    dv, dg, dgt = Dep(), Dep(), Dep()
                    with ExitStack() as st4:
                        wv = sb("wv", [128, 8, 256], BF16, stack=st4)
                        gtmp = sb("gtmp", [128, 256], stack=st4)
                        dwv = Dep()
                        sv_ = P.dsem()
                        for hv in range(2):
                            cs = slice(hv * 256, (hv + 1) * 256)
                            proj_tm(W, 512 + hv * 256, wv, dwv, sv_,
                                    lambda tt, pap, dp, cs=cs: acopy(vtok[:, tt, cs], pap, [dp], [dv]),
                                    lambda pap, dp, cs=cs: acopy(v_tok[:, cs], pap, [dp], [dprep]), n=256)

                            def cons_gt(tt, pap, dp, cs=cs):
                                P.op("act", lambda e: e.activation(out=gtmp[:], in_=pap, func=AF.Silu), reads=[dp], writes=[dgt])
                                P.op("dve", lambda e: e.tensor_tensor(out=ggt[:, tt, cs], in0=gtmp[:], in1=gnB[:, cs], op=ALU.mult), reads=[dgt, dgs], writes=[dg])

                            def cons_gt_s(pap, dp, cs=cs):
                                P.op("act", lambda e: e.activation(out=gtmp[0:NS, :], in_=pap, func=AF.Silu), reads=[dp], writes=[dgt])
                                P.op("dve", lambda e: e.tensor_tensor(out=gg_s[:, cs], in0=gtmp[0:NS, :], in1=gnB[0:NS, cs], op=ALU.mult), reads=[dgt, dgs], writes=[dgs])
                            proj_tm(W, 1024 + hv * 256, wv, dwv, sv_, cons_gt, cons_gt_s, n=256)
                        P.barrier()
                    stage(4.2)
                    gla_chunks(64, 4, lambda h: qt[:, h % 2, h // 2, :], kt, kd, vtok, ggt, elT, S, Sb,
                               lambda tt: hT[:, 0:4, tt * 128:(tt + 1) * 128], dhT, [dprep, dv, dg], o_gla_p[:, :, :])
                    stage(4.3)
                with ExitStack() as st5:
                    Sst = sb("Sst", [128, NS, 2, 128], stack=st5)
                    P.dma("sp", Sst[:], st_gla_d[:, :, :, :], s_in, writes=[dSst])
                    sample_state_update(64, q_s[:], k_tok[:], v_tok[:], a_s, Sst, dSst, [dprep], st5, "e")
                    rms_gate_sample(gg_s[:], dgs, hT[:, 0:4, L:TW], dhTs, st5, "e")
                    P.dma("sp", o_gla_s[:, :, :, :], Sst[:], s_out, reads=[dSst])
                    P.barrier()

        def build_bcast(dst, ddst, col_ap, add_one):
            for half in range(2):
                pb = nxt67()
                for q in range(4):
                    dc_ = half * 4 + q
                    P.op("dve", lambda e, dc_=dc_: e.tensor_scalar(out=hs_tmp[:].rearrange("p a b -> p (a b)")[:, 0:128], in0=ident[:],
                                                                    scalar1=col_ap[:, dc_:dc_ + 1], scalar2=None, op0=ALU.mult),
                         reads=[dmod, dconst, dsmall], writes=[dhs_tmp])
                    P.op("pe", lambda e, q=q, pb=pb: e.matmul(ps[pb][:, q * 128:(q + 1) * 128], lhsT=onesf[:],
                                                             rhs=hs_tmp[:].rearrange("p a b -> p (a b)")[:, 0:128], start=True, stop=True),
                         reads=[dhs_tmp, dconst], writes=[dps[pb]])
                if add_one:
                    P.op("act", lambda e, half=half, pb=pb: e.activation(out=dst[:, half * 512:(half + 1) * 512], in_=ps[pb][:], func=AF.Identity, bias=1.0),
                         reads=[dps[pb]], writes=[ddst])
                else:
                    acopy(dst[:, half * 512:(half + 1) * 512], ps[pb][:], [dps[pb]], [ddst])

        def ln_bufs(st, tag):
            stats = [sb("ln_stats%s%d" % (tag, i), [128, 2, 6], stack=st) for i in range(2)]
            mv = [sb("ln_mv%s%d" % (tag, i), [128, 8], stack=st) for i in range(2)]
            tmpn = [sb("ln_tmpn%s%d" % (tag, i), [128, D], stack=st) for i in range(2)]
            return stats, mv, tmpn, [Dep(), Dep()]

        def ln_s1(tt, lb):
            stats, mv, tmpn, dls = lb
            k = tt % 2
            for hh in range(2):
                P.op("dve", lambda e, hh=hh: e.bn_stats(out=stats[k][:, hh, :], in_=X[:, tt, hh * 512:(hh + 1) * 512]), reads=[dX[tt]], writes=[dls[k]])
            P.op("dve", lambda e: e.bn_aggr(out=mv[k][:, 0:2], in_=stats[k][:].rearrange("p a b -> p (a b)")), reads=[dls[k]], writes=[dls[k]])
            P.op("act", lambda e: e.activation(out=mv[k][:, 2:3], in_=mv[k][:, 1:2], func=AF.Sqrt, bias=LN_EPS), reads=[dls[k]], writes=[dls[k]])

        def ln_s2(tt, lb):
            stats, mv, tmpn, dls = lb
            k = tt % 2
            P.op("dve", lambda e: e.reciprocal(out=mv[k][:, 3:4], in_=mv[k][:, 2:3]), reads=[dls[k]], writes=[dls[k]])
            P.op("dve", lambda e: e.scalar_tensor_tensor(out=mv[k][:, 4:5], in0=mv[k][:, 0:1], scalar=-1.0, in1=mv[k][:, 3:4], op0=ALU.mult, op1=ALU.mult),
                 reads=[dls[k]], writes=[dls[k]])
            P.op("act", lambda e: e.activation(out=tmpn[k][:], in_=X[:, tt, :], func=AF.Identity, scale=mv[k][:, 3:4], bias=mv[k][:, 4:5]),
                 reads=[dls[k], dX[tt]], writes=[dls[k]])

        def ln_s3(tt, lb, bcs, dbcs):
            stats, mv, tmpn, dls = lb
            k = tt % 2
            P.op("dve", lambda e: e.tensor_tensor(out=tmpn[k][:], in0=tmpn[k][:], in1=bcs[1][:], op=ALU.mult), reads=[dls[k], dbcs[1]], writes=[dls[k]])
            P.op("dve", lambda e: e.tensor_tensor(out=X[:, tt, :], in0=tmpn[k][:], in1=bcs[2][:], op=ALU.add), reads=[dls[k], dbcs[2]], writes=[dX[tt]])

        def ln_sample(l, lni, g_gt, outT_ps_view, dpsv, st):
            v = sb("lns_v%d%d" % (l, lni), [128, 8, 2, NS], stack=st)
            mom = sb("lns_m%d%d" % (l, lni), [128, 4, NS], stack=st)
            lgb = sb("lns_g%d%d" % (l, lni), [128, 2, 8], stack=st)
            d = Dep()
            P.dma("sp", lgb[:, 0, :], ln_gT_d[:, l * 2 + lni, :], s_in, writes=[d])
            P.dma("sp", lgb[:, 1, :], ln_bT_d[:, l * 2 + lni, :], s_in, writes=[d])
            P.op("dve", lambda e: e.scalar_tensor_tensor(out=v[:, :, 0, :], in0=mods(l, g_gt), scalar=1.0, in1=outT_ps_view, op0=ALU.add, op1=ALU.mult),
                 reads=[dmod, dpsv], writes=[d])
            P.op("dve", lambda e: e.scalar_tensor_tensor(out=v[:, :, 0, :], in0=XsT[:], scalar=ALPHA, in1=v[:, :, 0, :], op0=ALU.mult, op1=ALU.add),
                 reads=[dXs, d], writes=[d])
            P.op("dve", lambda e: e.tensor_tensor(out=v[:, :, 1, :], in0=v[:, :, 0, :], in1=v[:, :, 0, :], op=ALU.mult), reads=[d], writes=[d])

            def mm(e):
                ins = None
                for c in range(8):
                    ins = e.matmul(ps[6][:, 0:2 * NS], lhsT=onesf[:], rhs=v[:, c, :, :].rearrange("p a b -> p (a b)"), start=(c == 0), stop=(c == 7))
                return ins
            P.op("pe", mm, reads=[d, dconst], writes=[dps[6]])
            P.op("dve", lambda e: e.tensor_scalar(out=mom[:, 0:2, :], in0=ps[6][:, 0:2 * NS].rearrange("p (a b) -> p a b", b=NS), scalar1=1.0 / D, scalar2=None,
                                                  op0=ALU.mult), reads=[dps[6]], writes=[d])
            P.op("dve", lambda e: e.tensor_tensor(out=mom[:, 2, :], in0=mom[:, 0, :], in1=mom[:, 0, :], op=ALU.mult), reads=[d], writes=[d])
            P.op("dve", lambda e: e.tensor_tensor(out=mom[:, 1, :], in0=mom[:, 1, :], in1=mom[:, 2, :], op=ALU.subtract), reads=[d], writes=[d])
            P.op("act", lambda e: e.activation(out=mom[:, 1, :], in_=mom[:, 1, :], func=AF.Sqrt, bias=LN_EPS), reads=[d], writes=[d])
            P.op("dve", lambda e: e.reciprocal(out=mom[:, 3, :], in_=mom[:, 1, :]), reads=[d], writes=[d])
            P.op("dve", lambda e: e.tensor_tensor(out=v[:, :, 0, :], in0=v[:, :, 0, :], in1=mom[:, 0, :].unsqueeze(1).broadcast_to([128, 8, NS]), op=ALU.subtract),
                 reads=[d], writes=[d])
            P.op("dve", lambda e: e.tensor_tensor(out=v[:, :, 0, :], in0=v[:, :, 0, :], in1=mom[:, 3, :].unsqueeze(1).broadcast_to([128, 8, NS]), op=ALU.mult),
                 reads=[d], writes=[d])
            P.op("dve", lambda e: e.tensor_tensor(out=v[:, :, 0, :], in0=v[:, :, 0, :], in1=lgb[:, 0, :].unsqueeze(2).broadcast_to([128, 8, NS]), op=ALU.mult),
                 reads=[d], writes=[d])
            P.op("dve", lambda e: e.tensor_tensor(out=XsT[:], in0=v[:, :, 0, :], in1=lgb[:, 1, :].unsqueeze(2).broadcast_to([128, 8, NS]), op=ALU.add),
                 reads=[d], writes=[dXs])

        def load_bcs(l, lni, g_gt, st):
            bcs = [sb("bc%d_%d%d" % (i, l, lni), [128, D], stack=st) for i in range(3)]
            dbcs = [Dep(), Dep(), Dep()]
            build_bcast(bcs[0], dbcs[0], modp(l, g_gt), True)
            P.dma("sp", bcs[1][:], ln_gb_d[:, (l * 2 + lni) * D:(l * 2 + lni + 1) * D].partition_broadcast(128), s_in, writes=[dbcs[1]])
            P.dma("sp", bcs[2][:], ln_gb_d[:, (4 + l * 2 + lni) * D:(4 + l * 2 + lni + 1) * D].partition_broadcast(128), s_in, writes=[dbcs[2]])
            return bcs, dbcs

        def out_proj_ln(l, wsrc, ysrc):
            with ExitStack() as st:
                wo = sb("wo", [128, 8, D], BF16, stack=st)
                lb = ln_bufs(st, "o%d" % l)
                dwo = Dep()
                swo = P.dsem()
                for c in range(8):
                    P.dma("pool", wo[:, c, :], wsrc[c * 128:(c + 1) * 128, :], swo, writes=[dwo])
                bcs, dbcs = load_bcs(l, 0, 2, st)
                for tt in range(NT):
                    if tt >= 1:
                        ln_s2(tt - 1, lb)
                    j = pcnt[0] % 2
                    pcnt[0] += 1

                    def mm(e, tt=tt, j=j):
                        ins = None
                        for dh in range(2):
                            for fc in range(8):
                                ins = e.matmul(pp2[j][:, dh * 512:(dh + 1) * 512], lhsT=ysrc(fc)[0](tt),
                                               rhs=wo[:, fc, dh * 512:(dh + 1) * 512], start=(fc == 0), stop=(fc == 7))
                        return ins
                    P.op("pe", mm, reads=[dwo] + [ysrc(fc)[1] for fc in range(8)], writes=[dpp[j]])
                    P.op("dve", lambda e, j=j: e.tensor_tensor(out=pp2[j][:, :], in0=pp2[j][:, :], in1=bcs[0][:], op=ALU.mult), reads=[dbcs[0]], writes=[dpp[j]])
                    P.op("dve", lambda e, tt=tt, j=j: e.scalar_tensor_tensor(out=X[:, tt, :], in0=X[:, tt, :], scalar=ALPHA, in1=pp2[j][:, :], op0=ALU.mult, op1=ALU.add),
                         reads=[dpp[j]], writes=[dX[tt]])
                    ln_s1(tt, lb)
                    if tt >= 1:
                        ln_s3(tt - 1, lb, bcs, dbcs)
                ln_s2(NT - 1, lb)
                ln_s3(NT - 1, lb, bcs, dbcs)
                def mms(e):
                    ins = None
                    for dc_ in range(8):
                        for fc in range(8):
                            ins = e.matmul(ps[4][:, dc_ * NS:(dc_ + 1) * NS], lhsT=wo[:, fc, dc_ * 128:(dc_ + 1) * 128], rhs=ysrc(fc)[2],
                                           start=(fc == 0), stop=(fc == 7))
                    return ins
                P.op("pe", mms, reads=[dwo] + [ysrc(fc)[3] for fc in range(8)], writes=[dps[4]])
                ln_sample(l, 0, 2, ps[4][:, 0:8 * NS].rearrange("p (c n) -> p c n", n=NS), dps[4], st)
                P.barrier()

        def ffn(l):
            make_hT(l, 4, 3)
            Wi = ffn_w_in[l]
            Wo = ffn_w_out[l]
            with ExitStack() as st:
                aT = sb("aT", [128, 22, 1024 + NS], BF16, stack=st)
                woh = sb("woh", [128, 22, 512], BF16, stack=st)
                sg = sb("ffn_sg", [128, 1024], stack=st)
                lb = ln_bufs(st, "f%d" % l)
                if len(wfm) < 4:
                    wfm.append(sb("wfm3_%d" % l, [128, 8, 128], BF16, stack=st))
                    dwfm.append(Dep())
                    swfm.append(P.dsem())
                fcnt = [0]
                daT, dwoh, dsg, daTs = Dep(), Dep(), Dep(), Dep()
                swoh = P.dsem()
                bcs, dbcs = load_bcs(l, 1, 5, st)
                for tblk in range(2):
                    for jf in range(22):
                        P.dma("pool", woh[:, jf, :], Wo[jf * 128:(jf + 1) * 128, 0:512], swoh, writes=[dwoh])
                    for jf in range(22):
                        ig_ = fcnt[0] % 4
                        fcnt[0] += 1
                        load_w(wfm[ig_], dwfm[ig_], swfm[ig_], Wi, jf * 128, 128)
                        iu_ = fcnt[0] % 4
                        fcnt[0] += 1
                        load_w(wfm[iu_], dwfm[iu_], swfm[iu_], Wi, DFF + jf * 128, 128)
                        for (wi_, j) in ((ig_, 0), (iu_, 1)):
                            def mm(e, wi_=wi_, j=j):
                                ins = None
                                for tb in range(2):
                                    for c in range(8):
                                        ins = e.matmul(pp2[j][:, tb * 512:(tb + 1) * 512], lhsT=wfm[wi_][:, c, :],
                                                       rhs=hT[:, c, tblk * 1024 + tb * 512: tblk * 1024 + (tb + 1) * 512], start=(c == 0), stop=(c == 7))
                                return ins
                            P.op("pe", mm, reads=[dwfm[wi_], dhT], writes=[dpp[j]])
                        P.op("act", lambda e: e.activation(out=sg[:], in_=pp2[0][:, :], func=AF.Silu), reads=[dpp[0]], writes=[dsg])
                        P.op("dve", lambda e, jf=jf: e.tensor_tensor(out=aT[:, jf, 0:1024], in0=pp2[1][:, :], in1=sg[:], op=ALU.mult), reads=[dpp[1], dsg], writes=[daT])
                        if tblk == 0:
                            for (wi_, col) in ((ig_, 0), (iu_, NS)):
                                def mms(e, wi_=wi_, col=col):
                                    ins = None
                                    for c in range(8):
                                        ins = e.matmul(ps[4][:, col:col + NS], lhsT=wfm[wi_][:, c, :], rhs=hT[:, c, L:TW], start=(c == 0), stop=(c == 7))
                                    return ins
                                P.op("pe", mms, reads=[dwfm[wi_], dhTs], writes=[dps[4]])
                            P.op("act", lambda e: e.activation(out=sg[:, 0:NS], in_=ps[4][:, 0:NS], func=AF.Silu), reads=[dps[4], dsg], writes=[dsg])
                            P.op("dve", lambda e, jf=jf: e.tensor_tensor(out=aT[:, jf, 1024:1024 + NS], in0=ps[4][:, NS:2 * NS], in1=sg[:, 0:NS], op=ALU.mult),
                                 reads=[dps[4], dsg], writes=[daTs])
                    for dh in range(2):
                        if dh == 1:
                            for jf in range(22):
                                P.dma("pool", woh[:, jf, :], Wo[jf * 128:(jf + 1) * 128, 512:1024], swoh, writes=[dwoh])
                        for t8 in range(8):
                            tt = tblk * 8 + t8
                            if dh == 1 and t8 >= 1:
                                ln_s2(tt - 1, lb)
                            pb = nxt67()

                            def mm2(e, t8=t8, pb=pb):
                                ins = None
                                for jf in range(22):
                                    ins = e.matmul(ps[pb][:, :], lhsT=aT[:, jf, t8 * 128:(t8 + 1) * 128], rhs=woh[:, jf, :], start=(jf == 0), stop=(jf == 21))
                                return ins
                            P.op("pe", mm2, reads=[daT, dwoh], writes=[dps[pb]])
                            P.op("dve", lambda e, pb=pb, dh=dh: e.tensor_tensor(out=ps[pb][:, :], in0=ps[pb][:, :], in1=bcs[0][:, dh * 512:(dh + 1) * 512], op=ALU.mult),
                                 reads=[dbcs[0]], writes=[dps[pb]])
                            P.op("dve", lambda e, tt=tt, dh=dh, pb=pb: e.scalar_tensor_tensor(out=X[:, tt, dh * 512:(dh + 1) * 512], in0=X[:, tt, dh * 512:(dh + 1) * 512],
                                                                                    scalar=ALPHA, in1=ps[pb][:, :], op0=ALU.mult, op1=ALU.add),
                                 reads=[dps[pb]], writes=[dX[tt]])
                            if dh == 1:
                                ln_s1(tt, lb)
                                if t8 >= 1:
                                    ln_s3(tt - 1, lb, bcs, dbcs)
                        if dh == 1:
                            ln_s2(tblk * 8 + 7, lb)
                            ln_s3(tblk * 8 + 7, lb, bcs, dbcs)
                        if tblk == 0:
                            def mms2(e, dh=dh):
                                ins = None
                                for q in range(4):
                                    for jf in range(22):
                                        ins = e.matmul(ps[5][:, (dh * 4 + q) * NS:(dh * 4 + q + 1) * NS], lhsT=woh[:, jf, q * 128:(q + 1) * 128],
                                                       rhs=aT[:, jf, 1024:1024 + NS], start=(jf == 0), stop=(jf == 21))
                                return ins
                            P.op("pe", mms2, reads=[daTs, dwoh], writes=[dps[5]])
                    if tblk == 0:
                        ln_sample(l, 1, 5, ps[5][:, 0:8 * NS].rearrange("p (c n) -> p c n", n=NS), dps[5], st)
                P.barrier()
                wfm.pop()
                dwfm.pop()
                swfm.pop()

        def s5_mixer(l, yoth, dyoth, dyoths):
            W = od_w_in
            PI = float(np.pi)
            with ExitStack() as st:
                sv = sb("s5v", [128, 3, 16], stack=st)
                pr = sb("s5pr", [128, 18, 16], stack=st)
                ce = sb("s5ce", [128, 11, 16], stack=st)
                dTt = sb("s5dT", [128, 2, 4], stack=st)
                Bb = sb("s5Bb", [128, 32, 128], BF16, stack=st)
                Cp = sb("s5Cp", [128, 32, 128], BF16, stack=st)
                s0 = sb("s5s0", [128, 2, 16, NS], stack=st)
                hl = sb("s5hl", [128, 2, 16], stack=st)
                ygb = sb("s5ygb", [128, 4, TW], BF16, stack=st)
                uT = yoth
                sms = sb("s5sm", [128, 1, NS], stack=st)
                ysa = sb("s5ysa", [128, 4, NS], stack=st)
                dpar, dB, dC, dCs, ds0, dos, dhl, dyg, dygs, du, dus = (Dep() for _ in range(11))
                dtab, dxr, dxi, dh, dt1_, dt2_, dsm, dysa = (Dep() for _ in range(8))
                dos_all = [Dep(), Dep()]
                dCst = [Dep(), Dep()]
                sB = P.dsem()
                P.dma("sp", sv[:], s5_vecT_d[:, :, :], s_in, writes=[dpar])
                P.dma("sp", dTt[:], s5_dT_d[:, :, :], s_in, writes=[dpar])
                P.dma("sp", s0[:], st_s5_d[:, :, :, :], s_in, writes=[ds0])
                for hf in range(2):
                    P.dma("pool", Bb[:, hf * 16:(hf + 1) * 16, :], s5_bbd_d[:, hf * 16:(hf + 1) * 16, :], sB, writes=[dB])
                LR, LI, DT, RHO, TH, C0, S0_, ABR, ABI, FR, FI, FIR, FII, T0, T1_, T2_ = (pr[:, i, :] for i in range(16))

                def dv(fn, reads=(dpar,), writes=(dpar,)):
                    P.op("dve", fn, reads=list(reads), writes=list(writes))

                def ac(fn):
                    P.op("act", fn, reads=[dpar], writes=[dpar])
                ac(lambda e: e.activation(out=DT, in_=sv[:, 2, :], func=AF.Exp))
                dv(lambda e: e.tensor_copy(out=LR, in_=sv[:, 0, :]))
                dv(lambda e: e.tensor_copy(out=LI, in_=sv[:, 1, :]))
                dv(lambda e: e.tensor_tensor(out=T0, in0=LR, in1=DT, op=ALU.mult))
                ac(lambda e: e.activation(out=RHO, in_=T0, func=AF.Exp))
                dv(lambda e: e.tensor_tensor(out=TH, in0=LI, in1=DT, op=ALU.mult))
                TWO_PI = 2 * PI

                def wrap_small(ap, tmp):
                    dv(lambda e: e.tensor_scalar(out=tmp, in0=ap, scalar1=1.0, scalar2=-1.0, op0=ALU.is_ge, op1=ALU.mult))
                    dv(lambda e: e.tensor_tensor(out=ap, in0=ap, in1=tmp, op=ALU.add))
                dv(lambda e: e.tensor_scalar(out=T0, in0=TH, scalar1=1.0 / TWO_PI, scalar2=None, op0=ALU.mult))
                for k in range(8):
                    wrap_small(T0, T1_)
                dv(lambda e: e.tensor_scalar(out=T1_, in0=T0, scalar1=0.0, scalar2=1.0, op0=ALU.is_lt, op1=ALU.mult))
                dv(lambda e: e.tensor_tensor(out=T0, in0=T0, in1=T1_, op=ALU.add))
                dv(lambda e: e.tensor_copy(out=ce[:, 0, :], in_=T0))
                for k in range(1, 11):
                    dv(lambda e, k=k: e.tensor_scalar(out=ce[:, k, :], in0=ce[:, k - 1, :], scalar1=2.0, scalar2=None, op0=ALU.mult))
                    wrap_small(ce[:, k, :], T1_)
                ac(lambda e: e.activation(out=S0_, in_=ce[:, 0, :], func=AF.Sin, scale=TWO_PI, bias=-PI))
                ac(lambda e: e.activation(out=T2_, in_=ce[:, 0, :], func=AF.Abs, scale=TWO_PI, bias=-PI))
                ac(lambda e: e.activation(out=C0, in_=T2_, func=AF.Sin, scale=-1.0, bias=PI / 2))
                dv(lambda e: e.scalar_tensor_tensor(out=ABR, in0=RHO, scalar=-1.0, in1=C0, op0=ALU.mult, op1=ALU.mult))
                dv(lambda e: e.scalar_tensor_tensor(out=ABI, in0=RHO, scalar=-1.0, in1=S0_, op0=ALU.mult, op1=ALU.mult))
                dv(lambda e: e.tensor_scalar(out=T0, in0=ABR, scalar1=-1.0, scalar2=None, op0=ALU.add))
                dv(lambda e: e.tensor_tensor(out=T1_, in0=LR, in1=LR, op=ALU.mult))
                dv(lambda e: e.tensor_tensor(out=T2_, in0=LI, in1=LI, op=ALU.mult))
                dv(lambda e: e.tensor_tensor(out=T1_, in0=T1_, in1=T2_, op=ALU.add))
                dv(lambda e: e.reciprocal(out=T1_, in_=T1_))
                dv(lambda e: e.tensor_tensor(out=FR, in0=T0, in1=LR, op=ALU.mult))
                dv(lambda e: e.tensor_tensor(out=T2_, in0=ABI, in1=LI, op=ALU.mult))
                dv(lambda e: e.tensor_tensor(out=FR, in0=FR, in1=T2_, op=ALU.add))
                dv(lambda e: e.tensor_tensor(out=FR, in0=FR, in1=T1_, op=ALU.mult))
                dv(lambda e: e.tensor_tensor(out=FI, in0=ABI, in1=LR, op=ALU.mult))
                dv(lambda e: e.tensor_tensor(out=T2_, in0=T0, in1=LI, op=ALU.mult))
                dv(lambda e: e.tensor_tensor(out=FI, in0=FI, in1=T2_, op=ALU.subtract))
                dv(lambda e: e.tensor_tensor(out=FI, in0=FI, in1=T1_, op=ALU.mult))
                dv(lambda e: e.tensor_tensor(out=T0, in0=FR, in1=FR, op=ALU.mult))
                dv(lambda e: e.tensor_tensor(out=T2_, in0=FI, in1=FI, op=ALU.mult))
                dv(lambda e: e.tensor_tensor(out=T0, in0=T0, in1=T2_, op=ALU.add))
                dv(lambda e: e.reciprocal(out=T0, in_=T0))
                dv(lambda e: e.tensor_tensor(out=FIR, in0=FR, in1=T0, op=ALU.mult))
                dv(lambda e: e.scalar_tensor_tensor(out=FII, in0=FI, scalar=-1.0, in1=T0, op0=ALU.mult, op1=ALU.mult))
                stc = ExitStack()
                Cst = [sb("s5Cst%d" % i, [128, 2, 128], stack=stc) for i in range(2)]
                ctmp = sb("s5ctmp", [128, 2, 128], stack=stc)
                for cg in range(16):
                    b_ = cg % 2
                    P.dma("sp", Cst[b_][:, 0, :], s5_cbd_d[:, cg, :], s_in, writes=[dCst[b_]])
                    P.dma("sp", Cst[b_][:, 1, :], s5_cbd_d[:, 16 + cg, :], s_in, writes=[dCst[b_]])
                    fr, fi = pr[:, 9, cg:cg + 1], pr[:, 10, cg:cg + 1]
                    P.op("dve", lambda e, b_=b_, fi=fi: e.tensor_scalar(out=ctmp[:, 0, :], in0=Cst[b_][:, 1, :], scalar1=fi, scalar2=None, op0=ALU.mult),
                         reads=[dCst[b_], dpar], writes=[dCs])
                    P.op("dve", lambda e, b_=b_, fr=fr, cg=cg: e.scalar_tensor_tensor(out=Cp[:, cg, :], in0=Cst[b_][:, 0, :], scalar=fr, in1=ctmp[:, 0, :],
                                                                                 op0=ALU.mult, op1=ALU.subtract), reads=[dCst[b_], dpar, dCs], writes=[dC])
                    P.op("dve", lambda e, b_=b_, fr=fr: e.tensor_scalar(out=ctmp[:, 1, :], in0=Cst[b_][:, 1, :], scalar1=fr, scalar2=-1.0, op0=ALU.mult, op1=ALU.mult),
                         reads=[dCst[b_], dpar], writes=[dCs])
                    P.op("dve", lambda e, b_=b_, fi=fi, cg=cg: e.scalar_tensor_tensor(out=ctmp[:, 0, :], in0=Cst[b_][:, 0, :], scalar=fi, in1=ctmp[:, 1, :],
                                                                                 op0=ALU.mult, op1=ALU.subtract), reads=[dCst[b_], dpar, dCs], writes=[dCs])
                    P.op("dve", lambda e, cg=cg: e.tensor_scalar(out=Cp[:, 16 + cg, :], in0=ctmp[:, 0, :], scalar1=-1.0, scalar2=None, op0=ALU.mult),
                         reads=[dCs], writes=[dC])
                P.barrier()
                stc.close()
                ctab = sb("s5c", [128, L], stack=st)
                stab = sb("s5s", [128, L], stack=st)
                xr = sb("s5xr", [128, L], stack=st)
                xi = sb("s5xi", [128, L], stack=st)
                hre = sb("s5hre", [128, L], BF16, stack=st)
                him = sb("s5him", [128, L], BF16, stack=st)
                t1 = xr[:, 0:512]
                t2 = xr[:, 512:1024]
                lo_t = sb("s5lo", [128, 4, 64], stack=st)
                hi_t = sb("s5hi", [128, 4, 32], stack=st)
                lh_tmp = sb("s5lht", [128, 4, 32], stack=st)
                dlohi = Dep()

                def build_lohi(cg0):
                    def lv(fn):
                        P.op("dve", fn, reads=[dlohi, dpar], writes=[dlohi])
                    for tab, k0, nlev in ((lo_t, 0, 6), (hi_t, 6, 5)):
                        lv(lambda e, tab=tab: e.memset(tab[:, :, 0:1], 0.0))
                        for k in range(nlev):
                            n = 1 << k
                            inc = ce[:, k0 + k, cg0:cg0 + 4].unsqueeze(2).broadcast_to([128, 4, n])
                            lv(lambda e, tab=tab, n=n, inc=inc: e.tensor_tensor(out=tab[:, :, n:2 * n], in0=tab[:, :, 0:n], in1=inc, op=ALU.add))
                            lv(lambda e, tab=tab, n=n: e.tensor_scalar(out=lh_tmp[:, :, 0:n], in0=tab[:, :, n:2 * n], scalar1=1.0, scalar2=-1.0,
                                                                       op0=ALU.is_ge, op1=ALU.mult))
                            lv(lambda e, tab=tab, n=n: e.tensor_tensor(out=tab[:, :, n:2 * n], in0=tab[:, :, n:2 * n], in1=lh_tmp[:, :, 0:n], op=ALU.add))
                dt1 = dxr
                dt2 = dxr
                for fo in range(4):
                    proj_fm(W, fo * 128, 128, hT, dhT, dhTs,
                            lambda th, pap, dp, fo=fo: acopy(uT[:, fo, th * 1024:(th + 1) * 1024], pap, [dp], [du]),
                            lambda pap, dp, fo=fo: acopy(uT[:, fo, L:TW], pap, [dp], [dus]))
                def rs_mm(e):
                    ins = None
                    for ri in range(2):
                        for cg_ in range(16):
                            ins = e.matmul(pp2[ri][:, cg_ * NS:(cg_ + 1) * NS], lhsT=Bb[:, ri * 16 + cg_, :], rhs=uT[:, cg_ // 4, L:TW], start=True, stop=True)
                    return ins
                P.op("pe", rs_mm, reads=[dB, dus], writes=[dpp[0], dpp[1]])
                RR = pp2[0][:, 0:16 * NS].rearrange("p (c n) -> p c n", n=NS)
                RI = pp2[1][:, 0:16 * NS].rearrange("p (c n) -> p c n", n=NS)
                tq = lambda i: xr[:, i * 256:(i + 1) * 256].rearrange("p (c n) -> p c n", n=NS)
                OS = xi[:, 0:512].rearrange("p (r c n) -> p r c n", r=2, n=NS)
                SMB = hre[:, 0:512].rearrange("p (r c n) -> p r c n", r=2, n=NS)
                bcp = lambda i: pr[:, i, :].unsqueeze(2).broadcast_to([128, 16, NS])

                def tt_(out, in0, in1, op, reads, writes):
                    P.op("dve", lambda e: e.tensor_tensor(out=out, in0=in0, in1=in1, op=op), reads=reads, writes=writes)
                rd = [dxr, dpar, ds0]
                tt_(tq(0), RI, bcp(10), ALU.mult, rd + [dpp[1]], [dxr])
                tt_(tq(1), RR, bcp(9), ALU.mult, rd + [dpp[0]], [dxr])
                tt_(tq(1), tq(1), tq(0), ALU.subtract, rd, [dxr])
                tt_(tq(0), RR, bcp(10), ALU.mult, rd + [dpp[0]], [dxr])
                tt_(tq(2), RI, bcp(9), ALU.mult, rd + [dpp[1]], [dxr])
                tt_(tq(2), tq(2), tq(0), ALU.add, rd, [dxr])
                tt_(tq(0), s0[:, 0, :, :], bcp(7), ALU.mult, rd, [dxr])
                tt_(tq(1), tq(1), tq(0), ALU.add, rd, [dxr])
                tt_(tq(0), s0[:, 1, :, :], bcp(8), ALU.mult, rd, [dxr])
                tt_(OS[:, 0], tq(1), tq(0), ALU.subtract, rd + [dxi], [dxi])
                tt_(tq(0), s0[:, 1, :, :], bcp(7), ALU.mult, rd, [dxr])
                tt_(tq(2), tq(2), tq(0), ALU.add, rd, [dxr])
                tt_(tq(0), s0[:, 0, :, :], bcp(8), ALU.mult, rd, [dxr])
                tt_(OS[:, 1], tq(2), tq(0), ALU.add, rd + [dxi], [dxi])
                P.dma("sp", o_s5_s[:, :, :, :], OS, s_out, reads=[dxi])
                tt_(tq(0), OS[:, 1], bcp(12), ALU.mult, rd + [dxi], [dxr])
                tt_(tq(1), OS[:, 0], bcp(11), ALU.mult, rd + [dxi], [dxr])
                tt_(SMB[:, 0], tq(1), tq(0), ALU.subtract, rd + [dh], [dh])
                tt_(tq(0), OS[:, 0], bcp(12), ALU.mult, rd + [dxi], [dxr])
                tt_(tq(1), OS[:, 1], bcp(11), ALU.mult, rd + [dxi], [dxr])
                tt_(SMB[:, 1], tq(1), tq(0), ALU.add, rd + [dh], [dh])

                def ys_mm(e):
                    ins = None
                    for fo_ in range(4):
                        out = pp2[0][:, 512 + fo_ * NS:512 + (fo_ + 1) * NS]
                        for k_ in range(4):
                            cg_ = fo_ * 4 + k_
                            e.matmul(out, lhsT=Cp[:, cg_, :], rhs=SMB[:, 0, cg_, :], start=(k_ == 0), stop=False)
                            ins = e.matmul(out, lhsT=Cp[:, 16 + cg_, :], rhs=SMB[:, 1, cg_, :], start=False, stop=(k_ == 3))
                    return ins
                P.op("pe", ys_mm, reads=[dC, dh], writes=[dpp[0]])
                P.op("dve", lambda e: e.tensor_copy(out=ysa[:], in_=pp2[0][:, 512:512 + 4 * NS].rearrange("p (f n) -> p f n", n=NS)), reads=[dpp[0]], writes=[dysa])
                for cg in range(16):
                    fo = cg // 4
                    first, last = (cg % 4 == 0), (cg % 4 == 3)
                    sc = lambda i: pr[:, i, cg:cg + 1]
                    if cg % 4 == 0:
                        build_lohi(cg)
                    cl = cg % 4
                    xr3 = xr[:].rearrange("p (a b) -> p a b", b=64)
                    P.op("dve", lambda e: e.tensor_tensor(out=xr3, in0=lo_t[:, cl, :].unsqueeze(1).broadcast_to([128, 32, 64]),
                                                          in1=hi_t[:, cl, :].unsqueeze(2).broadcast_to([128, 32, 64]), op=ALU.add),
                         reads=[dxr, dh, dlohi], writes=[dxr])
                    P.op("dve", lambda e: e.scalar_tensor_tensor(out=xr[:], in0=xr[:], scalar=1.0, in1=xr[:], op0=ALU.is_ge, op1=ALU.subtract),
                         reads=[dxr], writes=[dxr])
                    P.op("act", lambda e: e.activation(out=stab[:], in_=xr[:], func=AF.Sin, scale=-TWO_PI, bias=-PI), reads=[dxr, dh], writes=[dtab])
                    P.op("act", lambda e: e.activation(out=xi[:], in_=xr[:], func=AF.Abs, scale=-TWO_PI, bias=-PI), reads=[dxr, dxi, dh], writes=[dxi])
                    P.op("act", lambda e: e.activation(out=ctab[:], in_=xi[:], func=AF.Sin, scale=-1.0, bias=PI / 2), reads=[dxi], writes=[dtab])
                    for th in range(2):
                        sl = slice(th * 1024, (th + 1) * 1024)
                        for ri in range(2):
                            def rmm(e, ri=ri, th=th):
                                ins = None
                                for tb in range(2):
                                    ins = e.matmul(pp2[ri][:, tb * 512:(tb + 1) * 512], lhsT=Bb[:, ri * 16 + cg, :],
                                                   rhs=uT[:, fo, th * 1024 + tb * 512: th * 1024 + (tb + 1) * 512], start=True, stop=True)
                                return ins
                            P.op("pe", rmm, reads=[dB, du], writes=[dpp[ri]])
                        P0, P1 = pp2[0][:, :], pp2[1][:, :]
                        P.op("dve", lambda e, sl=sl: e.tensor_tensor(out=xr[:, sl], in0=P0, in1=ctab[:, sl], op=ALU.mult), reads=[dpp[0], dtab, dxr], writes=[dxr])
                        P.op("dve", lambda e, sl=sl: e.tensor_tensor(out=xi[:, sl], in0=P1, in1=ctab[:, sl], op=ALU.mult), reads=[dpp[1], dtab, dh], writes=[dxi])
                        P.op("dve", lambda e, sl=sl: e.tensor_tensor(out=P0, in0=P0, in1=stab[:, sl], op=ALU.mult), reads=[dtab, dxr], writes=[dpp[0]])
                        P.op("dve", lambda e, sl=sl: e.tensor_tensor(out=P1, in0=P1, in1=stab[:, sl], op=ALU.mult), reads=[dtab, dxi], writes=[dpp[1]])
                        P.op("dve", lambda e, sl=sl: e.tensor_tensor(out=xr[:, sl], in0=P1, in1=xr[:, sl], op=ALU.add), reads=[dpp[1]], writes=[dxr])
                        P.op("dve", lambda e, sl=sl: e.tensor_tensor(out=xi[:, sl], in0=xi[:, sl], in1=P0, op=ALU.subtract), reads=[dpp[0]], writes=[dxi])
                    rho_b = pr[:, 3, cg:cg + 1].broadcast_to([128, L])
                    P.op("dve", lambda e: e.tensor_tensor_scan(out=xr[:], data0=rho_b, data1=xr[:], initial=0.0, op0=ALU.mult, op1=ALU.add), reads=[dpar], writes=[dxr])
                    P.op("dve", lambda e: e.tensor_tensor_scan(out=xi[:], data0=rho_b, data1=xi[:], initial=0.0, op0=ALU.mult, op1=ALU.add), reads=[dpar], writes=[dxi])
                    for th in range(2):
                        sl = slice(th * 1024, (th + 1) * 1024)
                        P0, P1 = pp2[0][:, :], pp2[1][:, :]
                        P.op("dve", lambda e, sl=sl: e.tensor_tensor(out=P0, in0=xr[:, sl], in1=ctab[:, sl], op=ALU.mult), reads=[dxr, dtab], writes=[dpp[0]])
                        P.op("dve", lambda e, sl=sl: e.tensor_tensor(out=P1, in0=xi[:, sl], in1=ctab[:, sl], op=ALU.mult), reads=[dxi, dtab], writes=[dpp[1]])
                        P.op("dve", lambda e, sl=sl: e.tensor_tensor(out=xr[:, sl], in0=xr[:, sl], in1=stab[:, sl], op=ALU.mult), reads=[dtab, dpp[0]], writes=[dxr])
                        P.op("dve", lambda e, sl=sl: e.tensor_tensor(out=xi[:, sl], in0=xi[:, sl], in1=stab[:, sl], op=ALU.mult), reads=[dtab, dpp[1]], writes=[dxi])
                        if th == 1:
                            P.op("dve", lambda e: e.tensor_tensor(out=hl[:, 0, cg:cg + 1], in0=pp2[0][:, 1023:1024], in1=xi[:, L - 1:L], op=ALU.subtract),
                                 reads=[dpp[0], dxi], writes=[dhl])
                            P.op("dve", lambda e: e.tensor_tensor(out=hl[:, 1, cg:cg + 1], in0=pp2[1][:, 1023:1024], in1=xr[:, L - 1:L], op=ALU.add),
                                 reads=[dpp[1], dxr], writes=[dhl])
                        P.op("dve", lambda e, sl=sl: e.tensor_tensor(out=hre[:, sl], in0=P0, in1=xi[:, sl], op=ALU.subtract), reads=[dpp[0], dxi], writes=[dh])
                        P.op("dve", lambda e, sl=sl: e.tensor_tensor(out=him[:, sl], in0=P1, in1=xr[:, sl], op=ALU.add), reads=[dpp[1], dxr], writes=[dh])
                    for tb in range(4):
                        def ymm(e, tb=tb):
                            e.matmul(ps[4 + tb][:, :], lhsT=Cp[:, cg, :], rhs=hre[:, tb * 512:(tb + 1) * 512], start=first, stop=False)
                            return e.matmul(ps[4 + tb][:, :], lhsT=Cp[:, 16 + cg, :], rhs=him[:, tb * 512:(tb + 1) * 512], start=False, stop=last)
                        P.op("pe", ymm, reads=[dC, dh], writes=[dps[4 + tb]])
                    if last:
                        dtb = [Dep() for _ in range(4)]
                        T1 = [xr[:, tb * 512:(tb + 1) * 512] for tb in range(4)]
                        T2 = [xi[:, tb * 512:(tb + 1) * 512] for tb in range(4)]
                        SL = [slice(tb * 512, (tb + 1) * 512) for tb in range(4)]
                        for tb in range(4):
                            P.op("dve", lambda e, tb=tb: e.scalar_tensor_tensor(out=T1[tb], in0=uT[:, fo, SL[tb]], scalar=dTt[:, 0, fo:fo + 1], in1=ps[4 + tb][:, :],
                                                                             op0=ALU.mult, op1=ALU.add), reads=[du, dpar, dps[4 + tb], dh], writes=[dtb[tb], dxr, dxi])
                        for tb in range(4):
                            P.op("act", lambda e, tb=tb: e.activation(out=T2[tb], in_=T1[tb], func=AF.Square), reads=[dtb[tb]], writes=[dtb[tb]])
                        for tb in range(4):
                            P.op("dve", lambda e, tb=tb: e.tensor_scalar(out=T2[tb], in0=T2[tb], scalar1=0.044715, scalar2=1.0, op0=ALU.mult, op1=ALU.add),
                                 reads=[dtb[tb]], writes=[dtb[tb]])
                        for tb in range(4):
                            P.op("dve", lambda e, tb=tb: e.tensor_tensor(out=T2[tb], in0=T1[tb], in1=T2[tb], op=ALU.mult), reads=[dtb[tb]], writes=[dtb[tb]])
                        for tb in range(4):
                            P.op("act", lambda e, tb=tb: e.activation(out=T2[tb], in_=T2[tb], func=AF.Sigmoid, scale=GELU_C), reads=[dtb[tb]], writes=[dtb[tb]])
                        for tb in range(4):
                            P.op("dve", lambda e, tb=tb: e.tensor_tensor(out=ygb[:, fo, SL[tb]], in0=T1[tb], in1=T2[tb], op=ALU.mult),
                                 reads=[dtb[tb], dxr, dxi], writes=[dyg])
                        P.op("dve", lambda e: e.scalar_tensor_tensor(out=t1[:, 0:NS], in0=uT[:, fo, L:TW], scalar=dTt[:, 0, fo:fo + 1], in1=ysa[:, fo, :],
                                                                     op0=ALU.mult, op1=ALU.add), reads=[dus, dpar, dysa, dt1], writes=[dt1])
                        gelu_from(t1[:, 0:NS], dt1, ygb[:, fo, L:TW], dygs, t2[:, 0:NS], dt2, NS)
                P.op("dve", lambda e: e.tensor_tensor(out=pr[:, 16, :], in0=hl[:, 1, :], in1=FI, op=ALU.mult), reads=[dhl, dpar], writes=[dpar])
                P.op("dve", lambda e: e.tensor_tensor(out=pr[:, 17, :], in0=hl[:, 0, :], in1=FR, op=ALU.mult), reads=[dhl, dpar], writes=[dpar])
                P.op("dve", lambda e: e.tensor_tensor(out=stg[:, 16:32], in0=pr[:, 17, :], in1=pr[:, 16, :], op=ALU.subtract), reads=[dpar], writes=[dstg])
                P.op("dve", lambda e: e.tensor_tensor(out=pr[:, 16, :], in0=hl[:, 0, :], in1=FI, op=ALU.mult), reads=[dhl, dpar], writes=[dpar])
                P.op("dve", lambda e: e.tensor_tensor(out=pr[:, 17, :], in0=hl[:, 1, :], in1=FR, op=ALU.mult), reads=[dhl, dpar], writes=[dpar])
                P.op("dve", lambda e: e.tensor_tensor(out=stg[:, 32:48], in0=pr[:, 17, :], in1=pr[:, 16, :], op=ALU.add), reads=[dpar], writes=[dstg])
                P.dma("sp", o_s5_p.rearrange("p a b -> p (a b)"), stg[:, 16:48], s_out, reads=[dstg])
                P.barrier()
                for fo2 in range(4):
                    def cons_g(th, pap, dp, fo2=fo2):
                        sl = slice(th * 1024, (th + 1) * 1024)
                        P.op("act", lambda e: e.activation(out=xr[:, sl], in_=pap, func=AF.Sigmoid, bias=dTt[:, 1, fo2:fo2 + 1]), reads=[dp, dpar], writes=[dxr])
                        P.op("dve", lambda e: e.tensor_tensor(out=yoth[:, fo2, sl], in0=xr[:, sl], in1=ygb[:, fo2, sl], op=ALU.mult), reads=[dxr, dyg], writes=[dyoth])

                    def cons_g_s(pap, dp, fo2=fo2):
                        P.op("act", lambda e: e.activation(out=sms[:, 0, :], in_=pap, func=AF.Sigmoid, bias=dTt[:, 1, fo2:fo2 + 1]), reads=[dp, dpar], writes=[dsm])
                        P.op("dve", lambda e: e.tensor_tensor(out=yoth[:, fo2, L:TW], in0=sms[:, 0, :], in1=ygb[:, fo2, L:TW], op=ALU.mult), reads=[dsm, dygs], writes=[dyoths])
                    proj_fm(w_glu, fo2 * 128, 128, ygb, dyg, dygs, cons_g, cons_g_s, nk=4)
                P.barrier()

        def hgrn_mixer(l, yh01, dyh01, yhs, dyhs):
            W = od_w_in
            with ExitStack() as st:
                lbt = sb("hg_lbt", [128, 2, 4], stack=st)
                lbv = sb("hg_lbv", [128, 2, 4], stack=st)
                q_s = sb("hq_s", [128, 4, NS], stack=st)
                a_s = sb("ha_s", [128, 4, NS], stack=st)
                kks = sb("hkks", [128, 4, NS], stack=st)
                k_tok = sb("hk_tok", [NS, 512], BF16, stack=st)
                v_tok = sb("hv_tok", [NS, 512], BF16, stack=st)
                gg_s = sb("hgg_s", [NS, 512], stack=st)
                gnB = sb("hgnB", [128, 512], stack=st)
                dpar, dsm, dgs, dSst = Dep(), Dep(), Dep(), Dep()
                P.dma("sp", lbt[:], hg_lbT_d[:, :, :], s_in, writes=[dpar])
                P.dma("sp", gnB[:], hg_ng_d.partition_broadcast(128), s_in, writes=[dgs])
                P.op("dve", lambda e: e.tensor_tensor(out=lbv[:, 0, :], in0=lbt[:, 1, :], in1=lbt[:, 0, :], op=ALU.subtract), reads=[dpar], writes=[dpar])
                P.op("act", lambda e: e.activation(out=lbv[:, 0, :], in_=lbv[:, 0, :], func=AF.Sigmoid), reads=[dpar], writes=[dpar])
                P.op("dve", lambda e: e.tensor_scalar(out=lbv[:, 1, :], in0=lbv[:, 0, :], scalar1=-1.0, scalar2=1.0, op0=ALU.mult, op1=ALU.add), reads=[dpar], writes=[dpar])
                for pss in range(2):
                    with ExitStack() as sp_:
                        qt = sb("hqt", [128, 2, L], BF16, stack=sp_)
                        kt = sb("hkt", [128, 2, L], BF16, stack=sp_)
                        kd = sb("hkd", [128, NT, 256], BF16, stack=sp_)
                        elT = sb("helT", [128, 2, 32], stack=sp_)
                        S = sb("hS", [128, 2, 128], stack=sp_)
                        Sb = sb("hSb", [128, 2, 128], BF16, stack=sp_)
                        dprep = Dep()
                        with ExitStack() as st2:
                            A = sb("hA", [128, L], stack=st2)
                            C = sb("hC", [128, L], stack=st2)
                            kdT = sb("hkdT", [128, 1024], BF16, stack=st2)
                            rmask = sb("hrmask", [128, 1024], stack=st2)
                            dA, dC, dkdT, drm = Dep(), Dep(), Dep(), Dep()
                            P.op("pool", lambda e: e.memset(rmask[:], 1.0), writes=[drm])
                            P.op("pool", lambda e: e.affine_select(out=rmask[:].rearrange("p (c j) -> p c j", j=64), in_=rmask[:].rearrange("p (c j) -> p c j", j=64),
                                                                    pattern=[[0, 16], [1, 64]], compare_op=ALU.is_gt, fill=0.0, base=0, channel_multiplier=0),
                                 reads=[drm], writes=[drm])
                            for hl in range(2):
                                h = pss * 2 + hl
                                lb_, oml_ = lbv[:, 0, h:h + 1], lbv[:, 1, h:h + 1]

                                def cons_f(th, pap, dp):
                                    sl = slice(th * 1024, (th + 1) * 1024)
                                    P.op("act", lambda e: e.activation(out=A[:, sl], in_=pap, func=AF.Sigmoid), reads=[dp, dprep, dkdT], writes=[dA])
                                    P.op("dve", lambda e: e.tensor_scalar(out=A[:, sl], in0=A[:, sl], scalar1=oml_, scalar2=lb_, op0=ALU.mult, op1=ALU.add),
                                         reads=[dpar], writes=[dA])

                                def cons_f_s(pap, dp):
                                    P.op("act", lambda e: e.activation(out=a_s[:, h, :], in_=pap, func=AF.Sigmoid), reads=[dp], writes=[dsm])
                                    P.op("dve", lambda e: e.tensor_scalar(out=a_s[:, h, :], in0=a_s[:, h, :], scalar1=oml_, scalar2=lb_, op0=ALU.mult, op1=ALU.add),
                                         reads=[dpar, dsm], writes=[dsm])
                                    P.op("dve", lambda e: e.tensor_scalar(out=kks[:, h, :], in0=a_s[:, h, :], scalar1=-1.0, scalar2=1.0, op0=ALU.mult, op1=ALU.add),
                                         reads=[dsm], writes=[dsm])
                                proj_fm(W, 1024 + h * 128, 128, hT, dhT, dhTs, cons_f, cons_f_s)
                                P.op("act", lambda e: e.activation(out=C[:], in_=A[:], func=AF.Ln), reads=[dA, dprep, dkdT], writes=[dC])
                                P.op("dve", lambda e: e.tensor_scalar(out=A[:], in0=A[:], scalar1=-1.0, scalar2=1.0, op0=ALU.mult, op1=ALU.add), reads=[dC], writes=[dA])
                                for th in range(2):
                                    sl = slice(th * 1024, (th + 1) * 1024)
                                    P.op("dve", lambda e, sl=sl: e.tensor_tensor_scan(out=C[:, sl], data0=rmask[:], data1=C[:, sl], initial=0.0, op0=ALU.mult, op1=ALU.add),
                                         reads=[drm], writes=[dC])
                                P.op("act", lambda e, hl=hl: e.activation(out=elT[:, hl, :], in_=C[:].rearrange("p (c j) -> p c j", j=64)[:, :, 63], func=AF.Exp),
                                     reads=[dC], writes=[dprep])
                                P.op("act", lambda e: e.activation(out=C[:], in_=C[:], func=AF.Exp), reads=[dprep], writes=[dC])

                                def cons_q(th, pap, dp, hl=hl):
                                    sl = slice(th * 1024, (th + 1) * 1024)
                                    P.op("act", lambda e: e.activation(out=pap, in_=pap, func=AF.Silu), reads=[], writes=[dp])
                                    P.op("dve", lambda e: e.tensor_tensor(out=qt[:, hl, sl], in0=pap, in1=C[:, sl], op=ALU.mult), reads=[dp, dC], writes=[dprep])

                                def cons_q_s(pap, dp):
                                    P.op("act", lambda e: e.activation(out=q_s[:, h, :], in_=pap, func=AF.Silu), reads=[dp], writes=[dsm])
                                proj_fm(W, 512 + h * 128, 128, hT, dhT, dhTs, cons_q, cons_q_s)
                                P.op("dve", lambda e: e.reciprocal(out=C[:], in_=C[:]), reads=[dprep], writes=[dC])
                                P.op("dve", lambda e, hl=hl: e.tensor_tensor(out=kt[:, hl, :], in0=A[:], in1=C[:], op=ALU.mult), reads=[dA, dC], writes=[dprep])
                                for th in range(2):
                                    sl = slice(th * 1024, (th + 1) * 1024)
                                    P.op("dve", lambda e, sl=sl, th=th, hl=hl: e.tensor_tensor(
                                        out=C[:, sl].rearrange("p (c j) -> p c j", j=64), in0=C[:, sl].rearrange("p (c j) -> p c j", j=64),
                                        in1=elT[:, hl, th * 16:(th + 1) * 16].unsqueeze(2).broadcast_to([128, 16, 64]), op=ALU.mult), reads=[dprep], writes=[dC])
                                    P.op("dve", lambda e, sl=sl: e.tensor_tensor(out=kdT[:], in0=A[:, sl], in1=C[:, sl], op=ALU.mult), reads=[dA, dC, dprep], writes=[dkdT])
                                    for tb in range(2):
                                        pb = nxt67()

                                        def tr(e, tb=tb, pb=pb):
                                            ins = None
                                            for q in range(4):
                                                ins = e.transpose(out=ps[pb][:].bitcast(BF16)[:, q * 128:(q + 1) * 128],
                                                                  in_=kdT[:, (tb * 4 + q) * 128:(tb * 4 + q + 1) * 128], identity=identb[:])
                                            return ins
                                        P.op("pe", tr, reads=[dkdT, dconst], writes=[dps[pb]])
                                        t_base = th * 8 + tb * 4
                                        P.op("act", lambda e, pb=pb, t_base=t_base, hl=hl: e.activation(
                                            out=kd[:, t_base:t_base + 4, hl * 128:(hl + 1) * 128],
                                            in_=ps[pb][:].bitcast(BF16)[:, 0:512].rearrange("p (q t) -> p q t", t=128), func=AF.Copy),
                                            reads=[dps[pb]], writes=[dprep])
                            P.barrier()
                        with ExitStack() as st3:
                            vtok = sb("hvtok", [128, NT, 256], BF16, stack=st3)
                            ggt = sb("hggt", [128, NT, 256], BF16, stack=st3)
                            dv_, dg_, dgt = Dep(), Dep(), Dep()
                            cs = slice(pss * 256, (pss + 1) * 256)
                            with ExitStack() as st4:
                                wv = sb("hwv", [128, 8, 256], BF16, stack=st4)
                                gtmp = sb("hgtmp", [128, 256], stack=st4)
                                dwv = Dep()
                                sv_ = P.dsem()
                                proj_tm(W, 1536 + pss * 256, wv, dwv, sv_,
                                        lambda tt, pap, dp: acopy(vtok[:, tt, :], pap, [dp], [dv_]),
                                        lambda pap, dp: acopy(v_tok[:, cs], pap, [dp], [dsm]), n=256)

                                def cons_gt(tt, pap, dp):
                                    P.op("act", lambda e: e.activation(out=gtmp[:], in_=pap, func=AF.Silu), reads=[dp], writes=[dgt])
                                    P.op("dve", lambda e: e.tensor_tensor(out=ggt[:, tt, :], in0=gtmp[:], in1=gnB[:, cs], op=ALU.mult), reads=[dgt, dgs], writes=[dg_])

                                def cons_gt_s(pap, dp):
                                    P.op("act", lambda e: e.activation(out=gtmp[0:NS, :], in_=pap, func=AF.Silu), reads=[dp], writes=[dgt])
                                    P.op("dve", lambda e: e.tensor_tensor(out=gg_s[:, cs], in0=gtmp[0:NS, :], in1=gnB[0:NS, cs], op=ALU.mult), reads=[dgt, dgs], writes=[dgs])
                                proj_tm(W, 2048 + pss * 256, wv, dwv, sv_, cons_gt, cons_gt_s, n=256)
                                P.barrier()
                            if pss == 0:
                                ydst = lambda tt: yh01[:, 0:2, tt * 128:(tt + 1) * 128]
                                dy = dyh01
                            else:
                                ydst = lambda tt: hT[:, 6:8, tt * 128:(tt + 1) * 128]
                                dy = dhT
                            gla_chunks(128, 2, lambda hl: qt[:, hl, :], kt, kd, vtok, ggt, elT, S, Sb, ydst, dy, [dprep, dv_, dg_],
                                       o_hg_p[:, pss * 2:(pss + 1) * 2, :])
                def ktr(e):
                    ins = None
                    for h in range(4):
                        ins = e.transpose(out=ps[6][0:NS, h * 128:(h + 1) * 128], in_=kks[:, h, :], identity=ident[:])
                    return ins
                P.op("pe", ktr, reads=[dsm, dconst], writes=[dps[6]])
                acopy(k_tok[:], ps[6][0:NS, :], [dps[6]], [dsm])
                with ExitStack() as st5:
                    Sst = sb("hSst", [128, NS, 4, 128], stack=st5)
                    P.dma("sp", Sst[:], st_hg_d[:, :, :, :], s_in, writes=[dSst])
                    sample_state_update(128, q_s[:], k_tok[:], v_tok[:], a_s, Sst, dSst, [dsm], st5, "o")
                    rms_gate_sample(gg_s[:], dgs, yhs[:], dyhs, st5, "o")
                    P.dma("sp", o_hg_s[:, :, :, :], Sst[:], s_out, reads=[dSst])
                    P.barrier()


        _DEAD[0] = False
        if True:
            stage(2)
            with ExitStack() as stl:
                yoth = sb("yoth", [128, 4, TW], BF16, stack=stl)
                dyoth, dyoths = Dep(), Dep()
                pa = ExitStack()
                mod_issue = setup_mod(pa)
                mod_issue(4)
                mod_issue.drain()
                make_hT(0, 1, 0)
                even_mixer(0, yoth, dyoth, dyoths, bg=mod_issue, bg_stack=pa)

                def ysrc0(fc):
                    if fc < 4:
                        return (lambda tt, fc=fc: hT[:, fc, tt * 128:(tt + 1) * 128], dhT, hT[:, fc, L:TW], dhTs)
                    return (lambda tt, fc=fc: yoth[:, fc - 4, tt * 128:(tt + 1) * 128], dyoth, yoth[:, fc - 4, L:TW], dyoths)
                stage(5)
                out_proj_ln(0, ev_w_out, ysrc0)
            stage(6)
            ffn(0)
            stage(7)
            make_hT(1, 1, 0)
            with ExitStack() as stl:
                yoth1 = sb("yoth1", [128, 4, TW], BF16, stack=stl)
                dyoth1, dyoths1 = Dep(), Dep()
                s5_mixer(1, yoth1, dyoth1, dyoths1)
                stage(8)
                yh01 = sb("yh01", [128, 2, L], BF16, stack=stl)
                yhs = sb("yhs", [128, 4, NS], BF16, stack=stl)
                dyh01, dyhs = Dep(), Dep()
                hgrn_mixer(1, yh01, dyh01, yhs, dyhs)

                def ysrc1(fc):
                    if fc < 4:
                        return (lambda tt, fc=fc: yoth1[:, fc, tt * 128:(tt + 1) * 128], dyoth1, yoth1[:, fc, L:TW], dyoths1)
                    if fc < 6:
                        return (lambda tt, fc=fc: yh01[:, fc - 4, tt * 128:(tt + 1) * 128], dyh01, yhs[:, fc - 4, :], dyhs)
                    return (lambda tt, fc=fc: hT[:, fc, tt * 128:(tt + 1) * 128], dhT, yhs[:, fc - 4, :], dyhs)
                stage(9)
                out_proj_ln(1, od_w_out, ysrc1)
            stage(10)
            ffn(1)
        _DEAD[0] = False
        P.barrier()
        ypv = y_p.rearrange("(n p) d -> p n d", p=128)
        for tt in range(NT):
            P.dma("sp", ypv[:, tt, :], X[:, tt, :], s_out, reads=[dX[tt]])
        P.dma("sp", y_sT[:, :, :], XsT[:], s_out, reads=[dXs])
        P.barrier()
    return nc


_CACHE = {}


def _fm(v, nch):
    return np.ascontiguousarray(np.asarray(v, np.float32).reshape(nch, 128).T)


def _prepare(inp):
    f32 = np.float32
    g = {k: np.asarray(v) for k, v in inp.items()}
    shared = {
        "w_ada": g["w_ada"], "ev_w_in": g["ev_w_in"][0], "ev_w_out": g["ev_w_out"][0],
        "od_w_in": g["od_w_in"][0], "od_w_out": g["od_w_out"][0], "ffn_w_in": g["ffn_w_in"], "ffn_w_out": g["ffn_w_out"],
        "w_glu": g["od_s5_w_glu"][0], "gla_w_lr": g["ev_gla_w_lr"][0],
    }
    shared["b_adaT"] = np.ascontiguousarray(g["b_ada"].reshape(2, 48, 128).transpose(2, 0, 1))
    shared["ln_gb"] = np.concatenate([g["ln_g"].reshape(-1), g["ln_b"].reshape(-1)]).reshape(1, 8 * D).astype(f32)
    shared["ln_gT"] = np.ascontiguousarray(g["ln_g"].reshape(4, 8, 128).transpose(2, 0, 1))
    shared["ln_bT"] = np.ascontiguousarray(g["ln_b"].reshape(4, 8, 128).transpose(2, 0, 1))
    shared["gla_b_lrT"] = _fm(g["ev_gla_b_lr"][0], 2)
    shared["gla_ng"] = np.tile(g["ev_gla_norm_g"][0], 4).reshape(1, 512).astype(f32)
    shared["hg_ng"] = np.tile(g["od_hg_norm_g"][0], 4).reshape(1, 512).astype(f32)
    shared["conv_wT"] = np.ascontiguousarray(g["ev_conv_w"][0].reshape(4, 4, 128).transpose(2, 1, 0))
    shared["lru_vecT"] = np.ascontiguousarray(np.stack([_fm(g["ev_conv_b"][0], 4), _fm(g["ev_lru_b_r"][0], 4),
                                                        _fm(g["ev_lru_b_i"][0], 4), _fm(g["ev_lru_lam"][0], 4)], axis=2))
    wbd = np.zeros((128, 8, 128), f32)
    for gi, wname in enumerate(("ev_lru_w_r", "ev_lru_w_i")):
        w = g[wname][0]
        for h in range(8):
            grp, hh = h // 2, h % 2
            wbd[hh * 64:(hh + 1) * 64, gi * 4 + grp, hh * 64:(hh + 1) * 64] = w[h]
    shared["lru_wbd"] = wbd
    def chT(a):
        return np.ascontiguousarray(np.asarray(a, f32).reshape(16, 2, 64).transpose(1, 2, 0).reshape(128, 16))
    shared["s5_vecT"] = np.ascontiguousarray(np.stack([chT(g["od_s5_lam_re"][0]), chT(g["od_s5_lam_im"][0]),
                                                       chT(np.repeat(g["od_s5_log_dt"][0][:, None], 64, axis=1))], axis=1))
    bbd = np.zeros((128, 32, 128), f32)
    cbd = np.zeros((128, 32, 128), f32)
    for ri, (bn, cn) in enumerate((("od_s5_b_re", "od_s5_c_re"), ("od_s5_b_im", "od_s5_c_im"))):
        bsrc, csrc = g[bn][0], g[cn][0]
        for gg_ in range(32):
            cg, two = gg_ // 2, gg_ % 2
            r0 = (gg_ % 8) * 16
            bbd[r0:r0 + 16, ri * 16 + cg, two * 64:(two + 1) * 64] = bsrc[gg_].T
            cbd[two * 64:(two + 1) * 64, ri * 16 + cg, r0:r0 + 16] = csrc[gg_].T
    shared["s5_bbd"] = bbd
    shared["s5_cbd"] = cbd
    shared["s5_dT"] = np.ascontiguousarray(np.stack([_fm(g["od_s5_d"][0], 4), _fm(g["od_s5_b_glu"][0], 4)], axis=1))
    shared["hg_lbT"] = np.ascontiguousarray(g["hg_lb_logits"].reshape(2, 4, 128).transpose(2, 0, 1))
    in_maps = []
    for b in range(NCORES):
        sl = slice(b * NS, (b + 1) * NS)
        m = dict(shared)
        m["xp"] = np.ascontiguousarray(g["x_prompt"][b])
        xs = g["x_sample"][sl, 0, :]
        m["xsT"] = np.ascontiguousarray(xs.T.reshape(8, 128, NS).transpose(1, 0, 2))
        c17 = np.concatenate([g["c_prompt"][b:b + 1], g["c_sample"][sl]], axis=0)
        m["cT"] = np.ascontiguousarray(c17.T.reshape(8, 128, 17).transpose(1, 0, 2))
        sg = g["state_gla"][0, sl]
        m["st_gla"] = np.ascontiguousarray(sg.reshape(NS, 2, 2, 64, 128).transpose(2, 3, 0, 1, 4).reshape(128, NS, 2, 128))
        sc = g["state_rglru_conv"][0, sl]
        m["st_conv"] = np.ascontiguousarray(sc.reshape(NS, 3, 4, 128).transpose(3, 2, 1, 0))
        m["st_lru"] = np.ascontiguousarray(g["state_rglru_h"][0, sl].reshape(NS, 4, 128).transpose(2, 1, 0))
        s5 = np.stack([g["state_s5_re"][0, sl], g["state_s5_im"][0, sl]], axis=0)
        m["st_s5"] = np.ascontiguousarray(s5.reshape(2, NS, 16, 2, 64).transpose(3, 4, 0, 2, 1).reshape(128, 2, 16, NS))
        m["st_hg"] = np.ascontiguousarray(g["state_hgrn"][0, sl].transpose(2, 0, 1, 3))
        in_maps.append({k: np.ascontiguousarray(v, dtype=f32) for k, v in m.items()})
    return in_maps


def kernel(**inp):
    if "nc" not in _CACHE:
        _CACHE["nc"] = build_program()
    nc = _CACHE["nc"]
    in_maps = _prepare(inp)
    res = run_bass_kernel_spmd(nc, in_maps, core_ids=list(range(NCORES)))
    return _assemble(res.results)


def _assemble(R):
    f32 = np.float32
    y_p = np.stack([R[b]["y_p"] for b in range(NCORES)], axis=0)
    y_s = np.concatenate([R[b]["y_sT"].transpose(2, 1, 0).reshape(NS, 1, D) for b in range(NCORES)], axis=0)
    gla_p = np.stack([R[b]["o_gla_p"].reshape(2, 64, 2, 128).transpose(2, 0, 1, 3).reshape(4, 64, 128) for b in range(NCORES)], axis=0)[None]
    gla_s = np.concatenate([R[b]["o_gla_s"].reshape(2, 64, NS, 2, 128).transpose(2, 3, 0, 1, 4).reshape(NS, 4, 64, 128)
                            for b in range(NCORES)], axis=0)[None]
    conv_p = np.stack([R[b]["o_conv_p"].transpose(2, 1, 0).reshape(3, 512) for b in range(NCORES)], axis=0)[None]
    conv_s = np.concatenate([R[b]["o_conv_s"].transpose(3, 2, 1, 0).reshape(NS, 3, 512) for b in range(NCORES)], axis=0)[None]
    lru_p = np.stack([R[b]["o_lru_p"].T.reshape(512) for b in range(NCORES)], axis=0)[None]
    lru_s = np.concatenate([R[b]["o_lru_s"].transpose(2, 1, 0).reshape(NS, 512) for b in range(NCORES)], axis=0)[None]

    def s5p(b, i):
        return R[b]["o_s5_p"][:, i, :].reshape(2, 64, 16).transpose(2, 0, 1).reshape(32, 64)

    def s5s(b, i):
        return R[b]["o_s5_s"][:, i].reshape(2, 64, 16, NS).transpose(3, 2, 0, 1).reshape(NS, 32, 64)
    re_p = np.stack([s5p(b, 0) for b in range(NCORES)], axis=0)[None]
    im_p = np.stack([s5p(b, 1) for b in range(NCORES)], axis=0)[None]
    re_s = np.concatenate([s5s(b, 0) for b in range(NCORES)], axis=0)[None]
    im_s = np.concatenate([s5s(b, 1) for b in range(NCORES)], axis=0)[None]
    hg_p = np.stack([R[b]["o_hg_p"].transpose(1, 0, 2) for b in range(NCORES)], axis=0)[None]
    hg_s = np.concatenate([R[b]["o_hg_s"].transpose(1, 2, 0, 3) for b in range(NCORES)], axis=0)[None]
    outs = (y_p, y_s, gla_p, gla_s, conv_p, conv_s, lru_p, lru_s, re_p, re_s, im_p, im_s, hg_p, hg_s)
    return tuple(np.ascontiguousarray(o, dtype=f32) for o in outs)
```
